# Optimizing a Trainium2 kernel written in Bass

```python
import jax, jax.numpy as jnp
from jax import lax
import numpy as np

D_MODEL = 1024
BATCH = 8
SEQ = 4096
DEPTH = 2

ATT_HEADS = 8
HEAD_DIM = 64
D_ATT = ATT_HEADS * HEAD_DIM
Q_BLOCK = 128
ATT_SCALE = HEAD_DIM ** -0.5
FORGET_BIAS_MEAN = 2.0
D_CONV = 256
CONV_WIDTH = 31
POOL_WINDOWS = (2, 4, 8, 16)
N_POOL = len(POOL_WINDOWS)
POOL_GROUP = 64
D_POOL = N_POOL * POOL_GROUP
D_MIX = D_ATT + D_CONV + D_POOL
D_IN = 3 * D_ATT + ATT_HEADS + 2 * D_CONV + D_POOL
D_FF = 2816
FFN_CONV_WIDTH = 3
EPS = 1e-6
NEG = -1e30

kernel_name = 'hybrid_fox_conformer_pool_block'


def rms_norm(x, g):
    xf = x.astype(jnp.float32)
    y = xf * lax.rsqrt(jnp.mean(xf * xf, axis=-1, keepdims=True) + EPS)
    return (y * g.astype(jnp.float32)).astype(x.dtype)


def layer_norm(x, g, b):
    xf = x.astype(jnp.float32)
    mu = jnp.mean(xf, axis=-1, keepdims=True)
    var = jnp.mean(jnp.square(xf - mu), axis=-1, keepdims=True)
    y = (xf - mu) * lax.rsqrt(var + EPS)
    return (y * g.astype(jnp.float32) + b.astype(jnp.float32)).astype(x.dtype)


def causal_depthwise_conv(x, w):
    width, channels = w.shape
    return lax.conv_general_dilated(
        x, w.astype(x.dtype)[:, None, :], window_strides=(1,),
        padding=[(width - 1, 0)], dimension_numbers=('NWC', 'WIO', 'NWC'),
        feature_group_count=channels)


def forgetting_attention(q, k, v, fg_logit):
    B, S, H, Dh = q.shape
    log_f = jax.nn.log_sigmoid(fg_logit.astype(jnp.float32))
    cum = jnp.transpose(jnp.cumsum(log_f, axis=1), (0, 2, 1))
    outs = []
    for start in range(0, S, Q_BLOCK):
        end = start + Q_BLOCK
        qb = q[:, start:end]
        kb = k[:, :end]
        vb = v[:, :end]
        s = jnp.einsum('bqhd,bkhd->bhqk', qb, kb).astype(jnp.float32) * ATT_SCALE
        s = s + cum[:, :, start:end, None] - cum[:, :, None, :end]
        qi = jnp.arange(start, end)[:, None]
        ki = jnp.arange(end)[None, :]
        s = jnp.where(ki <= qi, s, NEG)
        p = jax.nn.softmax(s, axis=-1).astype(vb.dtype)
        outs.append(jnp.einsum('bhqk,bkhd->bqhd', p, vb))
    return jnp.concatenate(outs, axis=1)


def conformer_conv(u, dw_w, dw_b, ln_g, ln_b, pw_w):
    a, b = jnp.split(u, 2, axis=-1)
    h = a * jax.nn.sigmoid(b)
    h = causal_depthwise_conv(h, dw_w) + dw_b.astype(h.dtype)
    h = layer_norm(h, ln_g, ln_b)
    h = jax.nn.silu(h)
    return h @ pw_w.astype(h.dtype)


def multiscale_pool(u, pool_w, pool_scale):
    B, S, _ = u.shape
    uf = u.astype(jnp.float32)
    cs = jnp.cumsum(uf, axis=1)
    pos = jnp.arange(1, S + 1, dtype=jnp.float32)[:, None]
    diffs = []
    for g, w in enumerate(POOL_WINDOWS):
        sl = slice(g * POOL_GROUP, (g + 1) * POOL_GROUP)
        c = cs[..., sl]
        prev = jnp.pad(c, ((0, 0), (w, 0), (0, 0)))[:, :S]
        mean = (c - prev) / jnp.minimum(pos, float(w))
        diffs.append(mean - uf[..., sl])
    d = jnp.stack(diffs, axis=2)
    y = jnp.einsum('bsgc,gcd->bsgd', d, pool_w.astype(jnp.float32))
    y = y.reshape(B, S, D_POOL) * pool_scale.astype(jnp.float32)
    return y.astype(u.dtype)


def setup_inputs(seed: int = 0) -> dict:
    key = jax.random.key(seed)
    ks = jax.random.split(key, 20)
    f32 = jnp.float32
    L = DEPTH

    def nrm(k, shape, scale):
        return jax.random.normal(k, shape, f32) * scale

    return {
        'x': nrm(ks[0], (BATCH, SEQ, D_MODEL), 1.0),
        'norm1_g': 1.0 + nrm(ks[1], (L, D_MODEL), 0.02),
        'w_in': nrm(ks[2], (L, D_MODEL, D_IN), D_MODEL ** -0.5),
        'b_f': FORGET_BIAS_MEAN + nrm(ks[3], (L, ATT_HEADS), 0.5),
        'q_norm_g': 1.0 + nrm(ks[4], (L, HEAD_DIM), 0.02),
        'k_norm_g': 1.0 + nrm(ks[5], (L, HEAD_DIM), 0.02),
        'conv_dw_w': nrm(ks[6], (L, CONV_WIDTH, D_CONV), CONV_WIDTH ** -0.5),
        'conv_dw_b': nrm(ks[7], (L, D_CONV), 0.02),
        'conv_ln_g': 1.0 + nrm(ks[8], (L, D_CONV), 0.02),
        'conv_ln_b': nrm(ks[9], (L, D_CONV), 0.02),
        'conv_pw_w': nrm(ks[10], (L, D_CONV, D_CONV), D_CONV ** -0.5),
        'pool_w': nrm(ks[11], (L, N_POOL, POOL_GROUP, POOL_GROUP), POOL_GROUP ** -0.5),
        'pool_scale': 1.0 + nrm(ks[12], (L, D_POOL), 0.1),
        'w_out': nrm(ks[13], (L, D_MIX, D_MODEL), D_MIX ** -0.5),
        'norm2_g': 1.0 + nrm(ks[14], (L, D_MODEL), 0.02),
        'w_up': nrm(ks[15], (L, D_MODEL, 2 * D_FF), D_MODEL ** -0.5),
        'ffn_dw_w': nrm(ks[16], (L, FFN_CONV_WIDTH, 2 * D_FF), FFN_CONV_WIDTH ** -0.5),
        'w_down': nrm(ks[17], (L, D_FF, D_MODEL), D_FF ** -0.5),
    }


def reference(x, norm1_g, w_in, b_f, q_norm_g, k_norm_g, conv_dw_w, conv_dw_b,
              conv_ln_g, conv_ln_b, conv_pw_w, pool_w, pool_scale, w_out,
              norm2_g, w_up, ffn_dw_w, w_down):
    B, S, _ = x.shape
    for l in range(DEPTH):
        h = rms_norm(x, norm1_g[l])
        proj = h @ w_in[l].astype(h.dtype)
        o = 0
        q = proj[..., o:o + D_ATT]; o += D_ATT
        k = proj[..., o:o + D_ATT]; o += D_ATT
        v = proj[..., o:o + D_ATT]; o += D_ATT
        fg = proj[..., o:o + ATT_HEADS]; o += ATT_HEADS
        conv_in = proj[..., o:o + 2 * D_CONV]; o += 2 * D_CONV
        pool_in = proj[..., o:o + D_POOL]

        q = rms_norm(q.reshape(B, S, ATT_HEADS, HEAD_DIM), q_norm_g[l])
        k = rms_norm(k.reshape(B, S, ATT_HEADS, HEAD_DIM), k_norm_g[l])
        v = v.reshape(B, S, ATT_HEADS, HEAD_DIM)
        fg = fg + b_f[l].astype(fg.dtype)
        att = forgetting_attention(q, k, v, fg).reshape(B, S, D_ATT)

        conv = conformer_conv(conv_in, conv_dw_w[l], conv_dw_b[l],
                              conv_ln_g[l], conv_ln_b[l], conv_pw_w[l])
        pool = multiscale_pool(pool_in, pool_w[l], pool_scale[l])

        mix = jnp.concatenate([att, conv, pool], axis=-1)
        x = x + mix @ w_out[l].astype(mix.dtype)

        h2 = rms_norm(x, norm2_g[l])
        up = h2 @ w_up[l].astype(h2.dtype)
        up = causal_depthwise_conv(up, ffn_dw_w[l])
        gate, val = jnp.split(up, 2, axis=-1)
        x = x + (jax.nn.silu(gate) * val) @ w_down[l].astype(up.dtype)
    return x
```

```python
import numpy as np
from contextlib import ExitStack
import concourse.bass as bass
import concourse.mybir as mybir
from concourse.bass_utils import run_bass_kernel_spmd

F32 = mybir.dt.float32
BF16 = mybir.dt.bfloat16
U8 = mybir.dt.uint8
AF = mybir.ActivationFunctionType
ALU = mybir.AluOpType

D = 1024
SEQ = 4096
NT = 8
DIN = 2312
DFF = 2816
EPS = 1e-6
NPP = 221
C_N1G, C_N2G, C_QG, C_KG, C_DWB, C_LNG, C_LNB, C_PSC, C_BF, C_DWW, C_FDW = 0, 8, 16, 17, 18, 20, 22, 24, 26, 27, 89

ENGS = ("pe", "act", "dve", "pool", "sp")


class Op:
    __slots__ = ("eng", "fn", "deps", "is_dma", "sem", "tick", "marked", "group")

    def __init__(self, eng, fn, is_dma):
        self.eng = eng
        self.fn = fn
        self.deps = []
        self.is_dma = is_dma
        self.sem = None
        self.tick = None
        self.marked = False
        self.group = None


class Sched:
    def __init__(self, nc):
        self.nc = nc
        self.streams = {e: [] for e in ENGS}
        self.last_writer = {}
        self.readers = {}
        self.all_ops = []
        self.gid = 0

    def _add(self, op, reads, writes):
        deps = {}
        for r in reads:
            w = self.last_writer.get(r)
            if w is not None:
                deps[id(w)] = w
        for wkey in writes:
            w = self.last_writer.get(wkey)
            if w is not None:
                deps[id(w)] = w
            for rd in self.readers.get(wkey, ()):
                deps[id(rd)] = rd
        op.deps = list(deps.values())
        for r in reads:
            self.readers.setdefault(r, []).append(op)
        for wkey in writes:
            self.last_writer[wkey] = op
            self.readers[wkey] = []
        self.streams[op.eng].append(op)
        self.all_ops.append(op)
        return op

    def op(self, eng, fn, reads=(), writes=()):
        return self._add(Op(eng, fn, False), reads, writes)

    def dma(self, eng, fn, semkey, reads=(), writes=(), group=None):
        o = Op(eng, fn, True)
        o.sem = semkey
        if group is None:
            self.gid += 1
            group = ("_g", self.gid)
        o.group = (semkey, group)
        return self._add(o, reads, writes)

    def barrier(self):
        lasts = []
        for e in ENGS:
            for o in reversed(self.streams[e]):
                if not o.is_dma and o.fn is not None:
                    lasts.append(o)
                    break
        dma_last = {}
        for o in self.all_ops:
            if o.is_dma:
                dma_last[o.sem] = o
        deps = lasts + list(dma_last.values())
        for e in ENGS:
            b = Op(e, None, False)
            b.deps = list(deps)
            self.streams[e].append(b)
            self.all_ops.append(b)
        self.last_writer = {}
        self.readers = {}

    def emit(self, stack):
        nc = self.nc

        def needs_wait(o, d):
            if d.is_dma:
                if o.is_dma and o.group == d.group:
                    return False
                return True
            if d.eng == o.eng and not o.is_dma and d.eng == "pe":
                return False
            return True

        for o in self.all_ops:
            for d in o.deps:
                if not d.is_dma and needs_wait(o, d):
                    d.marked = True
        esem = {}
        for e in ENGS:
            if e != "sp":
                esem[e] = stack.enter_context(nc.semaphore("s_" + e))
        for e in ENGS:
            c = 0
            for o in self.streams[e]:
                if o.is_dma or o.fn is None:
                    continue
                if o.marked:
                    c += 1
                    o.tick = c
                o.sem = esem.get(e)
        dsem = {}
        dcount = {}
        gmax = {}
        for o in self.all_ops:
            if o.is_dma:
                k = o.sem
                if k not in dsem:
                    dsem[k] = stack.enter_context(nc.semaphore("d_%d" % len(dsem)))
                    dcount[k] = 0
                dcount[k] += 16
                o.sem = dsem[k]
                gmax[o.group] = dcount[k]
        for o in self.all_ops:
            if o.is_dma:
                o.tick = gmax[o.group]
        final_waits = [(s, dcount[k]) for k, s in dsem.items()]
        block = stack.enter_context(nc.Block())

        def make(ename):
            def body(eng):
                waited = {}
                for o in self.streams[ename]:
                    for d in o.deps:
                        if not needs_wait(o, d):
                            continue
                        key = id(d.sem)
                        if waited.get(key, 0) >= d.tick:
                            continue
                        eng.wait_ge(d.sem, d.tick)
                        waited[key] = d.tick
                    if o.fn is None:
                        continue
                    inst = o.fn(eng)
                    if o.is_dma:
                        inst.then_inc(o.sem, 16)
                    elif o.marked:
                        inst.then_inc(o.sem, 1)
                if ename == "sp":
                    for s, v in final_waits:
                        eng.wait_ge(s, v)
            return body

        block.tensor(make("pe"))
        block.scalar(make("act"))
        block.vector(make("dve"))
        block.gpsimd(make("pool"))
        block.sync(make("sp"))


class Arena:
    def __init__(self, nc, nbytes):
        self.t = nc.alloc_sbuf_tensor("arena", [128, nbytes], U8)
        self.n = nbytes
        self.top = 0
        self.cnt = 0

    def alloc(self, free, dt, parts=128):
        esz = 4 if dt == F32 else 2
        n = esz
        for f in free:
            n *= f
        off = (self.top + 63) // 64 * 64
        self.top = off + n
        assert self.top <= self.n, ("SBUF arena overflow", self.top, self.n)
        ap = self.t[0:parts, off:off + n].bitcast(dt)
        if len(free) == 2:
            ap = ap.rearrange("p (a b) -> p a b", a=free[0])
        elif len(free) == 3:
            ap = ap.rearrange("p (a b c) -> p a b c", a=free[0], b=free[1])
        self.cnt += 1
        return ap, ("sb", self.cnt)


class Rot:
    def __init__(self, items):
        self.items = items
        self.i = 0

    def next(self):
        it = self.items[self.i % len(self.items)]
        self.i += 1
        return it


def build_program(layers):
    nc = bass.Bass("TRN2", target_bir_lowering=False)
    x_in = nc.dram_tensor("x", [SEQ, D], F32, kind="ExternalInput").ap()
    w_in = nc.dram_tensor("w_in", [2, D, DIN], F32, kind="ExternalInput").ap()
    w_out = nc.dram_tensor("w_out", [2, D, D], F32, kind="ExternalInput").ap()
    w_up = nc.dram_tensor("w_up", [2, D, 2 * DFF], F32, kind="ExternalInput").ap()
    w_down = nc.dram_tensor("w_down", [2, DFF, D], F32, kind="ExternalInput").ap()
    pw_w = nc.dram_tensor("pw", [2, 256, 256], F32, kind="ExternalInput").ap()
    pool_w = nc.dram_tensor("pool_w", [2, 4, 64, 64], F32, kind="ExternalInput").ap()
    pp_d = nc.dram_tensor("pp", [2, 128, NPP], F32, kind="ExternalInput").ap()
    cst_d = nc.dram_tensor("cst", [128, 3 * 128 + 1024], F32, kind="ExternalInput").ap()
    y_out = nc.dram_tensor("y", [SEQ, D], F32, kind="ExternalOutput").ap()
    qs_d = nc.dram_tensor("qs", [8, 70, SEQ], BF16, kind="Internal").ap()
    ks_d = nc.dram_tensor("ks", [8, 70, SEQ], BF16, kind="Internal").ap()
    vs_d = nc.dram_tensor("vs", [128, 32, 1024], BF16, kind="Internal").ap()
    cps_d = nc.dram_tensor("cps", [128, 4, SEQ], BF16, kind="Internal").ap()
    xmid_d = nc.dram_tensor("xmid", [SEQ, D], F32, kind="Internal").ap()
    xl_d = nc.dram_tensor("xl", [SEQ, D], F32, kind="Internal").ap()

    with ExitStack() as st:
        S = Sched(nc)
        AR = Arena(nc, 212000)
        ps = []
        for i in range(8):
            ps.append((nc.alloc_psum_tensor("ps%d" % i, [128, 512], F32), ("ps", i)))

        def psf(i):
            return ps[i][0][:, :], ps[i][1]

        def psb(i):
            return ps[i][0][:, :].bitcast(BF16), ps[i][1]

        def act(out, in_, func, reads, writes, **kw):
            S.op("act", lambda e: e.activation(out=out, in_=in_, func=func, **kw), reads, writes)

        def mm(out, lhsT, rhs, start, stop, reads, writes):
            S.op("pe", lambda e: e.matmul(out, lhsT, rhs, start=start, stop=stop), reads, writes)

        def tr(out, in_, ident, reads, writes):
            S.op("pe", lambda e: e.transpose(out=out, in_=in_, identity=ident), reads, writes)

        def tt(eng, out, in0, in1, op, reads, writes):
            S.op(eng, lambda e: e.tensor_tensor(out=out, in0=in0, in1=in1, op=op), reads, writes)

        def ts(eng, out, in0, s1, s2, op0, op1, reads, writes):
            if s2 is None:
                S.op(eng, lambda e: e.tensor_scalar(out=out, in0=in0, scalar1=s1, scalar2=None, op0=op0),
                     reads, writes)
            else:
                S.op(eng, lambda e: e.tensor_scalar(out=out, in0=in0, scalar1=s1, scalar2=s2, op0=op0, op1=op1),
                     reads, writes)

        def stt(out, in0, scalar, in1, op0, op1, reads, writes):
            S.op("dve", lambda e: e.scalar_tensor_tensor(out=out, in0=in0, scalar=scalar, in1=in1,
                                                         op0=op0, op1=op1), reads, writes)

        def cp(eng, out, in_, reads, writes):
            S.op(eng, lambda e: e.tensor_copy(out=out, in_=in_), reads, writes)

        def recip(out, in_, reads, writes):
            S.op("dve", lambda e: e.reciprocal(out=out, in_=in_), reads, writes)

        def mset(eng, ap, val, writes):
            S.op(eng, lambda e: e.memset(ap, val), (), writes)

        def dma(q, out, in_, semkey, reads, writes, group=None):
            S.dma(q, lambda e: e.dma_start(out=out, in_=in_), semkey, reads, writes, group)

        identb, k_identb = AR.alloc([128], BF16)
        maskb, k_maskb = AR.alloc([128], BF16)
        bonesb, k_bonesb = AR.alloc([128], BF16)
        o256b, k_o256b = AR.alloc([128], BF16)
        invdiv, k_invdiv = AR.alloc([2, 512], F32)
        ones8, k_ones8 = AR.alloc([512], F32, parts=8)
        pp, k_pp = AR.alloc([2, NPP], F32)
        dma("pool", identb, cst_d[:, 0:128], "c0", (), [k_identb])
        dma("pool", maskb, cst_d[:, 128:256], "c1", (), [k_maskb])
        dma("pool", bonesb, cst_d[:, 256:384], "c2", (), [k_bonesb])
        dma("sp", invdiv, cst_d[:, 384:1408].rearrange("p (a b) -> p a b", a=2), "c3", (), [k_invdiv])
        dma("sp", pp, pp_d.rearrange("l p n -> p l n"), "c4", (), [k_pp])
        mset("pool", o256b, 1.0 / 256.0, [k_o256b])
        mset("pool", ones8, 1.0, [k_ones8])
        base_top = AR.top

        def ppc(l, c, n=1, parts=128):
            return pp[0:parts, l, c:c + n]

        def rmsnorm_to_hT(l, xt, k_xt, gcol, hT, k_hT, sm, psT_rot):
            junk, k_junk, ss, k_ss, sd, k_sd, rstd, k_rstd, xn, k_xn = sm
            for s in range(4):
                act(junk, xt[:, s, :], AF.Square, [k_xt], [k_junk, k_ss], accum_out=ss[:, s:s + 1])
            act(sd, ss, AF.Sqrt, [k_ss], [k_sd], scale=1.0 / D, bias=EPS)
            recip(rstd, sd, [k_sd], [k_rstd])
            for s in range(4):
                eng = "dve" if s % 2 == 0 else "pool"
                ts(eng, xn[:, s, :], xt[:, s, :], rstd[:, s:s + 1], None, ALU.mult, None,
                   [k_xt, k_rstd], [(k_xn, s)])
            for c in range(8):
                pt, k_pt = psT_rot.next()
                for s in range(4):
                    tr(pt[:, s * 128:(s + 1) * 128], xn[:, s, c * 128:(c + 1) * 128], identb,
                       [(k_xn, s), k_identb], [k_pt])
                if c % 2 == 0:
                    act(hT[:, c, :], pt[:, 0:512], AF.Copy, [k_pt, k_pp], [(k_hT, c)], scale=ppc(l, gcol + c))
                else:
                    ts("dve", hT[:, c, :], pt[:, 0:512], ppc(l, gcol + c), None, ALU.mult, None,
                       [k_pt, k_pp], [(k_hT, c)])

        def phase_A(l, x_src):
            AR.top = base_top
            win, k_win = AR.alloc([8, DIN], BF16)
            pwb, k_pwb = AR.alloc([2, 256], BF16)
            pbd, k_pbd = AR.alloc([2, 128], BF16)
            dg, k_dg = AR.alloc([2, 31, 128], BF16)
            xt, k_xt = AR.alloc([4, 1024], F32)
            junk, k_junk = AR.alloc([1024], BF16)
            ss, k_ss = AR.alloc([4], F32)
            sd, k_sd = AR.alloc([4], F32)
            rstd, k_rstd = AR.alloc([4], F32)
            xn, k_xn = AR.alloc([4, 1024], BF16)
            hTs = [AR.alloc([8, 512], BF16) for _ in range(2)]
            qraws = Rot([AR.alloc([512], F32) for _ in range(2)])
            qsqs = Rot([AR.alloc([512], BF16) for _ in range(2)])
            sdqs = Rot([AR.alloc([512], F32) for _ in range(2)])
            rqs = Rot([AR.alloc([512], F32) for _ in range(2)])
            qsts = Rot([AR.alloc([512], BF16) for _ in range(2)])
            zf, k_zf = AR.alloc([512], F32, parts=8)
            ef, k_ef = zf, k_zf
            nl, k_nl = AR.alloc([512], F32, parts=8)
            Gb = [AR.alloc([512], F32, parts=8) for _ in range(2)]
            r1, k_r1 = AR.alloc([512], F32, parts=8)
            r2, k_r2 = AR.alloc([512], F32, parts=8)
            gk, k_gk = AR.alloc([6, 512], BF16, parts=8)
            gq, k_gq = AR.alloc([6, 512], BF16, parts=8)
            vsts = Rot([AR.alloc([4, 256], BF16) for _ in range(2)])
            sg, k_sg = AR.alloc([2, 512], F32)
            g2, k_g2 = AR.alloc([2, 542], BF16)
            cv, k_cv = AR.alloc([2, 512], F32)
            cvb, k_cvb = AR.alloc([2, 512], BF16)
            csq, k_csq = AR.alloc([2, 512], BF16)
            mu, k_mu = AR.alloc([512], F32)
            musq, k_musq = AR.alloc([512], F32)
            var, k_var = musq, k_musq
            sdc, k_sdc = AR.alloc([512], F32)
            rc, k_rc = sdc, k_sdc
            dd, k_dd = AR.alloc([2, 512], F32)
            sT, k_sT = AR.alloc([2, 512], BF16)
            PU, k_PU = AR.alloc([2, 528], F32)
            S2, k_S2 = AR.alloc([2, 528], F32)
            S4, k_S4 = AR.alloc([2, 528], F32)
            S8, k_S8 = AR.alloc([528], F32)
            S16, k_S16 = AR.alloc([528], F32)
            dT, k_dT = AR.alloc([2, 512], BF16)
            cpsts = Rot([AR.alloc([4, 512], BF16) for _ in range(2)])

            wv = w_in[l].rearrange("(c p) n -> p c n", p=128)
            for c in range(8):
                for hf in range(2):
                    dma("pool", win[:, c, hf * 1156:(hf + 1) * 1156], wv[:, c, hf * 1156:(hf + 1) * 1156],
                        "win", (), [(k_win, c, hf)], group=("win", l))
            dma("pool", pwb, pw_w[l].rearrange("(c p) n -> p c n", p=128), "pw", (), [k_pwb])
            mset("dve", pbd, 0.0, [(k_pbd, g_) for g_ in range(4)])
            for g in range(4):
                c, o = g // 2, g % 2
                dma("pool", pbd[64 * o:64 * o + 64, c, 64 * o:64 * o + 64], pool_w[l, g], "pbd", [], [(k_pbd, g)],
                    group=("pbd", l))
            for c in range(2):
                for k in range(31):
                    eng = "dve" if k % 2 == 0 else "pool"
                    ts(eng, dg[:, c, k, :], identb, ppc(l, C_DWW + c * 31 + k), None, ALU.mult, None,
                       [k_identb, k_pp], [(k_dg, c)])
            mset("pool", gk, 1.0, [k_gk])
            mset("pool", gq, 1.0, [k_gq])
            for (v_, kv_) in vsts.items:
                mset("pool", v_, 1.0, [kv_])
            mset("dve", g2[:, :, 0:30], 0.0, [k_g2])
            mset("dve", PU[:, :, 0:16], 0.0, [k_PU])
            mset("pool", S2, 0.0, [k_S2])
            mset("pool", S4, 0.0, [k_S4])
            mset("pool", S8, 0.0, [k_S8])
            mset("pool", S16, 0.0, [k_S16])

            psT_rot = Rot([psb(6), psb(7)])
            prot = Rot([psf(i) for i in range(6)])
            sm = (junk, k_junk, ss, k_ss, sd, k_sd, rstd, k_rstd, xn, k_xn)

            pending = []

            def defer(n, fn):
                pending.append([n, fn])

            def tick():
                for p in list(pending):
                    p[0] -= 1
                    if p[0] <= 0:
                        pending.remove(p)
                        p[1]()

            def flush():
                while pending:
                    tick()

            dma("sp", xt, x_src[0:512, :].rearrange("(s p) d -> p s d", p=128), "xt", (), [k_xt])
            for i in range(NT):
                T0 = i * 512
                hT, k_hT = hTs[i % 2]
                rmsnorm_to_hT(l, xt, k_xt, C_N1G, hT, k_hT, sm, psT_rot)
                if i + 1 < NT:
                    dma("sp", xt, x_src[T0 + 512:T0 + 1024, :].rearrange("(s p) d -> p s d", p=128),
                        "xt", (), [k_xt])
                hreads = [(k_hT, c) for c in range(8)]

                def proj(col, M):
                    pa, k_pa = prot.next()
                    for c in range(8):
                        mm(pa[0:M, :], win[:, c, col:col + M], hT[:, c, :], c == 0, c == 7,
                           [(k_win, c, 0), (k_win, c, 1), (k_hT, c)], [k_pa])
                    return pa, k_pa

                def qk_group(isq, hp):
                    col = (0 if isq else 512) + hp * 128
                    pa, k_pa = proj(col, 128)
                    qraw, k_qraw = qraws.next()
                    qsq, k_qsq = qsqs.next()
                    sdq, k_sdq = sdqs.next()
                    rq, k_rq = rqs.next()
                    qst, k_qst = qsts.next()
                    act(qraw, pa, AF.Copy, [k_pa], [k_qraw])
                    tt("pool", qsq, qraw, qraw, ALU.mult, [k_qraw], [k_qsq])

                    def stage2():
                        pb, k_pb = prot.next()
                        mm(pb, bonesb, qsq, True, True, [k_bonesb, k_qsq], [k_pb])
                        if isq:
                            act(sdq, pb, AF.Sqrt, [k_pb], [k_sdq], scale=1.0, bias=64.0 * EPS)
                        else:
                            act(sdq, pb, AF.Sqrt, [k_pb], [k_sdq], scale=1.0 / 64.0, bias=EPS)
                        recip(rq, sdq, [k_sdq], [k_rq])
                        stt(qst, qraw, ppc(l, C_QG if isq else C_KG), rq, ALU.mult, ALU.mult,
                            [k_qraw, k_rq, k_pp], [k_qst])
                        dst = qs_d if isq else ks_d
                        for o in range(2):
                            h = 2 * hp + o
                            dma("sp", dst[h, 0:64, T0:T0 + 512], qst[64 * o:64 * o + 64, :],
                                ("qst", qsts.i % 2), [k_qst], [("qk", isq, h, i)])
                    defer(2, stage2)

                for hp in range(4):
                    qk_group(True, hp)
                    tick()
                    qk_group(False, hp)
                    tick()

                pa, k_pa = proj(1536, 8)
                act(zf, pa[0:8, :], AF.Identity, [k_pa, k_pp], [k_zf], bias=ppc(l, C_BF, 1, 8))
                act(ef, zf, AF.Exp, [k_zf], [k_ef], scale=-1.0)
                act(nl, ef, AF.Ln, [k_ef], [k_nl], bias=1.0)
                G, k_G = Gb[i % 2]
                Gp, k_Gp = Gb[(i + 1) % 2]
                if i == 0:
                    S.op("dve", lambda e, G=G: e.tensor_tensor_scan(out=G, data0=ones8, data1=nl, initial=0.0,
                                                                    op0=ALU.mult, op1=ALU.add),
                         [k_ones8, k_nl], [k_G])
                else:
                    S.op("dve", lambda e, G=G, Gp=Gp: e.tensor_tensor_scan(
                        out=G, data0=ones8, data1=nl, initial=Gp[:, 511:512], op0=ALU.mult, op1=ALU.add),
                        [k_ones8, k_nl, k_Gp], [k_G])
                cp("dve", gk[:, 3, :], G, [k_G], [k_gk])
                tt("dve", r1, G, gk[:, 3, :], ALU.subtract, [k_G, k_gk], [k_r1])
                cp("dve", gk[:, 4, :], r1, [k_r1], [k_gk])
                tt("dve", r2, r1, gk[:, 4, :], ALU.subtract, [k_r1, k_gk], [k_r2])
                cp("dve", gk[:, 5, :], r2, [k_r2], [k_gk])
                ts("dve", gq[:, 0:3, :], gk[:, 3:6, :], -1.0, None, ALU.mult, None, [k_gk], [k_gq])
                dma("sp", ks_d[:, 64:70, T0:T0 + 512], gk, "gk", [k_gk], [("gkd", i)])
                dma("sp", qs_d[:, 64:70, T0:T0 + 512], gq, "gq", [k_gq], [("gqd", i)])
                tick()

                for s in range(4):
                    pa, k_pa = prot.next()
                    for c in range(8):
                        mm(pa, hT[:, c, s * 128:(s + 1) * 128], win[:, c, 1024:1536], c == 0, c == 7,
                           [(k_win, c, 0), (k_win, c, 1), (k_hT, c)], [k_pa])
                    vst, k_vst = vsts.next()
                    pv4 = pa.rearrange("p (a b d) -> p a b d", a=4, b=2)
                    act(vst[:, :, 0:64], pv4[:, :, 0, :], AF.Copy, [k_pa], [k_vst])
                    cp("dve", vst[:, :, 192:256], pv4[:, :, 1, :], [k_pa], [k_vst])
                    dma("sp", vs_d[:, 4 * i + s, :], vst.rearrange("p a b -> p (a b)"),
                        ("vst", vsts.i % 2), [k_vst], [("vsd", i, s)])
                    tick()

                cpst, k_cpst = cpsts.next()
                if i > 0:
                    cp("pool", g2[:, :, 0:30], g2[:, :, 512:542], [k_g2], [k_g2])
                pas = [proj(1544 + 128 * c, 128) for c in range(2)]
                pbs = [proj(1544 + 256 + 128 * c, 128) for c in range(2)]
                for c in range(2):
                    act(sg[:, c, :], pbs[c][0], AF.Sigmoid, [pbs[c][1]], [(k_sg, c)])
                    tt("dve", g2[:, c, 30:542], pas[c][0], sg[:, c, :], ALU.mult,
                       [pas[c][1], (k_sg, c)], [k_g2])
                tick()
                pcs = []
                for c in range(2):
                    pa, k_pa = prot.next()
                    for k in range(31):
                        mm(pa, dg[:, c, k, :], g2[:, c, k:k + 512], k == 0, k == 30, [(k_dg, c), k_g2], [k_pa])
                    act(cv[:, c, :], pa, AF.Identity, [k_pa, k_pp], [(k_cv, c)], bias=ppc(l, C_DWB + c))
                    cp("pool", cvb[:, c, :], cv[:, c, :], [(k_cv, c)], [(k_cvb, c)])
                    tt("pool", csq[:, c, :], cv[:, c, :], cv[:, c, :], ALU.mult, [(k_cv, c)], [(k_csq, c)])

                pps = [proj(2056 + 128 * c, 128) for c in range(2)]
                if i > 0:
                    cp("pool", PU[:, :, 0:16], PU[:, :, 512:528], [k_PU], [k_PU])
                for c in range(2):
                    act(PU[:, c, 16:528], pps[c][0], AF.Copy, [pps[c][1]], [k_PU])
                tt("pool", S2[:, :, 1:528], PU[:, :, 1:528], PU[:, :, 0:527], ALU.add, [k_PU], [k_S2])
                tt("pool", S4[:, :, 3:528], S2[:, :, 3:528], S2[:, :, 1:526], ALU.add, [k_S2], [k_S4])
                tt("pool", S8[:, 7:528], S4[:, 1, 7:528], S4[:, 1, 3:524], ALU.add, [k_S4], [k_S8])
                tt("pool", S16[64:128, 15:528], S8[64:128, 15:528], S8[64:128, 7:520], ALU.add, [k_S8], [k_S16])
                srcs = [(S2[0:64, 0, 16:528], k_S2, 0.5, 0, 0), (S4[64:128, 0, 16:528], k_S4, 0.25, 0, 64),
                        (S8[0:64, 16:528], k_S8, 0.125, 1, 0), (S16[64:128, 16:528], k_S16, 0.0625, 1, 64)]
                for (sap, ksap, inv, c, p0) in srcs:
                    if i == 0:
                        tt("dve", dd[p0:p0 + 64, c, :], sap, invdiv[p0:p0 + 64, c, :], ALU.mult,
                           [ksap, k_invdiv], [(k_dd, c)])
                        tt("dve", dT[p0:p0 + 64, c, :], dd[p0:p0 + 64, c, :], PU[p0:p0 + 64, c, 16:528],
                           ALU.subtract, [(k_dd, c), k_PU], [(k_dT, c)])
                    else:
                        stt(dT[p0:p0 + 64, c, :], sap, inv, PU[p0:p0 + 64, c, 16:528], ALU.mult, ALU.subtract,
                            [ksap, k_PU], [(k_dT, c)])

                pmu, k_pmu = prot.next()
                for c in range(2):
                    mm(pmu, o256b, cvb[:, c, :], c == 0, c == 1, [k_o256b, (k_cvb, c)], [k_pmu])
                pex, k_pex = prot.next()
                for c in range(2):
                    mm(pex, o256b, csq[:, c, :], c == 0, c == 1, [k_o256b, (k_csq, c)], [k_pex])
                cp("dve", mu, pmu, [k_pmu], [k_mu])
                tt("dve", musq, mu, mu, ALU.mult, [k_mu], [k_musq])
                tt("dve", var, pex, musq, ALU.subtract, [k_pex, k_musq], [k_var])
                ts("dve", var, var, 0.0, None, ALU.max, None, [k_var], [k_var])
                act(sdc, var, AF.Sqrt, [k_var], [k_sdc], scale=1.0, bias=EPS)
                recip(rc, sdc, [k_sdc], [k_rc])
                for c in range(2):
                    tt("dve", dd[:, c, :], cv[:, c, :], mu, ALU.subtract, [(k_cv, c), k_mu], [(k_dd, c)])
                    tt("dve", dd[:, c, :], dd[:, c, :], rc, ALU.mult, [(k_dd, c), k_rc], [(k_dd, c)])
                    act(sT[:, c, :], dd[:, c, :], AF.Silu, [(k_dd, c), k_pp], [(k_sT, c)],
                        scale=ppc(l, C_LNG + c), bias=ppc(l, C_LNB + c))
                for c in range(2):
                    pa, k_pa = prot.next()
                    mm(pa, pbd[:, c, :], dT[:, c, :], True, True, [(k_pbd, 2 * c), (k_pbd, 2 * c + 1), (k_dT, c)], [k_pa])
                    act(cpst[:, 2 + c, :], pa, AF.Copy, [k_pa, k_pp], [k_cpst], scale=ppc(l, C_PSC + c))
                for co in range(2):
                    pa, k_pa = prot.next()
                    for ci in range(2):
                        mm(pa, pwb[:, ci, co * 128:(co + 1) * 128], sT[:, ci, :], ci == 0, ci == 1,
                           [k_pwb, (k_sT, ci)], [k_pa])
                    cp("dve", cpst[:, co, :], pa, [k_pa], [k_cpst])
                dma("sp", cps_d[:, :, T0:T0 + 512], cpst, ("cpst", cpsts.i % 2), [k_cpst], [("cpsd", i)])
                flush()
            S.barrier()

        def phase_B(l, x_src, x_dst):
            AR.top = base_top
            kaug = [AR.alloc([SEQ], BF16) for _ in range(8)]
            vaug, k_vaug = AR.alloc([32, 1024], BF16)
            wout, k_wout = AR.alloc([8, 1024], BF16)
            qcs = [AR.alloc([8, 512], BF16) for _ in range(2)]
            mixT, k_mixT = AR.alloc([8, 512], BF16)
            pts = Rot([AR.alloc([512], BF16) for _ in range(4)])
            recs = Rot([AR.alloc([512], F32) for _ in range(2)])
            xt, k_xt = AR.alloc([4, 1024], F32)

            for h in range(8):
                dma("sp", kaug[h][0][0:70, :], ks_d[h], ("kaug", h), (), [kaug[h][1]])
            for q4 in range(4):
                dma("sp", vaug[:, 8 * q4:8 * q4 + 8, :], vs_d[:, 8 * q4:8 * q4 + 8, :], "vaug", (),
                    [(k_vaug, q4)], group=("vaug", l))
            wv = w_out[l].rearrange("(c p) n -> p c n", p=128)
            for c in range(8):
                dma("pool", wout[:, c, :], wv[:, c, :], "wout", (), [(k_wout, c)], group=("wout", l))

            srot = Rot([psf(i) for i in range(4)])
            orot = Rot([psf(4), psf(5)])
            xrot = Rot([psf(6), psf(7)])

            def load_chunk(c):
                qc, k_qc = qcs[c % 2]
                T0 = c * 512
                dma("sp", qc[0:70, :, :], qs_d[:, :, T0:T0 + 512].rearrange("h r t -> r h t"),
                    ("qc", c % 2), (), [k_qc])

            load_chunk(0)
            for c in range(NT):
                T0 = c * 512
                qc, k_qc = qcs[c % 2]
                if c + 1 < NT:
                    load_chunk(c + 1)
                dma("sp", mixT[:, 4:8, :], cps_d[:, :, T0:T0 + 512], "mixcp", (), [(k_mixT, "cp")])
                dma("sp", xt, x_src[T0:T0 + 512, :].rearrange("(s p) d -> p s d", p=128), "xtb", (), [k_xt])
                nkb = 4 * c + 4
                items = [(h, kb) for h in range(8) for kb in range(nkb)]
                state = {}

                def s_stage(h, kb):
                    q0 = max(0, 128 * kb - T0)
                    pS, k_pS = srot.next()
                    diag = kb >= 4 * c
                    ka, k_ka = kaug[h]
                    mm(pS[:, q0:512], ka[0:70, kb * 128:(kb + 1) * 128], qc[0:70, h, q0:512], True, not diag,
                       [k_ka, k_qc], [k_pS])
                    if diag:
                        mm(pS[:, q0:q0 + 128], identb, maskb, False, True, [k_identb, k_maskb], [k_pS])
                    pt, k_pt = pts.next()
                    act(pt[:, q0:512], pS[:, q0:512], AF.Exp, [k_pS], [k_pt])
                    state[(h, kb)] = (pt, k_pt, q0)

                def pv_stage(h, kb):
                    pt, k_pt, q0 = state.pop((h, kb))
                    if kb == 0:
                        state[("o", h)] = orot.next()
                    pO, k_pO = state[("o", h)]
                    mm(pO[:, q0:512], vaug[:, kb, h * 128:(h + 1) * 128], pt[:, q0:512], kb == 0, kb == nkb - 1,
                       [(k_vaug, kb // 8), k_pt], [k_pO])
                    if kb == nkb - 1:
                        hp, o = h // 2, h % 2
                        rec, k_rec = recs.next()
                        a0, b0 = (0, 64) if o == 0 else (64, 0)
                        recip(rec[b0:b0 + 64, :], pO[b0:b0 + 64, :], [k_pO], [k_rec])
                        tt("dve", mixT[a0:a0 + 64, hp, :], pO[a0:a0 + 64, :], rec[b0:b0 + 64, :], ALU.mult,
                           [k_pO, k_rec], [(k_mixT, hp)])
                        state.pop(("o", h))

                DEPTH = 2
                for n_, (h, kb) in enumerate(items):
                    s_stage(h, kb)
                    if n_ >= DEPTH:
                        pv_stage(*items[n_ - DEPTH])
                for n_ in range(max(0, len(items) - DEPTH), len(items)):
                    pv_stage(*items[n_])

                mreads = [(k_mixT, hp) for hp in range(4)] + [(k_mixT, "cp")]
                for s in range(4):
                    for n in range(2):
                        pX, k_pX = xrot.next()
                        for kc in range(8):
                            mm(pX, mixT[:, kc, s * 128:(s + 1) * 128], wout[:, kc, n * 512:(n + 1) * 512],
                               kc == 0, kc == 7, mreads + [(k_wout, kc)], [k_pX])
                        tt("dve", xt[:, s, n * 512:(n + 1) * 512], pX, xt[:, s, n * 512:(n + 1) * 512], ALU.add,
                           [k_pX, k_xt], [k_xt])
                dma("sp", x_dst[T0:T0 + 512, :].rearrange("(s p) d -> p s d", p=128), xt, "xtb_st",
                    [k_xt], [("xmid", c)])
            S.barrier()

        def phase_C(l, x_src, x_dst):
            AR.top = base_top
            wup, k_wup = AR.alloc([8, 2 * DFF], BF16)
            wdn, k_wdn = AR.alloc([22, 1024], BF16)
            xt, k_xt = AR.alloc([4, 1024], F32)
            junk, k_junk = AR.alloc([1024], BF16)
            ss, k_ss = AR.alloc([4], F32)
            sd, k_sd = AR.alloc([4], F32)
            rstd, k_rstd = AR.alloc([4], F32)
            xn, k_xn = AR.alloc([4, 1024], BF16)
            hT, k_hT = AR.alloc([8, 512], BF16)
            actT, k_actT = AR.alloc([11, 512], BF16)
            ubs = Rot([AR.alloc([514], F32) for _ in range(2)])
            abufs = Rot([AR.alloc([512], F32) for _ in range(4)])
            sgs = Rot([AR.alloc([512], F32) for _ in range(2)])
            halo, k_halo = AR.alloc([44, 2], F32)

            wv = w_up[l].rearrange("(c p) n -> p c n", p=128)
            for c in range(8):
                for q4 in range(4):
                    dma("pool", wup[:, c, q4 * 1408:(q4 + 1) * 1408], wv[:, c, q4 * 1408:(q4 + 1) * 1408],
                        "wup", (), [(k_wup, c, q4)], group=("wup", l))
            wd = w_down[l].rearrange("(j p) n -> p j n", p=128)
            for j in range(22):
                dma("pool", wdn[:, j, :], wd[:, j, :], "wdn", (), [(k_wdn, j)], group=("wdn", l))
            mset("dve", halo, 0.0, [(k_halo, ch_) for ch_ in range(44)])

            psT_rot = Rot([psb(7)])
            urot = Rot([psf(i) for i in range(5)])
            drot = Rot([psf(5), psf(6)])
            sm = (junk, k_junk, ss, k_ss, sd, k_sd, rstd, k_rstd, xn, k_xn)

            dma("sp", xt, x_src[0:512, :].rearrange("(s p) d -> p s d", p=128), "xtc", (), [k_xt])
            for i in range(NT):
                T0 = i * 512
                rmsnorm_to_hT(l, xt, k_xt, C_N2G, hT, k_hT, sm, psT_rot)
                for half in range(2):
                    for jj in range(11):
                        j = half * 11 + jj
                        res = []
                        for which in range(2):
                            ch = j + 22 * which
                            col = ch * 128
                            pU, k_pU = urot.next()
                            for c in range(8):
                                mm(pU, wup[:, c, col:col + 128], hT[:, c, :], c == 0, c == 7,
                                   [(k_wup, c, col // 1408), (k_hT, c)], [k_pU])
                            ub, k_ub = ubs.next()
                            ab, k_ab = abufs.next()
                            cp("pool", ub[:, 0:2], halo[:, ch, :], [(k_halo, ch)], [k_ub])
                            act(ub[:, 2:514], pU, AF.Copy, [k_pU], [k_ub])
                            act(ab, pU, AF.Copy, [k_pU, k_pp], [k_ab], scale=ppc(l, C_FDW + ch * 3 + 2))
                            stt(ab, ub[:, 1:513], ppc(l, C_FDW + ch * 3 + 1), ab, ALU.mult, ALU.add,
                                [k_ub, k_ab, k_pp], [k_ab])
                            stt(ab, ub[:, 0:512], ppc(l, C_FDW + ch * 3 + 0), ab, ALU.mult, ALU.add,
                                [k_ub, k_ab, k_pp], [k_ab])
                            cp("pool", halo[:, ch, :], ub[:, 512:514], [k_ub], [(k_halo, ch)])
                            res.append((ab, k_ab))
                        sgb, k_sgb = sgs.next()
                        act(sgb, res[0][0], AF.Silu, [res[0][1]], [k_sgb])
                        tt("pool", actT[:, jj, :], sgb, res[1][0], ALU.mult, [k_sgb, res[1][1]], [(k_actT, jj)])
                    areads = [(k_actT, jj) for jj in range(11)]
                    for s in range(4):
                        for n in range(2):
                            pD, k_pD = drot.next()
                            for jj in range(11):
                                j = half * 11 + jj
                                mm(pD, actT[:, jj, s * 128:(s + 1) * 128], wdn[:, j, n * 512:(n + 1) * 512],
                                   jj == 0, jj == 10, areads + [(k_wdn, j)], [k_pD])
                            tt("dve", xt[:, s, n * 512:(n + 1) * 512], pD, xt[:, s, n * 512:(n + 1) * 512],
                               ALU.add, [k_pD, k_xt], [k_xt])
                dma("sp", x_dst[T0:T0 + 512, :].rearrange("(s p) d -> p s d", p=128), xt, "xtc_st",
                    [k_xt], [("xout", i)])
                if i + 1 < NT:
                    dma("sp", xt, x_src[T0 + 512:T0 + 1024, :].rearrange("(s p) d -> p s d", p=128),
                        "xtc", (), [k_xt])
            S.barrier()

        nl_ = len(layers)
        for li, l in enumerate(layers):
            src = x_in if li == 0 else xl_d
            dst = y_out if li == nl_ - 1 else xl_d
            phase_A(l, src)
            phase_B(l, src, xmid_d)
            phase_C(l, xmid_d, dst)
        S.emit(st)
    return nc


def _pack_params(inp):
    pp = np.zeros((2, 128, NPP), np.float32)
    for l in range(2):
        pp[l, :, C_N1G:C_N1G + 8] = inp["norm1_g"][l].reshape(8, 128).T
        pp[l, :, C_N2G:C_N2G + 8] = inp["norm2_g"][l].reshape(8, 128).T
        pp[l, :, C_QG] = np.tile(inp["q_norm_g"][l], 2)
        pp[l, :, C_KG] = np.tile(inp["k_norm_g"][l], 2)
        pp[l, :, C_DWB:C_DWB + 2] = inp["conv_dw_b"][l].reshape(2, 128).T
        pp[l, :, C_LNG:C_LNG + 2] = inp["conv_ln_g"][l].reshape(2, 128).T
        pp[l, :, C_LNB:C_LNB + 2] = inp["conv_ln_b"][l].reshape(2, 128).T
        pp[l, :, C_PSC:C_PSC + 2] = inp["pool_scale"][l].reshape(2, 128).T
        pp[l, 0:8, C_BF] = inp["b_f"][l]
        pp[l, :, C_DWW:C_DWW + 62] = inp["conv_dw_w"][l].T.reshape(2, 128, 31).transpose(1, 0, 2).reshape(128, 62)
        pp[l, :, C_FDW:C_FDW + 132] = inp["ffn_dw_w"][l].T.reshape(44, 128, 3).transpose(1, 0, 2).reshape(128, 132)
    return pp


def _consts():
    cst = np.zeros((128, 3 * 128 + 1024), np.float32)
    cst[:, 0:128] = np.eye(128, dtype=np.float32)
    k = np.arange(128)[:, None]
    q = np.arange(128)[None, :]
    cst[:, 128:256] = np.where(k > q, -30000.0, 0.0)
    cst[:, 256:384] = (k // 64 == q // 64).astype(np.float32)
    t = np.arange(512, dtype=np.float32) + 1.0
    wins = [2.0, 4.0, 8.0, 16.0]
    inv = np.zeros((128, 2, 512), np.float32)
    for g in range(4):
        c, o = g // 2, g % 2
        inv[64 * o:64 * o + 64, c, :] = 1.0 / np.minimum(t, wins[g])
    cst[:, 384:] = inv.reshape(128, 1024)
    return cst


FUSED = False
_CACHE = {}


def _get_prog(layers):
    key = tuple(layers)
    if key not in _CACHE:
        _CACHE[key] = build_program(list(layers))
    return _CACHE[key]


def kernel(**inputs):
    inp = {k: np.ascontiguousarray(np.asarray(v)) for k, v in inputs.items()}
    x = inp["x"].astype(np.float32, copy=False)
    pp = _pack_params(inp)
    cst = _consts()
    common = {"w_in": inp["w_in"], "w_out": inp["w_out"], "w_up": inp["w_up"], "w_down": inp["w_down"],
              "pw": inp["conv_pw_w"], "pool_w": inp["pool_w"], "pp": pp, "cst": cst}
    n = 8
    if FUSED:
        nc = _get_prog((0, 1))
        in_maps = [dict(common, x=x[b]) for b in range(n)]
        res = run_bass_kernel_spmd(nc, in_maps, core_ids=list(range(n)))
        return np.stack([r["y"] for r in res.results], axis=0).astype(np.float32)
    cur = x
    for l in range(2):
        nc = _get_prog((l,))
        in_maps = [dict(common, x=cur[b]) for b in range(n)]
        res = run_bass_kernel_spmd(nc, in_maps, core_ids=list(range(n)))
        cur = np.stack([r["y"] for r in res.results], axis=0).astype(np.float32)
    return cur
```

```python
import numpy as np
from contextlib import ExitStack
import concourse.bass as bass
import concourse.mybir as mybir
from concourse.bass_utils import run_bass_kernel_spmd

F32 = mybir.dt.float32
BF16 = mybir.dt.bfloat16
U8 = mybir.dt.uint8
AF = mybir.ActivationFunctionType
ALU = mybir.AluOpType

D = 1024
SEQ = 4096
NT = 8
DIN = 2312
DFF = 2816
EPS = 1e-6
NPP = 221
C_N1G, C_N2G, C_QG, C_KG, C_DWB, C_LNG, C_LNB, C_PSC, C_BF, C_DWW, C_FDW = 0, 8, 16, 17, 18, 20, 22, 24, 26, 27, 89

ENGS = ("pe", "act", "dve", "pool", "sp")


class Op:
    __slots__ = ("eng", "fn", "deps", "is_dma", "sem", "tick", "marked", "group")

    def __init__(self, eng, fn, is_dma):
        self.eng = eng
        self.fn = fn
        self.deps = []
        self.is_dma = is_dma
        self.sem = None
        self.tick = None
        self.marked = False
        self.group = None


class Sched:
    def __init__(self, nc):
        self.nc = nc
        self.streams = {e: [] for e in ENGS}
        self.last_writer = {}
        self.readers = {}
        self.all_ops = []
        self.gid = 0

    def _add(self, op, reads, writes):
        deps = {}
        for r in reads:
            w = self.last_writer.get(r)
            if w is not None:
                deps[id(w)] = w
        for wkey in writes:
            w = self.last_writer.get(wkey)
            if w is not None:
                deps[id(w)] = w
            for rd in self.readers.get(wkey, ()):
                deps[id(rd)] = rd
        op.deps = list(deps.values())
        for r in reads:
            self.readers.setdefault(r, []).append(op)
        for wkey in writes:
            self.last_writer[wkey] = op
            self.readers[wkey] = []
        self.streams[op.eng].append(op)
        self.all_ops.append(op)
        return op

    def op(self, eng, fn, reads=(), writes=()):
        return self._add(Op(eng, fn, False), reads, writes)

    def dma(self, eng, fn, semkey, reads=(), writes=(), group=None):
        o = Op(eng, fn, True)
        o.sem = semkey
        if group is None:
            self.gid += 1
            group = ("_g", self.gid)
        o.group = (semkey, group)
        return self._add(o, reads, writes)

    def barrier(self):
        lasts = []
        for e in ENGS:
            for o in reversed(self.streams[e]):
                if not o.is_dma and o.fn is not None:
                    lasts.append(o)
                    break
        dma_last = {}
        for o in self.all_ops:
            if o.is_dma:
                dma_last[o.sem] = o
        deps = lasts + list(dma_last.values())
        for e in ENGS:
            b = Op(e, None, False)
            b.deps = list(deps)
            self.streams[e].append(b)
            self.all_ops.append(b)
        self.last_writer = {}
        self.readers = {}

    def emit(self, stack):
        nc = self.nc

        def needs_wait(o, d):
            if d.is_dma:
                if o.is_dma and o.group == d.group:
                    return False
                return True
            if d.eng == o.eng and not o.is_dma and d.eng == "pe":
                return False
            return True

        for o in self.all_ops:
            for d in o.deps:
                if not d.is_dma and needs_wait(o, d):
                    d.marked = True
        esem = {}
        for e in ENGS:
            if e != "sp":
                esem[e] = stack.enter_context(nc.semaphore("s_" + e))
        for e in ENGS:
            c = 0
            for o in self.streams[e]:
                if o.is_dma or o.fn is None:
                    continue
                if o.marked:
                    c += 1
                    o.tick = c
                o.sem = esem.get(e)
        dsem = {}
        dcount = {}
        gmax = {}
        for o in self.all_ops:
            if o.is_dma:
                k = o.sem
                if k not in dsem:
                    dsem[k] = stack.enter_context(nc.semaphore("d_%d" % len(dsem)))
                    dcount[k] = 0
                dcount[k] += 16
                o.sem = dsem[k]
                gmax[o.group] = dcount[k]
        for o in self.all_ops:
            if o.is_dma:
                o.tick = gmax[o.group]
        final_waits = [(s, dcount[k]) for k, s in dsem.items()]
        block = stack.enter_context(nc.Block())

        def make(ename):
            def body(eng):
                waited = {}
                for o in self.streams[ename]:
                    for d in o.deps:
                        if not needs_wait(o, d):
                            continue
                        key = id(d.sem)
                        if waited.get(key, 0) >= d.tick:
                            continue
                        eng.wait_ge(d.sem, d.tick)
                        waited[key] = d.tick
                    if o.fn is None:
                        continue
                    inst = o.fn(eng)
                    if o.is_dma:
                        inst.then_inc(o.sem, 16)
                    elif o.marked:
                        inst.then_inc(o.sem, 1)
                if ename == "sp":
                    for s, v in final_waits:
                        eng.wait_ge(s, v)
            return body

        block.tensor(make("pe"))
        block.scalar(make("act"))
        block.vector(make("dve"))
        block.gpsimd(make("pool"))
        block.sync(make("sp"))


class Arena:
    def __init__(self, nc, nbytes):
        self.t = nc.alloc_sbuf_tensor("arena", [128, nbytes], U8)
        self.n = nbytes
        self.top = 0
        self.cnt = 0

    def alloc(self, free, dt, parts=128):
        esz = 4 if dt == F32 else 2
        n = esz
        for f in free:
            n *= f
        off = (self.top + 63) // 64 * 64
        self.top = off + n
        assert self.top <= self.n, ("SBUF arena overflow", self.top, self.n)
        ap = self.t[0:parts, off:off + n].bitcast(dt)
        if len(free) == 2:
            ap = ap.rearrange("p (a b) -> p a b", a=free[0])
        elif len(free) == 3:
            ap = ap.rearrange("p (a b c) -> p a b c", a=free[0], b=free[1])
        self.cnt += 1
        return ap, ("sb", self.cnt)


class Rot:
    def __init__(self, items):
        self.items = items
        self.i = 0

    def next(self):
        it = self.items[self.i % len(self.items)]
        self.i += 1
        return it


def build_program(layers):
    nc = bass.Bass("TRN2", target_bir_lowering=False)
    x_in = nc.dram_tensor("x", [SEQ, D], F32, kind="ExternalInput").ap()
    w_in = nc.dram_tensor("w_in", [2, D, DIN], F32, kind="ExternalInput").ap()
    w_out = nc.dram_tensor("w_out", [2, D, D], F32, kind="ExternalInput").ap()
    w_up = nc.dram_tensor("w_up", [2, D, 2 * DFF], F32, kind="ExternalInput").ap()
    w_down = nc.dram_tensor("w_down", [2, DFF, D], F32, kind="ExternalInput").ap()
    pw_w = nc.dram_tensor("pw", [2, 256, 256], F32, kind="ExternalInput").ap()
    pool_w = nc.dram_tensor("pool_w", [2, 4, 64, 64], F32, kind="ExternalInput").ap()
    pp_d = nc.dram_tensor("pp", [2, 128, NPP], F32, kind="ExternalInput").ap()
    cst_d = nc.dram_tensor("cst", [128, 3 * 128 + 1024], F32, kind="ExternalInput").ap()
    y_out = nc.dram_tensor("y", [SEQ, D], F32, kind="ExternalOutput").ap()
    qs_d = nc.dram_tensor("qs", [8, 70, SEQ], BF16, kind="Internal").ap()
    ks_d = nc.dram_tensor("ks", [8, 70, SEQ], BF16, kind="Internal").ap()
    vs_d = nc.dram_tensor("vs", [128, 32, 1024], BF16, kind="Internal").ap()
    cps_d = nc.dram_tensor("cps", [128, 4, SEQ], BF16, kind="Internal").ap()
    xmid_d = nc.dram_tensor("xmid", [SEQ, D], F32, kind="Internal").ap()
    xl_d = nc.dram_tensor("xl", [SEQ, D], F32, kind="Internal").ap()

    with ExitStack() as st:
        S = Sched(nc)
        AR = Arena(nc, 212000)
        ps = []
        for i in range(8):
            ps.append((nc.alloc_psum_tensor("ps%d" % i, [128, 512], F32), ("ps", i)))

        def psf(i):
            return ps[i][0][:, :], ps[i][1]

        def psb(i):
            return ps[i][0][:, :].bitcast(BF16), ps[i][1]

        def act(out, in_, func, reads, writes, **kw):
            S.op("act", lambda e: e.activation(out=out, in_=in_, func=func, **kw), reads, writes)

        def mm(out, lhsT, rhs, start, stop, reads, writes):
            S.op("pe", lambda e: e.matmul(out, lhsT, rhs, start=start, stop=stop), reads, writes)

        def tr(out, in_, ident, reads, writes):
            S.op("pe", lambda e: e.transpose(out=out, in_=in_, identity=ident), reads, writes)

        def tt(eng, out, in0, in1, op, reads, writes):
            S.op(eng, lambda e: e.tensor_tensor(out=out, in0=in0, in1=in1, op=op), reads, writes)

        def ts(eng, out, in0, s1, s2, op0, op1, reads, writes):
            if s2 is None:
                S.op(eng, lambda e: e.tensor_scalar(out=out, in0=in0, scalar1=s1, scalar2=None, op0=op0),
                     reads, writes)
            else:
                S.op(eng, lambda e: e.tensor_scalar(out=out, in0=in0, scalar1=s1, scalar2=s2, op0=op0, op1=op1),
                     reads, writes)

        def stt(out, in0, scalar, in1, op0, op1, reads, writes):
            S.op("dve", lambda e: e.scalar_tensor_tensor(out=out, in0=in0, scalar=scalar, in1=in1,
                                                         op0=op0, op1=op1), reads, writes)

        def cp(eng, out, in_, reads, writes):
            S.op(eng, lambda e: e.tensor_copy(out=out, in_=in_), reads, writes)

        def recip(out, in_, reads, writes):
            S.op("dve", lambda e: e.reciprocal(out=out, in_=in_), reads, writes)

        def mset(eng, ap, val, writes):
            S.op(eng, lambda e: e.memset(ap, val), (), writes)

        def dma(q, out, in_, semkey, reads, writes, group=None):
            S.dma(q, lambda e: e.dma_start(out=out, in_=in_), semkey, reads, writes, group)

        identb, k_identb = AR.alloc([128], BF16)
        maskb, k_maskb = AR.alloc([128], BF16)
        bonesb, k_bonesb = AR.alloc([128], BF16)
        o256b, k_o256b = AR.alloc([128], BF16)
        invdiv, k_invdiv = AR.alloc([2, 512], F32)
        ones8, k_ones8 = AR.alloc([512], F32, parts=8)
        pp, k_pp = AR.alloc([2, NPP], F32)
        dma("pool", identb, cst_d[:, 0:128], "c0", (), [k_identb])
        dma("pool", maskb, cst_d[:, 128:256], "c1", (), [k_maskb])
        dma("pool", bonesb, cst_d[:, 256:384], "c2", (), [k_bonesb])
        dma("sp", invdiv, cst_d[:, 384:1408].rearrange("p (a b) -> p a b", a=2), "c3", (), [k_invdiv])
        dma("sp", pp, pp_d.rearrange("l p n -> p l n"), "c4", (), [k_pp])
        mset("pool", o256b, 1.0 / 256.0, [k_o256b])
        mset("pool", ones8, 1.0, [k_ones8])
        base_top = AR.top

        def ppc(l, c, n=1, parts=128):
            return pp[0:parts, l, c:c + n]

        def rmsnorm_to_hT(l, xt, k_xt, gcol, hT, k_hT, sm, psT_rot):
            junk, k_junk, ss, k_ss, sd, k_sd, rstd, k_rstd, xn, k_xn = sm
            for s in range(4):
                act(junk, xt[:, s, :], AF.Square, [k_xt], [k_junk, k_ss], accum_out=ss[:, s:s + 1])
            act(sd, ss, AF.Sqrt, [k_ss], [k_sd], scale=1.0 / D, bias=EPS)
            recip(rstd, sd, [k_sd], [k_rstd])
            for s in range(4):
                eng = "dve" if s % 2 == 0 else "pool"
                ts(eng, xn[:, s, :], xt[:, s, :], rstd[:, s:s + 1], None, ALU.mult, None,
                   [k_xt, k_rstd], [(k_xn, s)])
            for c in range(8):
                pt, k_pt = psT_rot.next()
                for s in range(4):
                    tr(pt[:, s * 128:(s + 1) * 128], xn[:, s, c * 128:(c + 1) * 128], identb,
                       [(k_xn, s), k_identb], [k_pt])
                if c % 2 == 0:
                    act(hT[:, c, :], pt[:, 0:512], AF.Copy, [k_pt, k_pp], [(k_hT, c)], scale=ppc(l, gcol + c))
                else:
                    ts("dve", hT[:, c, :], pt[:, 0:512], ppc(l, gcol + c), None, ALU.mult, None,
                       [k_pt, k_pp], [(k_hT, c)])

        def phase_A(l, x_src):
            AR.top = base_top
            win, k_win = AR.alloc([8, DIN], BF16)
            pwb, k_pwb = AR.alloc([2, 256], BF16)
            pbd, k_pbd = AR.alloc([2, 128], BF16)
            dg, k_dg = AR.alloc([2, 31, 128], BF16)
            xt, k_xt = AR.alloc([4, 1024], F32)
            junk, k_junk = AR.alloc([1024], BF16)
            ss, k_ss = AR.alloc([4], F32)
            sd, k_sd = AR.alloc([4], F32)
            rstd, k_rstd = AR.alloc([4], F32)
            xn, k_xn = AR.alloc([4, 1024], BF16)
            hTs = [AR.alloc([8, 512], BF16) for _ in range(2)]
            qraws = Rot([AR.alloc([512], F32) for _ in range(2)])
            qsqs = Rot([AR.alloc([512], BF16) for _ in range(2)])
            sdqs = Rot([AR.alloc([512], F32) for _ in range(2)])
            rqs = Rot([AR.alloc([512], F32) for _ in range(2)])
            qsts = Rot([AR.alloc([512], BF16) for _ in range(2)])
            zf, k_zf = AR.alloc([512], F32, parts=8)
            ef, k_ef = zf, k_zf
            nl, k_nl = AR.alloc([512], F32, parts=8)
            Gb = [AR.alloc([512], F32, parts=8) for _ in range(2)]
            r1, k_r1 = AR.alloc([512], F32, parts=8)
            r2, k_r2 = AR.alloc([512], F32, parts=8)
            gk, k_gk = AR.alloc([6, 512], BF16, parts=8)
            gq, k_gq = AR.alloc([6, 512], BF16, parts=8)
            vsts = Rot([AR.alloc([4, 256], BF16) for _ in range(2)])
            sg, k_sg = AR.alloc([2, 512], F32)
            g2, k_g2 = AR.alloc([2, 542], BF16)
            cv, k_cv = AR.alloc([2, 512], F32)
            cvb, k_cvb = AR.alloc([2, 512], BF16)
            csq, k_csq = AR.alloc([2, 512], BF16)
            mu, k_mu = AR.alloc([512], F32)
            musq, k_musq = AR.alloc([512], F32)
            var, k_var = musq, k_musq
            sdc, k_sdc = AR.alloc([512], F32)
            rc, k_rc = sdc, k_sdc
            dd, k_dd = AR.alloc([2, 512], F32)
            sT, k_sT = AR.alloc([2, 512], BF16)
            PU, k_PU = AR.alloc([2, 528], F32)
            S2, k_S2 = AR.alloc([2, 528], F32)
            S4, k_S4 = AR.alloc([2, 528], F32)
            S8, k_S8 = AR.alloc([528], F32)
            S16, k_S16 = AR.alloc([528], F32)
            dT, k_dT = AR.alloc([2, 512], BF16)
            cpsts = Rot([AR.alloc([4, 512], BF16) for _ in range(2)])

            wv = w_in[l].rearrange("(c p) n -> p c n", p=128)
            for c in range(8):
                for hf in range(2):
                    dma("pool", win[:, c, hf * 1156:(hf + 1) * 1156], wv[:, c, hf * 1156:(hf + 1) * 1156],
                        "win", (), [(k_win, c, hf)], group=("win", l))
            dma("pool", pwb, pw_w[l].rearrange("(c p) n -> p c n", p=128), "pw", (), [k_pwb])
            mset("dve", pbd, 0.0, [(k_pbd, g_) for g_ in range(4)])
            for g in range(4):
                c, o = g // 2, g % 2
                dma("pool", pbd[64 * o:64 * o + 64, c, 64 * o:64 * o + 64], pool_w[l, g], "pbd", [], [(k_pbd, g)],
                    group=("pbd", l))
            for c in range(2):
                for k in range(31):
                    eng = "dve" if k % 2 == 0 else "pool"
                    ts(eng, dg[:, c, k, :], identb, ppc(l, C_DWW + c * 31 + k), None, ALU.mult, None,
                       [k_identb, k_pp], [(k_dg, c)])
            mset("pool", gk, 1.0, [k_gk])
            mset("pool", gq, 1.0, [k_gq])
            for (v_, kv_) in vsts.items:
                mset("pool", v_, 1.0, [kv_])
            mset("dve", g2[:, :, 0:30], 0.0, [k_g2])
            mset("dve", PU[:, :, 0:16], 0.0, [k_PU])
            mset("pool", S2, 0.0, [k_S2])
            mset("pool", S4, 0.0, [k_S4])
            mset("pool", S8, 0.0, [k_S8])
            mset("pool", S16, 0.0, [k_S16])

            psT_rot = Rot([psb(6), psb(7)])
            prot = Rot([psf(i) for i in range(6)])
            sm = (junk, k_junk, ss, k_ss, sd, k_sd, rstd, k_rstd, xn, k_xn)

            pending = []

            def defer(n, fn):
                pending.append([n, fn])

            def tick():
                for p in list(pending):
                    p[0] -= 1
                    if p[0] <= 0:
                        pending.remove(p)
                        p[1]()

            def flush():
                while pending:
                    tick()

            dma("sp", xt, x_src[0:512, :].rearrange("(s p) d -> p s d", p=128), "xt", (), [k_xt])
            for i in range(NT):
                T0 = i * 512
                hT, k_hT = hTs[i % 2]
                rmsnorm_to_hT(l, xt, k_xt, C_N1G, hT, k_hT, sm, psT_rot)
                if i + 1 < NT:
                    dma("sp", xt, x_src[T0 + 512:T0 + 1024, :].rearrange("(s p) d -> p s d", p=128),
                        "xt", (), [k_xt])
                hreads = [(k_hT, c) for c in range(8)]

                def proj(col, M):
                    pa, k_pa = prot.next()
                    for c in range(8):
                        mm(pa[0:M, :], win[:, c, col:col + M], hT[:, c, :], c == 0, c == 7,
                           [(k_win, c, 0), (k_win, c, 1), (k_hT, c)], [k_pa])
                    return pa, k_pa

                def qk_group(isq, hp):
                    col = (0 if isq else 512) + hp * 128
                    pa, k_pa = proj(col, 128)
                    qraw, k_qraw = qraws.next()
                    qsq, k_qsq = qsqs.next()
                    sdq, k_sdq = sdqs.next()
                    rq, k_rq = rqs.next()
                    qst, k_qst = qsts.next()
                    act(qraw, pa, AF.Copy, [k_pa], [k_qraw])
                    tt("pool", qsq, qraw, qraw, ALU.mult, [k_qraw], [k_qsq])

                    def stage2():
                        pb, k_pb = prot.next()
                        mm(pb, bonesb, qsq, True, True, [k_bonesb, k_qsq], [k_pb])
                        if isq:
                            act(sdq, pb, AF.Sqrt, [k_pb], [k_sdq], scale=1.0, bias=64.0 * EPS)
                        else:
                            act(sdq, pb, AF.Sqrt, [k_pb], [k_sdq], scale=1.0 / 64.0, bias=EPS)
                        recip(rq, sdq, [k_sdq], [k_rq])
                        stt(qst, qraw, ppc(l, C_QG if isq else C_KG), rq, ALU.mult, ALU.mult,
                            [k_qraw, k_rq, k_pp], [k_qst])
                        dst = qs_d if isq else ks_d
                        for o in range(2):
                            h = 2 * hp + o
                            dma("sp", dst[h, 0:64, T0:T0 + 512], qst[64 * o:64 * o + 64, :],
                                ("qst", qsts.i % 2), [k_qst], [("qk", isq, h, i)])
                    defer(2, stage2)

                for hp in range(4):
                    qk_group(True, hp)
                    tick()
                    qk_group(False, hp)
                    tick()

                pa, k_pa = proj(1536, 8)
                act(zf, pa[0:8, :], AF.Identity, [k_pa, k_pp], [k_zf], bias=ppc(l, C_BF, 1, 8))
                act(ef, zf, AF.Exp, [k_zf], [k_ef], scale=-1.0)
                act(nl, ef, AF.Ln, [k_ef], [k_nl], bias=1.0)
                G, k_G = Gb[i % 2]
                Gp, k_Gp = Gb[(i + 1) % 2]
                if i == 0:
                    S.op("dve", lambda e, G=G: e.tensor_tensor_scan(out=G, data0=ones8, data1=nl, initial=0.0,
                                                                    op0=ALU.mult, op1=ALU.add),
                         [k_ones8, k_nl], [k_G])
                else:
                    S.op("dve", lambda e, G=G, Gp=Gp: e.tensor_tensor_scan(
                        out=G, data0=ones8, data1=nl, initial=Gp[:, 511:512], op0=ALU.mult, op1=ALU.add),
                        [k_ones8, k_nl, k_Gp], [k_G])
                cp("dve", gk[:, 3, :], G, [k_G], [k_gk])
                tt("dve", r1, G, gk[:, 3, :], ALU.subtract, [k_G, k_gk], [k_r1])
                cp("dve", gk[:, 4, :], r1, [k_r1], [k_gk])
                tt("dve", r2, r1, gk[:, 4, :], ALU.subtract, [k_r1, k_gk], [k_r2])
                cp("dve", gk[:, 5, :], r2, [k_r2], [k_gk])
                ts("dve", gq[:, 0:3, :], gk[:, 3:6, :], -1.0, None, ALU.mult, None, [k_gk], [k_gq])
                dma("sp", ks_d[:, 64:70, T0:T0 + 512], gk, "gk", [k_gk], [("gkd", i)])
                dma("sp", qs_d[:, 64:70, T0:T0 + 512], gq, "gq", [k_gq], [("gqd", i)])
                tick()

                for s in range(4):
                    pa, k_pa = prot.next()
                    for c in range(8):
                        mm(pa, hT[:, c, s * 128:(s + 1) * 128], win[:, c, 1024:1536], c == 0, c == 7,
                           [(k_win, c, 0), (k_win, c, 1), (k_hT, c)], [k_pa])
                    vst, k_vst = vsts.next()
                    pv4 = pa.rearrange("p (a b d) -> p a b d", a=4, b=2)
                    act(vst[:, :, 0:64], pv4[:, :, 0, :], AF.Copy, [k_pa], [k_vst])
                    cp("dve", vst[:, :, 192:256], pv4[:, :, 1, :], [k_pa], [k_vst])
                    dma("sp", vs_d[:, 4 * i + s, :], vst.rearrange("p a b -> p (a b)"),
                        ("vst", vsts.i % 2), [k_vst], [("vsd", i, s)])
                    tick()

                cpst, k_cpst = cpsts.next()
                if i > 0:
                    cp("pool", g2[:, :, 0:30], g2[:, :, 512:542], [k_g2], [k_g2])
                pas = [proj(1544 + 128 * c, 128) for c in range(2)]
                pbs = [proj(1544 + 256 + 128 * c, 128) for c in range(2)]
                for c in range(2):
                    act(sg[:, c, :], pbs[c][0], AF.Sigmoid, [pbs[c][1]], [(k_sg, c)])
                    tt("dve", g2[:, c, 30:542], pas[c][0], sg[:, c, :], ALU.mult,
                       [pas[c][1], (k_sg, c)], [k_g2])
                tick()
                pcs = []
                for c in range(2):
                    pa, k_pa = prot.next()
                    for k in range(31):
                        mm(pa, dg[:, c, k, :], g2[:, c, k:k + 512], k == 0, k == 30, [(k_dg, c), k_g2], [k_pa])
                    act(cv[:, c, :], pa, AF.Identity, [k_pa, k_pp], [(k_cv, c)], bias=ppc(l, C_DWB + c))
                    cp("pool", cvb[:, c, :], cv[:, c, :], [(k_cv, c)], [(k_cvb, c)])
                    tt("pool", csq[:, c, :], cv[:, c, :], cv[:, c, :], ALU.mult, [(k_cv, c)], [(k_csq, c)])

                pps = [proj(2056 + 128 * c, 128) for c in range(2)]
                if i > 0:
                    cp("pool", PU[:, :, 0:16], PU[:, :, 512:528], [k_PU], [k_PU])
                for c in range(2):
                    act(PU[:, c, 16:528], pps[c][0], AF.Copy, [pps[c][1]], [k_PU])
                tt("pool", S2[:, :, 1:528], PU[:, :, 1:528], PU[:, :, 0:527], ALU.add, [k_PU], [k_S2])
                tt("pool", S4[:, :, 3:528], S2[:, :, 3:528], S2[:, :, 1:526], ALU.add, [k_S2], [k_S4])
                tt("pool", S8[:, 7:528], S4[:, 1, 7:528], S4[:, 1, 3:524], ALU.add, [k_S4], [k_S8])
                tt("pool", S16[64:128, 15:528], S8[64:128, 15:528], S8[64:128, 7:520], ALU.add, [k_S8], [k_S16])
                srcs = [(S2[0:64, 0, 16:528], k_S2, 0.5, 0, 0), (S4[64:128, 0, 16:528], k_S4, 0.25, 0, 64),
                        (S8[0:64, 16:528], k_S8, 0.125, 1, 0), (S16[64:128, 16:528], k_S16, 0.0625, 1, 64)]
                for (sap, ksap, inv, c, p0) in srcs:
                    if i == 0:
                        tt("dve", dd[p0:p0 + 64, c, :], sap, invdiv[p0:p0 + 64, c, :], ALU.mult,
                           [ksap, k_invdiv], [(k_dd, c)])
                        tt("dve", dT[p0:p0 + 64, c, :], dd[p0:p0 + 64, c, :], PU[p0:p0 + 64, c, 16:528],
                           ALU.subtract, [(k_dd, c), k_PU], [(k_dT, c)])
                    else:
                        stt(dT[p0:p0 + 64, c, :], sap, inv, PU[p0:p0 + 64, c, 16:528], ALU.mult, ALU.subtract,
                            [ksap, k_PU], [(k_dT, c)])

                pmu, k_pmu = prot.next()
                for c in range(2):
                    mm(pmu, o256b, cvb[:, c, :], c == 0, c == 1, [k_o256b, (k_cvb, c)], [k_pmu])
                pex, k_pex = prot.next()
                for c in range(2):
                    mm(pex, o256b, csq[:, c, :], c == 0, c == 1, [k_o256b, (k_csq, c)], [k_pex])
                cp("dve", mu, pmu, [k_pmu], [k_mu])
                tt("dve", musq, mu, mu, ALU.mult, [k_mu], [k_musq])
                tt("dve", var, pex, musq, ALU.subtract, [k_pex, k_musq], [k_var])
                ts("dve", var, var, 0.0, None, ALU.max, None, [k_var], [k_var])
                act(sdc, var, AF.Sqrt, [k_var], [k_sdc], scale=1.0, bias=EPS)
                recip(rc, sdc, [k_sdc], [k_rc])
                for c in range(2):
                    tt("dve", dd[:, c, :], cv[:, c, :], mu, ALU.subtract, [(k_cv, c), k_mu], [(k_dd, c)])
                    tt("dve", dd[:, c, :], dd[:, c, :], rc, ALU.mult, [(k_dd, c), k_rc], [(k_dd, c)])
                    act(sT[:, c, :], dd[:, c, :], AF.Silu, [(k_dd, c), k_pp], [(k_sT, c)],
                        scale=ppc(l, C_LNG + c), bias=ppc(l, C_LNB + c))
                for c in range(2):
                    pa, k_pa = prot.next()
                    mm(pa, pbd[:, c, :], dT[:, c, :], True, True, [(k_pbd, 2 * c), (k_pbd, 2 * c + 1), (k_dT, c)], [k_pa])
                    act(cpst[:, 2 + c, :], pa, AF.Copy, [k_pa, k_pp], [k_cpst], scale=ppc(l, C_PSC + c))
                for co in range(2):
                    pa, k_pa = prot.next()
                    for ci in range(2):
                        mm(pa, pwb[:, ci, co * 128:(co + 1) * 128], sT[:, ci, :], ci == 0, ci == 1,
                           [k_pwb, (k_sT, ci)], [k_pa])
                    cp("dve", cpst[:, co, :], pa, [k_pa], [k_cpst])
                dma("sp", cps_d[:, :, T0:T0 + 512], cpst, ("cpst", cpsts.i % 2), [k_cpst], [("cpsd", i)])
                flush()
            S.barrier()

        def phase_B(l, x_src, x_dst):
            AR.top = base_top
            kaug = [AR.alloc([SEQ], BF16) for _ in range(8)]
            vaug, k_vaug = AR.alloc([32, 1024], BF16)
            wout, k_wout = AR.alloc([8, 1024], BF16)
            qcs = [AR.alloc([8, 512], BF16) for _ in range(2)]
            mixT, k_mixT = AR.alloc([8, 512], BF16)
            pts = Rot([AR.alloc([512], BF16) for _ in range(4)])
            recs = Rot([AR.alloc([512], F32) for _ in range(2)])
            xt, k_xt = AR.alloc([4, 1024], F32)

            for h in range(8):
                dma("sp", kaug[h][0][0:70, :], ks_d[h], ("kaug", h), (), [kaug[h][1]])
            for q4 in range(4):
                dma("sp", vaug[:, 8 * q4:8 * q4 + 8, :], vs_d[:, 8 * q4:8 * q4 + 8, :], "vaug", (),
                    [(k_vaug, q4)], group=("vaug", l))
            wv = w_out[l].rearrange("(c p) n -> p c n", p=128)
            for c in range(8):
                dma("pool", wout[:, c, :], wv[:, c, :], "wout", (), [(k_wout, c)], group=("wout", l))

            srot = Rot([psf(i) for i in range(4)])
            orot = Rot([psf(4), psf(5)])
            xrot = Rot([psf(6), psf(7)])

            def load_chunk(c):
                qc, k_qc = qcs[c % 2]
                T0 = c * 512
                dma("sp", qc[0:70, :, :], qs_d[:, :, T0:T0 + 512].rearrange("h r t -> r h t"),
                    ("qc", c % 2), (), [k_qc])

            load_chunk(0)
            for c in range(NT):
                T0 = c * 512
                qc, k_qc = qcs[c % 2]
                if c + 1 < NT:
                    load_chunk(c + 1)
                dma("sp", mixT[:, 4:8, :], cps_d[:, :, T0:T0 + 512], "mixcp", (), [(k_mixT, "cp")])
                dma("sp", xt, x_src[T0:T0 + 512, :].rearrange("(s p) d -> p s d", p=128), "xtb", (), [k_xt])
                nkb = 4 * c + 4
                items = [(h, kb) for h in range(8) for kb in range(nkb)]
                state = {}

                def s_stage(h, kb):
                    q0 = max(0, 128 * kb - T0)
                    pS, k_pS = srot.next()
                    diag = kb >= 4 * c
                    ka, k_ka = kaug[h]
                    mm(pS[:, q0:512], ka[0:70, kb * 128:(kb + 1) * 128], qc[0:70, h, q0:512], True, not diag,
                       [k_ka, k_qc], [k_pS])
                    if diag:
                        mm(pS[:, q0:q0 + 128], identb, maskb, False, True, [k_identb, k_maskb], [k_pS])
                    pt, k_pt = pts.next()
                    act(pt[:, q0:512], pS[:, q0:512], AF.Exp, [k_pS], [k_pt])
                    state[(h, kb)] = (pt, k_pt, q0)

                def pv_stage(h, kb):
                    pt, k_pt, q0 = state.pop((h, kb))
                    if kb == 0:
                        state[("o", h)] = orot.next()
                    pO, k_pO = state[("o", h)]
                    mm(pO[:, q0:512], vaug[:, kb, h * 128:(h + 1) * 128], pt[:, q0:512], kb == 0, kb == nkb - 1,
                       [(k_vaug, kb // 8), k_pt], [k_pO])
                    if kb == nkb - 1:
                        hp, o = h // 2, h % 2
                        rec, k_rec = recs.next()
                        a0, b0 = (0, 64) if o == 0 else (64, 0)
                        recip(rec[b0:b0 + 64, :], pO[b0:b0 + 64, :], [k_pO], [k_rec])
                        tt("dve", mixT[a0:a0 + 64, hp, :], pO[a0:a0 + 64, :], rec[b0:b0 + 64, :], ALU.mult,
                           [k_pO, k_rec], [(k_mixT, hp)])
                        state.pop(("o", h))

                DEPTH = 2
                for n_, (h, kb) in enumerate(items):
                    s_stage(h, kb)
                    if n_ >= DEPTH:
                        pv_stage(*items[n_ - DEPTH])
                for n_ in range(max(0, len(items) - DEPTH), len(items)):
                    pv_stage(*items[n_])

                mreads = [(k_mixT, hp) for hp in range(4)] + [(k_mixT, "cp")]
                for s in range(4):
                    for n in range(2):
                        pX, k_pX = xrot.next()
                        for kc in range(8):
                            mm(pX, mixT[:, kc, s * 128:(s + 1) * 128], wout[:, kc, n * 512:(n + 1) * 512],
                               kc == 0, kc == 7, mreads + [(k_wout, kc)], [k_pX])
                        tt("dve", xt[:, s, n * 512:(n + 1) * 512], pX, xt[:, s, n * 512:(n + 1) * 512], ALU.add,
                           [k_pX, k_xt], [k_xt])
                dma("sp", x_dst[T0:T0 + 512, :].rearrange("(s p) d -> p s d", p=128), xt, "xtb_st",
                    [k_xt], [("xmid", c)])
            S.barrier()

        def phase_C(l, x_src, x_dst):
            AR.top = base_top
            wup, k_wup = AR.alloc([8, 2 * DFF], BF16)
            wdn, k_wdn = AR.alloc([22, 1024], BF16)
            xt, k_xt = AR.alloc([4, 1024], F32)
            junk, k_junk = AR.alloc([1024], BF16)
            ss, k_ss = AR.alloc([4], F32)
            sd, k_sd = AR.alloc([4], F32)
            rstd, k_rstd = AR.alloc([4], F32)
            xn, k_xn = AR.alloc([4, 1024], BF16)
            hT, k_hT = AR.alloc([8, 512], BF16)
            actT, k_actT = AR.alloc([11, 512], BF16)
            abufs = Rot([AR.alloc([512], F32) for _ in range(6)])
            sgs = Rot([AR.alloc([512], F32) for _ in range(3)])
            hl, k_hl = AR.alloc([44, 2], F32)
            HC, k_HC = AR.alloc([44, 2], F32)
            htmp, k_htmp = AR.alloc([44], F32)
            fdw = pp[:, l, C_FDW:C_FDW + 132].rearrange("p (c k) -> p c k", k=3)

            wv = w_up[l].rearrange("(c p) n -> p c n", p=128)
            for c in range(8):
                for q4 in range(4):
                    dma("pool", wup[:, c, q4 * 1408:(q4 + 1) * 1408], wv[:, c, q4 * 1408:(q4 + 1) * 1408],
                        "wup", (), [(k_wup, c, q4)], group=("wup", l))
            wd = w_down[l].rearrange("(j p) n -> p j n", p=128)
            for j in range(22):
                dma("pool", wdn[:, j, :], wd[:, j, :], "wdn", (), [(k_wdn, j)], group=("wdn", l))

            psT_rot = Rot([psb(7)])
            urot = Rot([psf(i) for i in range(5)])
            drot = Rot([psf(5), psf(6)])
            sm = (junk, k_junk, ss, k_ss, sd, k_sd, rstd, k_rstd, xn, k_xn)

            dma("sp", xt, x_src[0:512, :].rearrange("(s p) d -> p s d", p=128), "xtc", (), [k_xt])
            for i in range(NT):
                T0 = i * 512
                rmsnorm_to_hT(l, xt, k_xt, C_N2G, hT, k_hT, sm, psT_rot)
                if i > 0:
                    hlk = [(k_hl, ch_) for ch_ in range(44)]
                    tt("pool", HC[:, :, 0], hl[:, :, 1], fdw[:, :, 1], ALU.mult, hlk + [k_pp], [k_HC])
                    tt("pool", htmp, hl[:, :, 0], fdw[:, :, 0], ALU.mult, hlk + [k_pp], [k_htmp])
                    tt("pool", HC[:, :, 0], HC[:, :, 0], htmp, ALU.add, [k_HC, k_htmp], [k_HC])
                    tt("pool", HC[:, :, 1], hl[:, :, 1], fdw[:, :, 0], ALU.mult, hlk + [k_pp], [k_HC])
                for half in range(2):
                    for jj in range(11):
                        j = half * 11 + jj
                        res = []
                        for which in range(2):
                            ch = j + 22 * which
                            col = ch * 128
                            pU, k_pU = urot.next()
                            for c in range(8):
                                mm(pU, wup[:, c, col:col + 128], hT[:, c, :], c == 0, c == 7,
                                   [(k_wup, c, col // 1408), (k_hT, c)], [k_pU])
                            ab, k_ab = abufs.next()
                            act(ab, pU, AF.Copy, [k_pU, k_pp], [k_ab], scale=ppc(l, C_FDW + ch * 3 + 2))
                            if i > 0:
                                tt("pool", ab[:, 0:2], ab[:, 0:2], HC[:, ch, :], ALU.add, [k_ab, k_HC], [k_ab])
                            stt(ab[:, 1:512], pU[:, 0:511], ppc(l, C_FDW + ch * 3 + 1), ab[:, 1:512],
                                ALU.mult, ALU.add, [k_pU, k_ab, k_pp], [k_ab])
                            stt(ab[:, 2:512], pU[:, 0:510], ppc(l, C_FDW + ch * 3 + 0), ab[:, 2:512],
                                ALU.mult, ALU.add, [k_pU, k_ab, k_pp], [k_ab])
                            if i + 1 < NT:
                                act(hl[:, ch, :], pU[:, 510:512], AF.Copy, [k_pU], [(k_hl, ch)])
                            res.append((ab, k_ab))
                        sgb, k_sgb = sgs.next()
                        act(sgb, res[0][0], AF.Silu, [res[0][1]], [k_sgb])
                        tt("pool", actT[:, jj, :], sgb, res[1][0], ALU.mult, [k_sgb, res[1][1]], [(k_actT, jj)])
                    areads = [(k_actT, jj) for jj in range(11)]
                    for s in range(4):
                        for n in range(2):
                            pD, k_pD = drot.next()
                            for jj in range(11):
                                j = half * 11 + jj
                                mm(pD, actT[:, jj, s * 128:(s + 1) * 128], wdn[:, j, n * 512:(n + 1) * 512],
                                   jj == 0, jj == 10, areads + [(k_wdn, j)], [k_pD])
                            tt("dve", xt[:, s, n * 512:(n + 1) * 512], pD, xt[:, s, n * 512:(n + 1) * 512],
                               ALU.add, [k_pD, k_xt], [k_xt])
                dma("sp", x_dst[T0:T0 + 512, :].rearrange("(s p) d -> p s d", p=128), xt, "xtc_st",
                    [k_xt], [("xout", i)])
                if i + 1 < NT:
                    dma("sp", xt, x_src[T0 + 512:T0 + 1024, :].rearrange("(s p) d -> p s d", p=128),
                        "xtc", (), [k_xt])
            S.barrier()

        nl_ = len(layers)
        for li, l in enumerate(layers):
            src = x_in if li == 0 else xl_d
            dst = y_out if li == nl_ - 1 else xl_d
            phase_A(l, src)
            phase_B(l, src, xmid_d)
            phase_C(l, xmid_d, dst)
        S.emit(st)
    return nc


def _pack_params(inp):
    pp = np.zeros((2, 128, NPP), np.float32)
    for l in range(2):
        pp[l, :, C_N1G:C_N1G + 8] = inp["norm1_g"][l].reshape(8, 128).T
        pp[l, :, C_N2G:C_N2G + 8] = inp["norm2_g"][l].reshape(8, 128).T
        pp[l, :, C_QG] = np.tile(inp["q_norm_g"][l], 2)
        pp[l, :, C_KG] = np.tile(inp["k_norm_g"][l], 2)
        pp[l, :, C_DWB:C_DWB + 2] = inp["conv_dw_b"][l].reshape(2, 128).T
        pp[l, :, C_LNG:C_LNG + 2] = inp["conv_ln_g"][l].reshape(2, 128).T
        pp[l, :, C_LNB:C_LNB + 2] = inp["conv_ln_b"][l].reshape(2, 128).T
        pp[l, :, C_PSC:C_PSC + 2] = inp["pool_scale"][l].reshape(2, 128).T
        pp[l, 0:8, C_BF] = inp["b_f"][l]
        pp[l, :, C_DWW:C_DWW + 62] = inp["conv_dw_w"][l].T.reshape(2, 128, 31).transpose(1, 0, 2).reshape(128, 62)
        pp[l, :, C_FDW:C_FDW + 132] = inp["ffn_dw_w"][l].T.reshape(44, 128, 3).transpose(1, 0, 2).reshape(128, 132)
    return pp


def _consts():
    cst = np.zeros((128, 3 * 128 + 1024), np.float32)
    cst[:, 0:128] = np.eye(128, dtype=np.float32)
    k = np.arange(128)[:, None]
    q = np.arange(128)[None, :]
    cst[:, 128:256] = np.where(k > q, -30000.0, 0.0)
    cst[:, 256:384] = (k // 64 == q // 64).astype(np.float32)
    t = np.arange(512, dtype=np.float32) + 1.0
    wins = [2.0, 4.0, 8.0, 16.0]
    inv = np.zeros((128, 2, 512), np.float32)
    for g in range(4):
        c, o = g // 2, g % 2
        inv[64 * o:64 * o + 64, c, :] = 1.0 / np.minimum(t, wins[g])
    cst[:, 384:] = inv.reshape(128, 1024)
    return cst


FUSED = True
_CACHE = {}


def _get_prog(layers):
    key = tuple(layers)
    if key not in _CACHE:
        _CACHE[key] = build_program(list(layers))
    return _CACHE[key]


def kernel(**inputs):
    inp = {k: np.ascontiguousarray(np.asarray(v)) for k, v in inputs.items()}
    x = inp["x"].astype(np.float32, copy=False)
    pp = _pack_params(inp)
    cst = _consts()
    common = {"w_in": inp["w_in"], "w_out": inp["w_out"], "w_up": inp["w_up"], "w_down": inp["w_down"],
              "pw": inp["conv_pw_w"], "pool_w": inp["pool_w"], "pp": pp, "cst": cst}
    n = 8
    if FUSED:
        nc = _get_prog((0, 1))
        in_maps = [dict(common, x=x[b]) for b in range(n)]
        res = run_bass_kernel_spmd(nc, in_maps, core_ids=list(range(n)))
        return np.stack([r["y"] for r in res.results], axis=0).astype(np.float32)
    cur = x
    for l in range(2):
        nc = _get_prog((l,))
        in_maps = [dict(common, x=cur[b]) for b in range(n)]
        res = run_bass_kernel_spmd(nc, in_maps, core_ids=list(range(n)))
        cur = np.stack([r["y"] for r in res.results], axis=0).astype(np.float32)
    return cur
```

```python
import numpy as np
from contextlib import ExitStack
import concourse.bass as bass
import concourse.mybir as mybir
from concourse.bass_utils import run_bass_kernel_spmd

F32 = mybir.dt.float32
BF16 = mybir.dt.bfloat16
U8 = mybir.dt.uint8
AF = mybir.ActivationFunctionType
ALU = mybir.AluOpType

D = 1024
SEQ = 4096
NT = 8
DIN = 2312
DFF = 2816
EPS = 1e-6
NPP = 221
C_N1G, C_N2G, C_QG, C_KG, C_DWB, C_LNG, C_LNB, C_PSC, C_BF, C_DWW, C_FDW = 0, 8, 16, 17, 18, 20, 22, 24, 26, 27, 89

ENGS = ("pe", "act", "dve", "pool", "sp")


class Op:
    __slots__ = ("eng", "fn", "deps", "is_dma", "sem", "tick", "marked", "group")

    def __init__(self, eng, fn, is_dma):
        self.eng = eng
        self.fn = fn
        self.deps = []
        self.is_dma = is_dma
        self.sem = None
        self.tick = None
        self.marked = False
        self.group = None


class Sched:
    def __init__(self, nc):
        self.nc = nc
        self.streams = {e: [] for e in ENGS}
        self.last_writer = {}
        self.readers = {}
        self.all_ops = []
        self.gid = 0

    def _add(self, op, reads, writes):
        deps = {}
        for r in reads:
            w = self.last_writer.get(r)
            if w is not None:
                deps[id(w)] = w
        for wkey in writes:
            w = self.last_writer.get(wkey)
            if w is not None:
                deps[id(w)] = w
            for rd in self.readers.get(wkey, ()):
                deps[id(rd)] = rd
        op.deps = list(deps.values())
        for r in reads:
            self.readers.setdefault(r, []).append(op)
        for wkey in writes:
            self.last_writer[wkey] = op
            self.readers[wkey] = []
        self.streams[op.eng].append(op)
        self.all_ops.append(op)
        return op

    def op(self, eng, fn, reads=(), writes=()):
        return self._add(Op(eng, fn, False), reads, writes)

    def dma(self, eng, fn, semkey, reads=(), writes=(), group=None):
        o = Op(eng, fn, True)
        o.sem = semkey
        if group is None:
            self.gid += 1
            group = ("_g", self.gid)
        o.group = (semkey, group)
        return self._add(o, reads, writes)

    def barrier(self):
        lasts = []
        for e in ENGS:
            for o in reversed(self.streams[e]):
                if not o.is_dma and o.fn is not None:
                    lasts.append(o)
                    break
        dma_last = {}
        for o in self.all_ops:
            if o.is_dma:
                dma_last[o.sem] = o
        deps = lasts + list(dma_last.values())
        for e in ENGS:
            b = Op(e, None, False)
            b.deps = list(deps)
            self.streams[e].append(b)
            self.all_ops.append(b)
        self.last_writer = {}
        self.readers = {}

    def emit(self, stack):
        nc = self.nc

        def needs_wait(o, d):
            if d.is_dma:
                if o.is_dma and o.group == d.group:
                    return False
                return True
            if d.eng == o.eng and not o.is_dma and d.eng == "pe":
                return False
            return True

        for o in self.all_ops:
            for d in o.deps:
                if not d.is_dma and needs_wait(o, d):
                    d.marked = True
        esem = {}
        for e in ENGS:
            if e != "sp":
                esem[e] = stack.enter_context(nc.semaphore("s_" + e))
        for e in ENGS:
            c = 0
            for o in self.streams[e]:
                if o.is_dma or o.fn is None:
                    continue
                if o.marked:
                    c += 1
                    o.tick = c
                o.sem = esem.get(e)
        dsem = {}
        dcount = {}
        gmax = {}
        for o in self.all_ops:
            if o.is_dma:
                k = o.sem
                if k not in dsem:
                    dsem[k] = stack.enter_context(nc.semaphore("d_%d" % len(dsem)))
                    dcount[k] = 0
                dcount[k] += 16
                o.sem = dsem[k]
                gmax[o.group] = dcount[k]
        for o in self.all_ops:
            if o.is_dma:
                o.tick = gmax[o.group]
        final_waits = [(s, dcount[k]) for k, s in dsem.items()]
        block = stack.enter_context(nc.Block())

        def make(ename):
            def body(eng):
                waited = {}
                for o in self.streams[ename]:
                    for d in o.deps:
                        if not needs_wait(o, d):
                            continue
                        key = id(d.sem)
                        if waited.get(key, 0) >= d.tick:
                            continue
                        eng.wait_ge(d.sem, d.tick)
                        waited[key] = d.tick
                    if o.fn is None:
                        continue
                    inst = o.fn(eng)
                    if o.is_dma:
                        inst.then_inc(o.sem, 16)
                    elif o.marked:
                        inst.then_inc(o.sem, 1)
                if ename == "sp":
                    for s, v in final_waits:
                        eng.wait_ge(s, v)
            return body

        block.tensor(make("pe"))
        block.scalar(make("act"))
        block.vector(make("dve"))
        block.gpsimd(make("pool"))
        block.sync(make("sp"))


class Arena:
    def __init__(self, nc, nbytes):
        self.t = nc.alloc_sbuf_tensor("arena", [128, nbytes], U8)
        self.n = nbytes
        self.top = 0
        self.cnt = 0

    def alloc(self, free, dt, parts=128):
        esz = 4 if dt == F32 else 2
        n = esz
        for f in free:
            n *= f
        off = (self.top + 63) // 64 * 64
        self.top = off + n
        assert self.top <= self.n, ("SBUF arena overflow", self.top, self.n)
        ap = self.t[0:parts, off:off + n].bitcast(dt)
        if len(free) == 2:
            ap = ap.rearrange("p (a b) -> p a b", a=free[0])
        elif len(free) == 3:
            ap = ap.rearrange("p (a b c) -> p a b c", a=free[0], b=free[1])
        self.cnt += 1
        return ap, ("sb", self.cnt)


class Rot:
    def __init__(self, items):
        self.items = items
        self.i = 0

    def next(self):
        it = self.items[self.i % len(self.items)]
        self.i += 1
        return it


def build_program(layers):
    nc = bass.Bass("TRN2", target_bir_lowering=False)
    x_in = nc.dram_tensor("x", [SEQ, D], F32, kind="ExternalInput").ap()
    w_in = nc.dram_tensor("w_in", [2, D, DIN], F32, kind="ExternalInput").ap()
    w_out = nc.dram_tensor("w_out", [2, D, D], F32, kind="ExternalInput").ap()
    w_up = nc.dram_tensor("w_up", [2, D, 2 * DFF], F32, kind="ExternalInput").ap()
    w_down = nc.dram_tensor("w_down", [2, DFF, D], F32, kind="ExternalInput").ap()
    pw_w = nc.dram_tensor("pw", [2, 256, 256], F32, kind="ExternalInput").ap()
    pool_w = nc.dram_tensor("pool_w", [2, 4, 64, 64], F32, kind="ExternalInput").ap()
    pp_d = nc.dram_tensor("pp", [2, 128, NPP], F32, kind="ExternalInput").ap()
    cst_d = nc.dram_tensor("cst", [128, 3 * 128 + 1024], F32, kind="ExternalInput").ap()
    y_out = nc.dram_tensor("y", [SEQ, D], F32, kind="ExternalOutput").ap()
    qs_d = nc.dram_tensor("qs", [8, 70, SEQ], BF16, kind="Internal").ap()
    ks_d = nc.dram_tensor("ks", [8, 70, SEQ], BF16, kind="Internal").ap()
    vs_d = nc.dram_tensor("vs", [128, 32, 1024], BF16, kind="Internal").ap()
    cps_d = nc.dram_tensor("cps", [128, 4, SEQ], BF16, kind="Internal").ap()
    xmid_d = nc.dram_tensor("xmid", [SEQ, D], F32, kind="Internal").ap()
    xl_d = nc.dram_tensor("xl", [SEQ, D], F32, kind="Internal").ap()

    with ExitStack() as st:
        S = Sched(nc)
        AR = Arena(nc, 212000)
        ps = []
        for i in range(8):
            ps.append((nc.alloc_psum_tensor("ps%d" % i, [128, 512], F32), ("ps", i)))

        def psf(i):
            return ps[i][0][:, :], ps[i][1]

        def psb(i):
            return ps[i][0][:, :].bitcast(BF16), ps[i][1]

        def act(out, in_, func, reads, writes, **kw):
            S.op("act", lambda e: e.activation(out=out, in_=in_, func=func, **kw), reads, writes)

        def mm(out, lhsT, rhs, start, stop, reads, writes):
            S.op("pe", lambda e: e.matmul(out, lhsT, rhs, start=start, stop=stop), reads, writes)

        def tr(out, in_, ident, reads, writes):
            S.op("pe", lambda e: e.transpose(out=out, in_=in_, identity=ident), reads, writes)

        def tt(eng, out, in0, in1, op, reads, writes):
            S.op(eng, lambda e: e.tensor_tensor(out=out, in0=in0, in1=in1, op=op), reads, writes)

        def ts(eng, out, in0, s1, s2, op0, op1, reads, writes):
            if s2 is None:
                S.op(eng, lambda e: e.tensor_scalar(out=out, in0=in0, scalar1=s1, scalar2=None, op0=op0),
                     reads, writes)
            else:
                S.op(eng, lambda e: e.tensor_scalar(out=out, in0=in0, scalar1=s1, scalar2=s2, op0=op0, op1=op1),
                     reads, writes)

        def stt(out, in0, scalar, in1, op0, op1, reads, writes):
            S.op("dve", lambda e: e.scalar_tensor_tensor(out=out, in0=in0, scalar=scalar, in1=in1,
                                                         op0=op0, op1=op1), reads, writes)

        def cp(eng, out, in_, reads, writes):
            S.op(eng, lambda e: e.tensor_copy(out=out, in_=in_), reads, writes)

        def recip(out, in_, reads, writes):
            S.op("dve", lambda e: e.reciprocal(out=out, in_=in_), reads, writes)

        def mset(eng, ap, val, writes):
            S.op(eng, lambda e: e.memset(ap, val), (), writes)

        def dma(q, out, in_, semkey, reads, writes, group=None):
            S.dma(q, lambda e: e.dma_start(out=out, in_=in_), semkey, reads, writes, group)

        def xload(xt, k_xt, src, T0, semname):
            for s in range(4):
                dma("sp", xt[:, s, :], src[T0 + s * 128:T0 + (s + 1) * 128, :], (semname, s), (), [(k_xt, s)])

        def xstore(xt, k_xt, dst, T0, semname, s):
            dma("sp", dst[T0 + s * 128:T0 + (s + 1) * 128, :], xt[:, s, :], (semname, s), [(k_xt, s)],
                [("xdst", semname, T0, s)])

        identb, k_identb = AR.alloc([128], BF16)
        maskb, k_maskb = AR.alloc([128], BF16)
        bonesb, k_bonesb = AR.alloc([128], BF16)
        o256b, k_o256b = AR.alloc([128], BF16)
        invdiv, k_invdiv = AR.alloc([2, 512], F32)
        ones8, k_ones8 = AR.alloc([512], F32, parts=8)
        pp, k_pp = AR.alloc([2, NPP], F32)
        dma("pool", identb, cst_d[:, 0:128], "c0", (), [k_identb])
        dma("pool", maskb, cst_d[:, 128:256], "c1", (), [k_maskb])
        dma("pool", bonesb, cst_d[:, 256:384], "c2", (), [k_bonesb])
        dma("sp", invdiv, cst_d[:, 384:1408].rearrange("p (a b) -> p a b", a=2), "c3", (), [k_invdiv])
        dma("sp", pp, pp_d.rearrange("l p n -> p l n"), "c4", (), [k_pp])
        mset("pool", o256b, 1.0 / 256.0, [k_o256b])
        mset("pool", ones8, 1.0, [k_ones8])
        base_top = AR.top

        def ppc(l, c, n=1, parts=128):
            return pp[0:parts, l, c:c + n]

        def rmsnorm_to_hT(l, xt, k_xt, gcol, hT, k_hT, sm, psT_rot):
            junk, k_junk, ss, k_ss, sd, k_sd, rstd, k_rstd, xn, k_xn = sm
            for s in range(4):
                act(junk, xt[:, s, :], AF.Square, [(k_xt, s)], [k_junk, (k_ss, s)], accum_out=ss[:, s:s + 1])
            act(sd, ss, AF.Sqrt, [(k_ss, s_) for s_ in range(4)], [k_sd], scale=1.0 / D, bias=EPS)
            recip(rstd, sd, [k_sd], [k_rstd])
            for s in range(4):
                if s % 2 == 0:
                    ts("dve", xn[:, s, :], xt[:, s, :], rstd[:, s:s + 1], None, ALU.mult, None,
                       [(k_xt, s), k_rstd], [(k_xn, s)])
                else:
                    act(xn[:, s, :], xt[:, s, :], AF.Copy, [(k_xt, s), k_rstd], [(k_xn, s)],
                        scale=rstd[:, s:s + 1])
            for c in range(8):
                pt, k_pt = psT_rot.next()
                for s in range(4):
                    tr(pt[:, s * 128:(s + 1) * 128], xn[:, s, c * 128:(c + 1) * 128], identb,
                       [(k_xn, s), k_identb], [k_pt])
                if c % 2 == 0:
                    act(hT[:, c, :], pt[:, 0:512], AF.Copy, [k_pt, k_pp], [(k_hT, c)], scale=ppc(l, gcol + c))
                else:
                    ts("dve", hT[:, c, :], pt[:, 0:512], ppc(l, gcol + c), None, ALU.mult, None,
                       [k_pt, k_pp], [(k_hT, c)])

        def phase_A(l, x_src):
            AR.top = base_top
            win, k_win = AR.alloc([8, DIN], BF16)
            pwb, k_pwb = AR.alloc([2, 256], BF16)
            pbd, k_pbd = AR.alloc([2, 128], BF16)
            dg, k_dg = AR.alloc([2, 31, 128], BF16)
            xt, k_xt = AR.alloc([4, 1024], F32)
            junk, k_junk = AR.alloc([1024], BF16)
            ss, k_ss = AR.alloc([4], F32)
            sd, k_sd = AR.alloc([4], F32)
            rstd, k_rstd = AR.alloc([4], F32)
            xn, k_xn = AR.alloc([4, 1024], BF16)
            hTs = [AR.alloc([8, 512], BF16) for _ in range(2)]
            qraws = Rot([AR.alloc([512], F32) for _ in range(2)])
            qsqs = Rot([AR.alloc([512], BF16) for _ in range(2)])
            sdqs = Rot([AR.alloc([512], F32) for _ in range(2)])
            rqs = Rot([AR.alloc([512], F32) for _ in range(2)])
            qsts = Rot([AR.alloc([512], BF16) for _ in range(2)])
            zf, k_zf = AR.alloc([512], F32, parts=8)
            ef, k_ef = zf, k_zf
            nl, k_nl = AR.alloc([512], F32, parts=8)
            Gb = [AR.alloc([512], F32, parts=8) for _ in range(2)]
            r1, k_r1 = AR.alloc([512], F32, parts=8)
            r2, k_r2 = AR.alloc([512], F32, parts=8)
            gk, k_gk = AR.alloc([6, 512], BF16, parts=8)
            gq, k_gq = AR.alloc([6, 512], BF16, parts=8)
            vsts = Rot([AR.alloc([4, 256], BF16) for _ in range(2)])
            sg, k_sg = AR.alloc([2, 512], F32)
            g2, k_g2 = AR.alloc([2, 542], BF16)
            cv, k_cv = AR.alloc([2, 512], F32)
            cvb, k_cvb = AR.alloc([2, 512], BF16)
            csq, k_csq = AR.alloc([2, 512], BF16)
            mu, k_mu = AR.alloc([512], F32)
            musq, k_musq = AR.alloc([512], F32)
            var, k_var = musq, k_musq
            sdc, k_sdc = AR.alloc([512], F32)
            rc, k_rc = sdc, k_sdc
            dd, k_dd = AR.alloc([2, 512], F32)
            sT, k_sT = AR.alloc([2, 512], BF16)
            PU, k_PU = AR.alloc([2, 528], F32)
            S2, k_S2 = AR.alloc([2, 528], F32)
            S4, k_S4 = AR.alloc([2, 528], F32)
            S8, k_S8 = AR.alloc([528], F32)
            S16, k_S16 = AR.alloc([528], F32)
            dT, k_dT = AR.alloc([2, 512], BF16)
            cpsts = Rot([AR.alloc([4, 512], BF16) for _ in range(2)])

            wv = w_in[l].rearrange("(c p) n -> p c n", p=128)
            for c in range(8):
                for hf in range(2):
                    dma("pool", win[:, c, hf * 1156:(hf + 1) * 1156], wv[:, c, hf * 1156:(hf + 1) * 1156],
                        "win", (), [(k_win, c, hf)], group=("win", l))
            dma("pool", pwb, pw_w[l].rearrange("(c p) n -> p c n", p=128), "pw", (), [k_pwb])
            mset("dve", pbd, 0.0, [(k_pbd, g_) for g_ in range(4)])
            for g in range(4):
                c, o = g // 2, g % 2
                dma("pool", pbd[64 * o:64 * o + 64, c, 64 * o:64 * o + 64], pool_w[l, g], "pbd", [], [(k_pbd, g)],
                    group=("pbd", l))
            for c in range(2):
                for k in range(31):
                    if k % 2 == 0:
                        ts("dve", dg[:, c, k, :], identb, ppc(l, C_DWW + c * 31 + k), None, ALU.mult, None,
                           [k_identb, k_pp], [(k_dg, c, k)])
                    else:
                        act(dg[:, c, k, :], identb, AF.Copy, [k_identb, k_pp], [(k_dg, c, k)],
                            scale=ppc(l, C_DWW + c * 31 + k))
            mset("pool", gk, 1.0, [k_gk])
            mset("pool", gq, 1.0, [k_gq])
            for (v_, kv_) in vsts.items:
                mset("pool", v_, 1.0, [kv_])
            mset("dve", g2[:, :, 0:30], 0.0, [k_g2])
            mset("dve", PU[:, :, 0:16], 0.0, [k_PU])
            mset("pool", S2, 0.0, [k_S2])
            mset("pool", S4, 0.0, [k_S4])
            mset("pool", S8, 0.0, [k_S8])
            mset("pool", S16, 0.0, [k_S16])

            psT_rot = Rot([psb(6), psb(7)])
            prot = Rot([psf(i) for i in range(6)])
            sm = (junk, k_junk, ss, k_ss, sd, k_sd, rstd, k_rstd, xn, k_xn)

            pending = []

            def defer(n, fn):
                pending.append([n, fn])

            def tick():
                for p in list(pending):
                    p[0] -= 1
                    if p[0] <= 0:
                        pending.remove(p)
                        p[1]()

            def flush():
                while pending:
                    tick()

            xload(xt, k_xt, x_src, 0, "xt")
            for i in range(NT):
                T0 = i * 512
                hT, k_hT = hTs[i % 2]
                rmsnorm_to_hT(l, xt, k_xt, C_N1G, hT, k_hT, sm, psT_rot)
                if i + 1 < NT:
                    xload(xt, k_xt, x_src, T0 + 512, "xt")
                hreads = [(k_hT, c) for c in range(8)]

                def proj(col, M):
                    pa, k_pa = prot.next()
                    for c in range(8):
                        mm(pa[0:M, :], win[:, c, col:col + M], hT[:, c, :], c == 0, c == 7,
                           [(k_win, c, 0), (k_win, c, 1), (k_hT, c)], [k_pa])
                    return pa, k_pa

                def qk_group(isq, hp):
                    col = (0 if isq else 512) + hp * 128
                    pa, k_pa = proj(col, 128)
                    qraw, k_qraw = qraws.next()
                    qsq, k_qsq = qsqs.next()
                    sdq, k_sdq = sdqs.next()
                    rq, k_rq = rqs.next()
                    qst, k_qst = qsts.next()
                    act(qraw, pa, AF.Copy, [k_pa], [k_qraw])
                    tt("pool", qsq, qraw, qraw, ALU.mult, [k_qraw], [k_qsq])

                    def stage2():
                        pb, k_pb = prot.next()
                        mm(pb, bonesb, qsq, True, True, [k_bonesb, k_qsq], [k_pb])
                        if isq:
                            act(sdq, pb, AF.Ln, [k_pb], [k_sdq], scale=1.0, bias=64.0 * EPS)
                        else:
                            act(sdq, pb, AF.Ln, [k_pb], [k_sdq], scale=1.0 / 64.0, bias=EPS)
                        act(rq, sdq, AF.Exp, [k_sdq], [k_rq], scale=-0.5)
                        stt(qst, qraw, ppc(l, C_QG if isq else C_KG), rq, ALU.mult, ALU.mult,
                            [k_qraw, k_rq, k_pp], [k_qst])
                        dst = qs_d if isq else ks_d
                        for o in range(2):
                            h = 2 * hp + o
                            dma("sp", dst[h, 0:64, T0:T0 + 512], qst[64 * o:64 * o + 64, :],
                                ("qst", qsts.i % 2), [k_qst], [("qk", isq, h, i)])
                    defer(2, stage2)

                for hp in range(4):
                    qk_group(True, hp)
                    tick()
                    qk_group(False, hp)
                    tick()

                pa, k_pa = proj(1536, 8)
                act(zf, pa[0:8, :], AF.Identity, [k_pa, k_pp], [k_zf], bias=ppc(l, C_BF, 1, 8))
                act(ef, zf, AF.Exp, [k_zf], [k_ef], scale=-1.0)
                act(nl, ef, AF.Ln, [k_ef], [k_nl], bias=1.0)
                G, k_G = Gb[i % 2]
                Gp, k_Gp = Gb[(i + 1) % 2]
                if i == 0:
                    S.op("dve", lambda e, G=G: e.tensor_tensor_scan(out=G, data0=ones8, data1=nl, initial=0.0,
                                                                    op0=ALU.mult, op1=ALU.add),
                         [k_ones8, k_nl], [k_G])
                else:
                    S.op("dve", lambda e, G=G, Gp=Gp: e.tensor_tensor_scan(
                        out=G, data0=ones8, data1=nl, initial=Gp[:, 511:512], op0=ALU.mult, op1=ALU.add),
                        [k_ones8, k_nl, k_Gp], [k_G])
                cp("dve", gk[:, 3, :], G, [k_G], [k_gk])
                tt("dve", r1, G, gk[:, 3, :], ALU.subtract, [k_G, k_gk], [k_r1])
                cp("dve", gk[:, 4, :], r1, [k_r1], [k_gk])
                tt("dve", r2, r1, gk[:, 4, :], ALU.subtract, [k_r1, k_gk], [k_r2])
                cp("dve", gk[:, 5, :], r2, [k_r2], [k_gk])
                ts("dve", gq[:, 0:3, :], gk[:, 3:6, :], -1.0, None, ALU.mult, None, [k_gk], [k_gq])
                dma("sp", ks_d[:, 64:70, T0:T0 + 512], gk, "gk", [k_gk], [("gkd", i)])
                dma("sp", qs_d[:, 64:70, T0:T0 + 512], gq, "gq", [k_gq], [("gqd", i)])
                tick()

                for s in range(4):
                    pa, k_pa = prot.next()
                    for c in range(8):
                        mm(pa, hT[:, c, s * 128:(s + 1) * 128], win[:, c, 1024:1536], c == 0, c == 7,
                           [(k_win, c, 0), (k_win, c, 1), (k_hT, c)], [k_pa])
                    vst, k_vst = vsts.next()
                    pv4 = pa.rearrange("p (a b d) -> p a b d", a=4, b=2)
                    act(vst[:, :, 0:64], pv4[:, :, 0, :], AF.Copy, [k_pa], [k_vst])
                    cp("dve", vst[:, :, 192:256], pv4[:, :, 1, :], [k_pa], [k_vst])
                    dma("sp", vs_d[:, 4 * i + s, :], vst.rearrange("p a b -> p (a b)"),
                        ("vst", vsts.i % 2), [k_vst], [("vsd", i, s)])
                    tick()

                cpst, k_cpst = cpsts.next()
                if i > 0:
                    cp("pool", g2[:, :, 0:30], g2[:, :, 512:542], [k_g2], [k_g2])
                pas = [proj(1544 + 128 * c, 128) for c in range(2)]
                pbs = [proj(1544 + 256 + 128 * c, 128) for c in range(2)]
                for c in range(2):
                    act(sg[:, c, :], pbs[c][0], AF.Sigmoid, [pbs[c][1]], [(k_sg, c)])
                    tt("dve", g2[:, c, 30:542], pas[c][0], sg[:, c, :], ALU.mult,
                       [pas[c][1], (k_sg, c)], [k_g2])
                tick()
                pcs = []
                for c in range(2):
                    pa, k_pa = prot.next()
                    for k in range(31):
                        mm(pa, dg[:, c, k, :], g2[:, c, k:k + 512], k == 0, k == 30, [(k_dg, c, k), k_g2], [k_pa])
                    act(cv[:, c, :], pa, AF.Identity, [k_pa, k_pp], [(k_cv, c)], bias=ppc(l, C_DWB + c))
                    cp("pool", cvb[:, c, :], cv[:, c, :], [(k_cv, c)], [(k_cvb, c)])
                    tt("pool", csq[:, c, :], cv[:, c, :], cv[:, c, :], ALU.mult, [(k_cv, c)], [(k_csq, c)])

                pps = [proj(2056 + 128 * c, 128) for c in range(2)]
                if i > 0:
                    cp("pool", PU[:, :, 0:16], PU[:, :, 512:528], [k_PU], [k_PU])
                for c in range(2):
                    act(PU[:, c, 16:528], pps[c][0], AF.Copy, [pps[c][1]], [k_PU])
                tt("pool", S2[:, :, 1:528], PU[:, :, 1:528], PU[:, :, 0:527], ALU.add, [k_PU], [k_S2])
                tt("pool", S4[:, :, 3:528], S2[:, :, 3:528], S2[:, :, 1:526], ALU.add, [k_S2], [k_S4])
                tt("pool", S8[:, 7:528], S4[:, 1, 7:528], S4[:, 1, 3:524], ALU.add, [k_S4], [k_S8])
                tt("pool", S16[64:128, 15:528], S8[64:128, 15:528], S8[64:128, 7:520], ALU.add, [k_S8], [k_S16])
                srcs = [(S2[0:64, 0, 16:528], k_S2, 0.5, 0, 0), (S4[64:128, 0, 16:528], k_S4, 0.25, 0, 64),
                        (S8[0:64, 16:528], k_S8, 0.125, 1, 0), (S16[64:128, 16:528], k_S16, 0.0625, 1, 64)]
                for (sap, ksap, inv, c, p0) in srcs:
                    if i == 0:
                        tt("dve", dd[p0:p0 + 64, c, :], sap, invdiv[p0:p0 + 64, c, :], ALU.mult,
                           [ksap, k_invdiv], [(k_dd, c)])
                        tt("dve", dT[p0:p0 + 64, c, :], dd[p0:p0 + 64, c, :], PU[p0:p0 + 64, c, 16:528],
                           ALU.subtract, [(k_dd, c), k_PU], [(k_dT, c)])
                    else:
                        stt(dT[p0:p0 + 64, c, :], sap, inv, PU[p0:p0 + 64, c, 16:528], ALU.mult, ALU.subtract,
                            [ksap, k_PU], [(k_dT, c)])

                pmu, k_pmu = prot.next()
                for c in range(2):
                    mm(pmu, o256b, cvb[:, c, :], c == 0, c == 1, [k_o256b, (k_cvb, c)], [k_pmu])
                pex, k_pex = prot.next()
                for c in range(2):
                    mm(pex, o256b, csq[:, c, :], c == 0, c == 1, [k_o256b, (k_csq, c)], [k_pex])
                cp("dve", mu, pmu, [k_pmu], [k_mu])
                tt("dve", musq, mu, mu, ALU.mult, [k_mu], [k_musq])
                tt("dve", var, pex, musq, ALU.subtract, [k_pex, k_musq], [k_var])
                ts("dve", var, var, 0.0, None, ALU.max, None, [k_var], [k_var])
                act(sdc, var, AF.Ln, [k_var], [k_sdc], scale=1.0, bias=EPS)
                act(rc, sdc, AF.Exp, [k_sdc], [k_rc], scale=-0.5)
                for c in range(2):
                    tt("dve", dd[:, c, :], cv[:, c, :], mu, ALU.subtract, [(k_cv, c), k_mu], [(k_dd, c)])
                    tt("dve", dd[:, c, :], dd[:, c, :], rc, ALU.mult, [(k_dd, c), k_rc], [(k_dd, c)])
                    act(sT[:, c, :], dd[:, c, :], AF.Silu, [(k_dd, c), k_pp], [(k_sT, c)],
                        scale=ppc(l, C_LNG + c), bias=ppc(l, C_LNB + c))
                for c in range(2):
                    pa, k_pa = prot.next()
                    mm(pa, pbd[:, c, :], dT[:, c, :], True, True, [(k_pbd, 2 * c), (k_pbd, 2 * c + 1), (k_dT, c)], [k_pa])
                    act(cpst[:, 2 + c, :], pa, AF.Copy, [k_pa, k_pp], [k_cpst], scale=ppc(l, C_PSC + c))
                for co in range(2):
                    pa, k_pa = prot.next()
                    for ci in range(2):
                        mm(pa, pwb[:, ci, co * 128:(co + 1) * 128], sT[:, ci, :], ci == 0, ci == 1,
                           [k_pwb, (k_sT, ci)], [k_pa])
                    cp("dve", cpst[:, co, :], pa, [k_pa], [k_cpst])
                dma("sp", cps_d[:, :, T0:T0 + 512], cpst, ("cpst", cpsts.i % 2), [k_cpst], [("cpsd", i)])
                flush()
            S.barrier()

        def phase_B(l, x_src, x_dst):
            AR.top = base_top
            kaug = [AR.alloc([SEQ], BF16) for _ in range(8)]
            vaug, k_vaug = AR.alloc([32, 1024], BF16)
            wout, k_wout = AR.alloc([8, 1024], BF16)
            qcs = [AR.alloc([8, 512], BF16) for _ in range(2)]
            mixT, k_mixT = AR.alloc([8, 512], BF16)
            pts = Rot([AR.alloc([512], BF16) for _ in range(4)])
            recs = Rot([AR.alloc([512], F32) for _ in range(2)])
            xt, k_xt = AR.alloc([4, 1024], F32)

            for h in range(8):
                dma("sp", kaug[h][0][0:70, :], ks_d[h], ("kaug", h), (), [kaug[h][1]])
            for q4 in range(4):
                dma("sp", vaug[:, 8 * q4:8 * q4 + 8, :], vs_d[:, 8 * q4:8 * q4 + 8, :], "vaug", (),
                    [(k_vaug, q4)], group=("vaug", l))
            wv = w_out[l].rearrange("(c p) n -> p c n", p=128)
            for c in range(8):
                dma("pool", wout[:, c, :], wv[:, c, :], "wout", (), [(k_wout, c)], group=("wout", l))

            srot = Rot([psf(i) for i in range(4)])
            orot = Rot([psf(4), psf(5)])
            xrot = Rot([psf(6), psf(7)])

            def load_chunk(c):
                qc, k_qc = qcs[c % 2]
                T0 = c * 512
                dma("sp", qc[0:70, :, :], qs_d[:, :, T0:T0 + 512].rearrange("h r t -> r h t"),
                    ("qc", c % 2), (), [k_qc])

            load_chunk(0)
            for c in range(NT):
                T0 = c * 512
                qc, k_qc = qcs[c % 2]
                if c + 1 < NT:
                    load_chunk(c + 1)
                dma("sp", mixT[:, 4:8, :], cps_d[:, :, T0:T0 + 512], "mixcp", (), [(k_mixT, "cp")])
                xload(xt, k_xt, x_src, T0, "xtb")
                nkb = 4 * c + 4
                items = [(h, kb) for h in range(8) for kb in range(nkb)]
                state = {}

                def s_stage(h, kb):
                    q0 = max(0, 128 * kb - T0)
                    pS, k_pS = srot.next()
                    diag = kb >= 4 * c
                    ka, k_ka = kaug[h]
                    mm(pS[:, q0:512], ka[0:70, kb * 128:(kb + 1) * 128], qc[0:70, h, q0:512], True, not diag,
                       [k_ka, k_qc], [k_pS])
                    if diag:
                        mm(pS[:, q0:q0 + 128], identb, maskb, False, True, [k_identb, k_maskb], [k_pS])
                    pt, k_pt = pts.next()
                    act(pt[:, q0:512], pS[:, q0:512], AF.Exp, [k_pS], [k_pt])
                    state[(h, kb)] = (pt, k_pt, q0)

                def pv_stage(h, kb):
                    pt, k_pt, q0 = state.pop((h, kb))
                    if kb == 0:
                        state[("o", h)] = orot.next()
                    pO, k_pO = state[("o", h)]
                    mm(pO[:, q0:512], vaug[:, kb, h * 128:(h + 1) * 128], pt[:, q0:512], kb == 0, kb == nkb - 1,
                       [(k_vaug, kb // 8), k_pt], [k_pO])
                    if kb == nkb - 1:
                        hp, o = h // 2, h % 2
                        rec, k_rec = recs.next()
                        a0, b0 = (0, 64) if o == 0 else (64, 0)
                        recip(rec[b0:b0 + 64, :], pO[b0:b0 + 64, :], [k_pO], [k_rec])
                        tt("dve", mixT[a0:a0 + 64, hp, :], pO[a0:a0 + 64, :], rec[b0:b0 + 64, :], ALU.mult,
                           [k_pO, k_rec], [(k_mixT, hp)])
                        state.pop(("o", h))

                DEPTH = 2
                for n_, (h, kb) in enumerate(items):
                    s_stage(h, kb)
                    if n_ >= DEPTH:
                        pv_stage(*items[n_ - DEPTH])
                for n_ in range(max(0, len(items) - DEPTH), len(items)):
                    pv_stage(*items[n_])

                mreads = [(k_mixT, hp) for hp in range(4)] + [(k_mixT, "cp")]
                for s in range(4):
                    for n in range(2):
                        pX, k_pX = xrot.next()
                        for kc in range(8):
                            mm(pX, mixT[:, kc, s * 128:(s + 1) * 128], wout[:, kc, n * 512:(n + 1) * 512],
                               kc == 0, kc == 7, mreads + [(k_wout, kc)], [k_pX])
                        tt("dve", xt[:, s, n * 512:(n + 1) * 512], pX, xt[:, s, n * 512:(n + 1) * 512], ALU.add,
                           [k_pX, (k_xt, s)], [(k_xt, s)])
                    xstore(xt, k_xt, x_dst, T0, "xtb_st", s)
            S.barrier()

        def phase_C(l, x_src, x_dst):
            AR.top = base_top
            wup, k_wup = AR.alloc([8, 2 * DFF], BF16)
            wdn, k_wdn = AR.alloc([22, 1024], BF16)
            xt, k_xt = AR.alloc([4, 1024], F32)
            junk, k_junk = AR.alloc([1024], BF16)
            ss, k_ss = AR.alloc([4], F32)
            sd, k_sd = AR.alloc([4], F32)
            rstd, k_rstd = AR.alloc([4], F32)
            xn, k_xn = AR.alloc([4, 1024], BF16)
            hT, k_hT = AR.alloc([8, 512], BF16)
            actT, k_actT = AR.alloc([11, 512], BF16)
            abufs = Rot([AR.alloc([512], F32) for _ in range(6)])
            sgs = Rot([AR.alloc([512], F32) for _ in range(3)])
            hl, k_hl = AR.alloc([44, 2], F32)
            HC, k_HC = AR.alloc([44, 2], F32)
            htmp, k_htmp = AR.alloc([44], F32)
            fdw = pp[:, l, C_FDW:C_FDW + 132].rearrange("p (c k) -> p c k", k=3)

            wv = w_up[l].rearrange("(c p) n -> p c n", p=128)
            for c in range(8):
                for q4 in range(4):
                    dma("pool", wup[:, c, q4 * 1408:(q4 + 1) * 1408], wv[:, c, q4 * 1408:(q4 + 1) * 1408],
                        "wup", (), [(k_wup, c, q4)], group=("wup", l))
            wd = w_down[l].rearrange("(j p) n -> p j n", p=128)
            for j in range(22):
                dma("pool", wdn[:, j, :], wd[:, j, :], "wdn", (), [(k_wdn, j)], group=("wdn", l))

            psT_rot = Rot([psb(7)])
            urot = Rot([psf(i) for i in range(5)])
            drot = Rot([psf(5), psf(6)])
            sm = (junk, k_junk, ss, k_ss, sd, k_sd, rstd, k_rstd, xn, k_xn)

            xload(xt, k_xt, x_src, 0, "xtc")
            for i in range(NT):
                T0 = i * 512
                rmsnorm_to_hT(l, xt, k_xt, C_N2G, hT, k_hT, sm, psT_rot)
                if i > 0:
                    hlk = [(k_hl, ch_) for ch_ in range(44)]
                    tt("pool", HC[:, :, 0], hl[:, :, 1], fdw[:, :, 1], ALU.mult, hlk + [k_pp], [k_HC])
                    tt("pool", htmp, hl[:, :, 0], fdw[:, :, 0], ALU.mult, hlk + [k_pp], [k_htmp])
                    tt("pool", HC[:, :, 0], HC[:, :, 0], htmp, ALU.add, [k_HC, k_htmp], [k_HC])
                    tt("pool", HC[:, :, 1], hl[:, :, 1], fdw[:, :, 0], ALU.mult, hlk + [k_pp], [k_HC])
                for half in range(2):
                    for jj in range(11):
                        j = half * 11 + jj
                        res = []
                        for which in range(2):
                            ch = j + 22 * which
                            col = ch * 128
                            pU, k_pU = urot.next()
                            for c in range(8):
                                mm(pU, wup[:, c, col:col + 128], hT[:, c, :], c == 0, c == 7,
                                   [(k_wup, c, col // 1408), (k_hT, c)], [k_pU])
                            ab, k_ab = abufs.next()
                            act(ab, pU, AF.Copy, [k_pU, k_pp], [k_ab], scale=ppc(l, C_FDW + ch * 3 + 2))
                            if i > 0:
                                tt("pool", ab[:, 0:2], ab[:, 0:2], HC[:, ch, :], ALU.add, [k_ab, k_HC], [k_ab])
                            stt(ab[:, 1:512], pU[:, 0:511], ppc(l, C_FDW + ch * 3 + 1), ab[:, 1:512],
                                ALU.mult, ALU.add, [k_pU, k_ab, k_pp], [k_ab])
                            stt(ab[:, 2:512], pU[:, 0:510], ppc(l, C_FDW + ch * 3 + 0), ab[:, 2:512],
                                ALU.mult, ALU.add, [k_pU, k_ab, k_pp], [k_ab])
                            if i + 1 < NT:
                                act(hl[:, ch, :], pU[:, 510:512], AF.Copy, [k_pU], [(k_hl, ch)])
                            res.append((ab, k_ab))
                        sgb, k_sgb = sgs.next()
                        act(sgb, res[0][0], AF.Silu, [res[0][1]], [k_sgb])
                        tt("pool", actT[:, jj, :], sgb, res[1][0], ALU.mult, [k_sgb, res[1][1]], [(k_actT, jj)])
                    areads = [(k_actT, jj) for jj in range(11)]
                    for s in range(4):
                        for n in range(2):
                            pD, k_pD = drot.next()
                            for jj in range(11):
                                j = half * 11 + jj
                                mm(pD, actT[:, jj, s * 128:(s + 1) * 128], wdn[:, j, n * 512:(n + 1) * 512],
                                   jj == 0, jj == 10, areads + [(k_wdn, j)], [k_pD])
                            tt("dve", xt[:, s, n * 512:(n + 1) * 512], pD, xt[:, s, n * 512:(n + 1) * 512],
                               ALU.add, [k_pD, (k_xt, s)], [(k_xt, s)])
                        if half == 1:
                            xstore(xt, k_xt, x_dst, T0, "xtc_st", s)
                            if i + 1 < NT:
                                dma("sp", xt[:, s, :], x_src[T0 + 512 + s * 128:T0 + 512 + (s + 1) * 128, :],
                                    ("xtc", s), (), [(k_xt, s)])
            S.barrier()

        nl_ = len(layers)
        for li, l in enumerate(layers):
            src = x_in if li == 0 else xl_d
            dst = y_out if li == nl_ - 1 else xl_d
            phase_A(l, src)
            phase_B(l, src, xmid_d)
            phase_C(l, xmid_d, dst)
        S.emit(st)
    return nc


def _pack_params(inp):
    pp = np.zeros((2, 128, NPP), np.float32)
    for l in range(2):
        pp[l, :, C_N1G:C_N1G + 8] = inp["norm1_g"][l].reshape(8, 128).T
        pp[l, :, C_N2G:C_N2G + 8] = inp["norm2_g"][l].reshape(8, 128).T
        pp[l, :, C_QG] = np.tile(inp["q_norm_g"][l], 2)
        pp[l, :, C_KG] = np.tile(inp["k_norm_g"][l], 2)
        pp[l, :, C_DWB:C_DWB + 2] = inp["conv_dw_b"][l].reshape(2, 128).T
        pp[l, :, C_LNG:C_LNG + 2] = inp["conv_ln_g"][l].reshape(2, 128).T
        pp[l, :, C_LNB:C_LNB + 2] = inp["conv_ln_b"][l].reshape(2, 128).T
        pp[l, :, C_PSC:C_PSC + 2] = inp["pool_scale"][l].reshape(2, 128).T
        pp[l, 0:8, C_BF] = inp["b_f"][l]
        pp[l, :, C_DWW:C_DWW + 62] = inp["conv_dw_w"][l].T.reshape(2, 128, 31).transpose(1, 0, 2).reshape(128, 62)
        pp[l, :, C_FDW:C_FDW + 132] = inp["ffn_dw_w"][l].T.reshape(44, 128, 3).transpose(1, 0, 2).reshape(128, 132)
    return pp


def _consts():
    cst = np.zeros((128, 3 * 128 + 1024), np.float32)
    cst[:, 0:128] = np.eye(128, dtype=np.float32)
    k = np.arange(128)[:, None]
    q = np.arange(128)[None, :]
    cst[:, 128:256] = np.where(k > q, -30000.0, 0.0)
    cst[:, 256:384] = (k // 64 == q // 64).astype(np.float32)
    t = np.arange(512, dtype=np.float32) + 1.0
    wins = [2.0, 4.0, 8.0, 16.0]
    inv = np.zeros((128, 2, 512), np.float32)
    for g in range(4):
        c, o = g // 2, g % 2
        inv[64 * o:64 * o + 64, c, :] = 1.0 / np.minimum(t, wins[g])
    cst[:, 384:] = inv.reshape(128, 1024)
    return cst


FUSED = True
_CACHE = {}


def _get_prog(layers):
    key = tuple(layers)
    if key not in _CACHE:
        _CACHE[key] = build_program(list(layers))
    return _CACHE[key]


def kernel(**inputs):
    inp = {k: np.ascontiguousarray(np.asarray(v)) for k, v in inputs.items()}
    x = inp["x"].astype(np.float32, copy=False)
    pp = _pack_params(inp)
    cst = _consts()
    common = {"w_in": inp["w_in"], "w_out": inp["w_out"], "w_up": inp["w_up"], "w_down": inp["w_down"],
              "pw": inp["conv_pw_w"], "pool_w": inp["pool_w"], "pp": pp, "cst": cst}
    n = 8
    if FUSED:
        nc = _get_prog((0, 1))
        in_maps = [dict(common, x=x[b]) for b in range(n)]
        res = run_bass_kernel_spmd(nc, in_maps, core_ids=list(range(n)))
        return np.stack([r["y"] for r in res.results], axis=0).astype(np.float32)
    cur = x
    for l in range(2):
        nc = _get_prog((l,))
        in_maps = [dict(common, x=cur[b]) for b in range(n)]
        res = run_bass_kernel_spmd(nc, in_maps, core_ids=list(range(n)))
        cur = np.stack([r["y"] for r in res.results], axis=0).astype(np.float32)
    return cur
```

```python
import numpy as np
from contextlib import ExitStack
import concourse.bass as bass
import concourse.mybir as mybir
from concourse.bass_utils import run_bass_kernel_spmd

F32 = mybir.dt.float32
BF16 = mybir.dt.bfloat16
U8 = mybir.dt.uint8
AF = mybir.ActivationFunctionType
ALU = mybir.AluOpType

D = 1024
SEQ = 4096
NT = 8
DIN = 2312
DFF = 2816
EPS = 1e-6
NPP = 221
C_N1G, C_N2G, C_QG, C_KG, C_DWB, C_LNG, C_LNB, C_PSC, C_BF, C_DWW, C_FDW = 0, 8, 16, 17, 18, 20, 22, 24, 26, 27, 89

ENGS = ("pe", "act", "dve", "pool", "sp")


class Op:
    __slots__ = ("eng", "fn", "deps", "is_dma", "sem", "tick", "marked", "group")

    def __init__(self, eng, fn, is_dma):
        self.eng = eng
        self.fn = fn
        self.deps = []
        self.is_dma = is_dma
        self.sem = None
        self.tick = None
        self.marked = False
        self.group = None


class Sched:
    def __init__(self, nc):
        self.nc = nc
        self.streams = {e: [] for e in ENGS}
        self.last_writer = {}
        self.readers = {}
        self.all_ops = []
        self.gid = 0

    def _add(self, op, reads, writes):
        deps = {}
        for r in reads:
            w = self.last_writer.get(r)
            if w is not None:
                deps[id(w)] = w
        for wkey in writes:
            w = self.last_writer.get(wkey)
            if w is not None:
                deps[id(w)] = w
            for rd in self.readers.get(wkey, ()):
                deps[id(rd)] = rd
        op.deps = list(deps.values())
        for r in reads:
            self.readers.setdefault(r, []).append(op)
        for wkey in writes:
            self.last_writer[wkey] = op
            self.readers[wkey] = []
        self.streams[op.eng].append(op)
        self.all_ops.append(op)
        return op

    def op(self, eng, fn, reads=(), writes=()):
        return self._add(Op(eng, fn, False), reads, writes)

    def dma(self, eng, fn, semkey, reads=(), writes=(), group=None):
        o = Op(eng, fn, True)
        o.sem = semkey
        if group is None:
            self.gid += 1
            group = ("_g", self.gid)
        o.group = (semkey, group)
        return self._add(o, reads, writes)

    def barrier(self):
        lasts = []
        for e in ENGS:
            for o in reversed(self.streams[e]):
                if not o.is_dma and o.fn is not None:
                    lasts.append(o)
                    break
        dma_last = {}
        for o in self.all_ops:
            if o.is_dma:
                dma_last[o.sem] = o
        deps = lasts + list(dma_last.values())
        for e in ENGS:
            b = Op(e, None, False)
            b.deps = list(deps)
            self.streams[e].append(b)
            self.all_ops.append(b)
        self.last_writer = {}
        self.readers = {}

    def emit(self, stack):
        nc = self.nc

        def needs_wait(o, d):
            if d.is_dma:
                if o.is_dma and o.group == d.group:
                    return False
                return True
            if d.eng == o.eng and not o.is_dma and d.eng == "pe":
                return False
            return True

        for o in self.all_ops:
            for d in o.deps:
                if not d.is_dma and needs_wait(o, d):
                    d.marked = True
        esem = {}
        for e in ENGS:
            if e != "sp":
                esem[e] = stack.enter_context(nc.semaphore("s_" + e))
        for e in ENGS:
            c = 0
            for o in self.streams[e]:
                if o.is_dma or o.fn is None:
                    continue
                if o.marked:
                    c += 1
                    o.tick = c
                o.sem = esem.get(e)
        dsem = {}
        dcount = {}
        gmax = {}
        for o in self.all_ops:
            if o.is_dma:
                k = o.sem
                if k not in dsem:
                    dsem[k] = stack.enter_context(nc.semaphore("d_%d" % len(dsem)))
                    dcount[k] = 0
                dcount[k] += 16
                o.sem = dsem[k]
                gmax[o.group] = dcount[k]
        for o in self.all_ops:
            if o.is_dma:
                o.tick = gmax[o.group]
        final_waits = [(s, dcount[k]) for k, s in dsem.items()]
        block = stack.enter_context(nc.Block())

        def make(ename):
            def body(eng):
                waited = {}
                for o in self.streams[ename]:
                    for d in o.deps:
                        if not needs_wait(o, d):
                            continue
                        key = id(d.sem)
                        if waited.get(key, 0) >= d.tick:
                            continue
                        eng.wait_ge(d.sem, d.tick)
                        waited[key] = d.tick
                    if o.fn is None:
                        continue
                    inst = o.fn(eng)
                    if o.is_dma:
                        inst.then_inc(o.sem, 16)
                    elif o.marked:
                        inst.then_inc(o.sem, 1)
                if ename == "sp":
                    for s, v in final_waits:
                        eng.wait_ge(s, v)
            return body

        block.tensor(make("pe"))
        block.scalar(make("act"))
        block.vector(make("dve"))
        block.gpsimd(make("pool"))
        block.sync(make("sp"))


class Arena:
    def __init__(self, nc, nbytes):
        self.t = nc.alloc_sbuf_tensor("arena", [128, nbytes], U8)
        self.n = nbytes
        self.top = 0
        self.cnt = 0

    def alloc(self, free, dt, parts=128):
        esz = 4 if dt == F32 else 2
        n = esz
        for f in free:
            n *= f
        off = (self.top + 63) // 64 * 64
        self.top = off + n
        assert self.top <= self.n, ("SBUF arena overflow", self.top, self.n)
        ap = self.t[0:parts, off:off + n].bitcast(dt)
        if len(free) == 2:
            ap = ap.rearrange("p (a b) -> p a b", a=free[0])
        elif len(free) == 3:
            ap = ap.rearrange("p (a b c) -> p a b c", a=free[0], b=free[1])
        self.cnt += 1
        return ap, ("sb", self.cnt)


class Rot:
    def __init__(self, items):
        self.items = items
        self.i = 0

    def next(self):
        it = self.items[self.i % len(self.items)]
        self.i += 1
        return it


def build_program(layers):
    nc = bass.Bass("TRN2", target_bir_lowering=False)
    x_in = nc.dram_tensor("x", [SEQ, D], F32, kind="ExternalInput").ap()
    w_in = nc.dram_tensor("w_in", [2, D, DIN], F32, kind="ExternalInput").ap()
    w_out = nc.dram_tensor("w_out", [2, D, D], F32, kind="ExternalInput").ap()
    w_up = nc.dram_tensor("w_up", [2, D, 2 * DFF], F32, kind="ExternalInput").ap()
    w_down = nc.dram_tensor("w_down", [2, DFF, D], F32, kind="ExternalInput").ap()
    pw_w = nc.dram_tensor("pw", [2, 256, 256], F32, kind="ExternalInput").ap()
    pool_w = nc.dram_tensor("pool_w", [2, 4, 64, 64], F32, kind="ExternalInput").ap()
    pp_d = nc.dram_tensor("pp", [2, 128, NPP], F32, kind="ExternalInput").ap()
    cst_d = nc.dram_tensor("cst", [128, 3 * 128 + 1024], F32, kind="ExternalInput").ap()
    y_out = nc.dram_tensor("y", [SEQ, D], F32, kind="ExternalOutput").ap()
    qs_d = nc.dram_tensor("qs", [8, 70, SEQ], BF16, kind="Internal").ap()
    ks_d = nc.dram_tensor("ks", [8, 70, SEQ], BF16, kind="Internal").ap()
    vs_d = nc.dram_tensor("vs", [128, 32, 1024], BF16, kind="Internal").ap()
    cps_d = nc.dram_tensor("cps", [128, 4, SEQ], BF16, kind="Internal").ap()
    xmid_d = nc.dram_tensor("xmid", [SEQ, D], F32, kind="Internal").ap()
    xl_d = nc.dram_tensor("xl", [SEQ, D], F32, kind="Internal").ap()

    with ExitStack() as st:
        S = Sched(nc)
        AR = Arena(nc, 212000)
        ps = []
        for i in range(8):
            ps.append((nc.alloc_psum_tensor("ps%d" % i, [128, 512], F32), ("ps", i)))

        def psf(i):
            return ps[i][0][:, :], ps[i][1]

        def psb(i):
            return ps[i][0][:, :].bitcast(BF16), ps[i][1]

        def act(out, in_, func, reads, writes, **kw):
            S.op("act", lambda e: e.activation(out=out, in_=in_, func=func, **kw), reads, writes)

        def mm(out, lhsT, rhs, start, stop, reads, writes):
            S.op("pe", lambda e: e.matmul(out, lhsT, rhs, start=start, stop=stop), reads, writes)

        def tr(out, in_, ident, reads, writes):
            S.op("pe", lambda e: e.transpose(out=out, in_=in_, identity=ident), reads, writes)

        def tt(eng, out, in0, in1, op, reads, writes):
            S.op(eng, lambda e: e.tensor_tensor(out=out, in0=in0, in1=in1, op=op), reads, writes)

        def ts(eng, out, in0, s1, s2, op0, op1, reads, writes):
            if s2 is None:
                S.op(eng, lambda e: e.tensor_scalar(out=out, in0=in0, scalar1=s1, scalar2=None, op0=op0),
                     reads, writes)
            else:
                S.op(eng, lambda e: e.tensor_scalar(out=out, in0=in0, scalar1=s1, scalar2=s2, op0=op0, op1=op1),
                     reads, writes)

        def stt(out, in0, scalar, in1, op0, op1, reads, writes):
            S.op("dve", lambda e: e.scalar_tensor_tensor(out=out, in0=in0, scalar=scalar, in1=in1,
                                                         op0=op0, op1=op1), reads, writes)

        def cp(eng, out, in_, reads, writes):
            S.op(eng, lambda e: e.tensor_copy(out=out, in_=in_), reads, writes)

        def recip(out, in_, reads, writes):
            S.op("dve", lambda e: e.reciprocal(out=out, in_=in_), reads, writes)

        def mset(eng, ap, val, writes):
            S.op(eng, lambda e: e.memset(ap, val), (), writes)

        def dma(q, out, in_, semkey, reads, writes, group=None):
            S.dma(q, lambda e: e.dma_start(out=out, in_=in_), semkey, reads, writes, group)

        def xload(xt, k_xt, src, T0, semname):
            for s in range(4):
                dma("sp", xt[:, s, :], src[T0 + s * 128:T0 + (s + 1) * 128, :], (semname, s), (), [(k_xt, s)])

        def xstore(xt, k_xt, dst, T0, semname, s):
            dma("sp", dst[T0 + s * 128:T0 + (s + 1) * 128, :], xt[:, s, :], (semname, s), [(k_xt, s)],
                [("xdst", semname, T0, s)])

        identb, k_identb = AR.alloc([128], BF16)
        maskb, k_maskb = AR.alloc([128], BF16)
        bonesb, k_bonesb = AR.alloc([128], BF16)
        o256b, k_o256b = AR.alloc([128], BF16)
        invdiv, k_invdiv = AR.alloc([2, 512], F32)
        ones8, k_ones8 = AR.alloc([512], F32, parts=8)
        pp, k_pp = AR.alloc([2, NPP], F32)
        dma("pool", identb, cst_d[:, 0:128], "c0", (), [k_identb])
        dma("pool", maskb, cst_d[:, 128:256], "c1", (), [k_maskb])
        dma("pool", bonesb, cst_d[:, 256:384], "c2", (), [k_bonesb])
        dma("sp", invdiv, cst_d[:, 384:1408].rearrange("p (a b) -> p a b", a=2), "c3", (), [k_invdiv])
        dma("sp", pp, pp_d.rearrange("l p n -> p l n"), "c4", (), [k_pp])
        mset("pool", o256b, 1.0 / 256.0, [k_o256b])
        mset("pool", ones8, 1.0, [k_ones8])
        base_top = AR.top

        def ppc(l, c, n=1, parts=128):
            return pp[0:parts, l, c:c + n]

        def rmsnorm_to_hT(l, xt, k_xt, gcol, hT, k_hT, sm, psT_rot):
            junk, k_junk, ss, k_ss, sd, k_sd, rstd, k_rstd, xn, k_xn = sm
            for s in range(4):
                act(junk, xt[:, s, :], AF.Square, [(k_xt, s)], [k_junk, (k_ss, s)], accum_out=ss[:, s:s + 1])
            act(sd, ss, AF.Sqrt, [(k_ss, s_) for s_ in range(4)], [k_sd], scale=1.0 / D, bias=EPS)
            recip(rstd, sd, [k_sd], [k_rstd])
            for s in range(4):
                if s % 2 == 0:
                    ts("dve", xn[:, s, :], xt[:, s, :], rstd[:, s:s + 1], None, ALU.mult, None,
                       [(k_xt, s), k_rstd], [(k_xn, s)])
                else:
                    act(xn[:, s, :], xt[:, s, :], AF.Copy, [(k_xt, s), k_rstd], [(k_xn, s)],
                        scale=rstd[:, s:s + 1])
            for c in range(8):
                pt, k_pt = psT_rot.next()
                for s in range(4):
                    tr(pt[:, s * 128:(s + 1) * 128], xn[:, s, c * 128:(c + 1) * 128], identb,
                       [(k_xn, s), k_identb], [k_pt])
                if c % 2 == 0:
                    act(hT[:, c, :], pt[:, 0:512], AF.Copy, [k_pt, k_pp], [(k_hT, c)], scale=ppc(l, gcol + c))
                else:
                    ts("dve", hT[:, c, :], pt[:, 0:512], ppc(l, gcol + c), None, ALU.mult, None,
                       [k_pt, k_pp], [(k_hT, c)])

        def phase_A(l, x_src):
            AR.top = base_top
            win, k_win = AR.alloc([8, DIN], BF16)
            pwb, k_pwb = AR.alloc([2, 256], BF16)
            pbd, k_pbd = AR.alloc([2, 128], BF16)
            dg, k_dg = AR.alloc([2, 31, 128], BF16)
            xt, k_xt = AR.alloc([4, 1024], F32)
            junk, k_junk = AR.alloc([1024], BF16)
            ss, k_ss = AR.alloc([4], F32)
            sd, k_sd = AR.alloc([4], F32)
            rstd, k_rstd = AR.alloc([4], F32)
            xn, k_xn = AR.alloc([4, 1024], BF16)
            hTs = [AR.alloc([8, 512], BF16) for _ in range(2)]
            qraws = Rot([AR.alloc([512], F32) for _ in range(2)])
            qsqs = Rot([AR.alloc([512], BF16) for _ in range(2)])
            sdqs = Rot([AR.alloc([512], F32) for _ in range(2)])
            rqs = Rot([AR.alloc([512], F32) for _ in range(2)])
            qsts = Rot([AR.alloc([512], BF16) for _ in range(2)])
            zf, k_zf = AR.alloc([512], F32, parts=8)
            ef, k_ef = zf, k_zf
            nl, k_nl = AR.alloc([512], F32, parts=8)
            Gb = [AR.alloc([512], F32, parts=8) for _ in range(2)]
            r1, k_r1 = AR.alloc([512], F32, parts=8)
            r2, k_r2 = AR.alloc([512], F32, parts=8)
            gk, k_gk = AR.alloc([6, 512], BF16, parts=8)
            gq, k_gq = AR.alloc([6, 512], BF16, parts=8)
            vsts = Rot([AR.alloc([4, 256], BF16) for _ in range(2)])
            sg, k_sg = AR.alloc([2, 512], F32)
            g2, k_g2 = AR.alloc([2, 542], BF16)
            cv, k_cv = AR.alloc([2, 512], F32)
            cvb, k_cvb = AR.alloc([2, 512], BF16)
            csq, k_csq = AR.alloc([2, 512], BF16)
            mu, k_mu = AR.alloc([512], F32)
            musq, k_musq = AR.alloc([512], F32)
            var, k_var = musq, k_musq
            sdc, k_sdc = AR.alloc([512], F32)
            rc, k_rc = sdc, k_sdc
            dd, k_dd = AR.alloc([2, 512], F32)
            sT, k_sT = AR.alloc([2, 512], BF16)
            PU, k_PU = AR.alloc([2, 528], F32)
            S2, k_S2 = AR.alloc([2, 528], F32)
            S4, k_S4 = AR.alloc([2, 528], F32)
            S8, k_S8 = AR.alloc([528], F32)
            S16, k_S16 = AR.alloc([528], F32)
            dT, k_dT = AR.alloc([2, 512], BF16)
            cpsts = Rot([AR.alloc([4, 512], BF16) for _ in range(2)])

            wv = w_in[l].rearrange("(c p) n -> p c n", p=128)
            for c in range(8):
                for hf in range(2):
                    dma("pool", win[:, c, hf * 1156:(hf + 1) * 1156], wv[:, c, hf * 1156:(hf + 1) * 1156],
                        "win", (), [(k_win, c, hf)], group=("win", l))
            dma("pool", pwb, pw_w[l].rearrange("(c p) n -> p c n", p=128), "pw", (), [k_pwb])
            mset("dve", pbd, 0.0, [(k_pbd, g_) for g_ in range(4)])
            for g in range(4):
                c, o = g // 2, g % 2
                dma("pool", pbd[64 * o:64 * o + 64, c, 64 * o:64 * o + 64], pool_w[l, g], "pbd", [], [(k_pbd, g)],
                    group=("pbd", l))
            for c in range(2):
                for k in range(31):
                    if k % 2 == 0:
                        ts("dve", dg[:, c, k, :], identb, ppc(l, C_DWW + c * 31 + k), None, ALU.mult, None,
                           [k_identb, k_pp], [(k_dg, c, k)])
                    else:
                        act(dg[:, c, k, :], identb, AF.Copy, [k_identb, k_pp], [(k_dg, c, k)],
                            scale=ppc(l, C_DWW + c * 31 + k))
            mset("pool", gk, 1.0, [k_gk])
            mset("pool", gq, 1.0, [k_gq])
            for (v_, kv_) in vsts.items:
                mset("pool", v_, 1.0, [kv_])
            mset("dve", g2[:, :, 0:30], 0.0, [k_g2])
            mset("dve", PU[:, :, 0:16], 0.0, [k_PU])
            mset("pool", S2, 0.0, [k_S2])
            mset("pool", S4, 0.0, [k_S4])
            mset("pool", S8, 0.0, [k_S8])
            mset("pool", S16, 0.0, [k_S16])

            psT_rot = Rot([psb(6), psb(7)])
            prot = Rot([psf(i) for i in range(6)])
            sm = (junk, k_junk, ss, k_ss, sd, k_sd, rstd, k_rstd, xn, k_xn)

            pending = []

            def defer(n, fn):
                pending.append([n, fn])

            def tick():
                for p in list(pending):
                    p[0] -= 1
                    if p[0] <= 0:
                        pending.remove(p)
                        p[1]()

            def flush():
                while pending:
                    tick()

            xload(xt, k_xt, x_src, 0, "xt")
            for i in range(NT):
                T0 = i * 512
                hT, k_hT = hTs[i % 2]
                rmsnorm_to_hT(l, xt, k_xt, C_N1G, hT, k_hT, sm, psT_rot)
                if i + 1 < NT:
                    xload(xt, k_xt, x_src, T0 + 512, "xt")
                hreads = [(k_hT, c) for c in range(8)]

                def proj(col, M):
                    pa, k_pa = prot.next()
                    for c in range(8):
                        mm(pa[0:M, :], win[:, c, col:col + M], hT[:, c, :], c == 0, c == 7,
                           [(k_win, c, 0), (k_win, c, 1), (k_hT, c)], [k_pa])
                    return pa, k_pa

                def qk_group(isq, hp):
                    col = (0 if isq else 512) + hp * 128
                    pa, k_pa = proj(col, 128)
                    qraw, k_qraw = qraws.next()
                    qsq, k_qsq = qsqs.next()
                    sdq, k_sdq = sdqs.next()
                    rq, k_rq = rqs.next()
                    qst, k_qst = qsts.next()
                    act(qraw, pa, AF.Copy, [k_pa], [k_qraw])
                    tt("pool", qsq, qraw, qraw, ALU.mult, [k_qraw], [k_qsq])

                    def stage2():
                        pb, k_pb = prot.next()
                        mm(pb, bonesb, qsq, True, True, [k_bonesb, k_qsq], [k_pb])
                        if isq:
                            act(sdq, pb, AF.Ln, [k_pb], [k_sdq], scale=1.0, bias=64.0 * EPS)
                        else:
                            act(sdq, pb, AF.Ln, [k_pb], [k_sdq], scale=1.0 / 64.0, bias=EPS)
                        act(rq, sdq, AF.Exp, [k_sdq], [k_rq], scale=-0.5)
                        stt(qst, qraw, ppc(l, C_QG if isq else C_KG), rq, ALU.mult, ALU.mult,
                            [k_qraw, k_rq, k_pp], [k_qst])
                        dst = qs_d if isq else ks_d
                        for o in range(2):
                            h = 2 * hp + o
                            dma("sp", dst[h, 0:64, T0:T0 + 512], qst[64 * o:64 * o + 64, :],
                                ("qst", qsts.i % 2), [k_qst], [("qk", isq, h, i)])
                    defer(2, stage2)

                for hp in range(4):
                    qk_group(True, hp)
                    tick()
                    qk_group(False, hp)
                    tick()

                pa, k_pa = proj(1536, 8)
                act(zf, pa[0:8, :], AF.Identity, [k_pa, k_pp], [k_zf], bias=ppc(l, C_BF, 1, 8))
                act(ef, zf, AF.Exp, [k_zf], [k_ef], scale=-1.0)
                act(nl, ef, AF.Ln, [k_ef], [k_nl], bias=1.0)
                G, k_G = Gb[i % 2]
                Gp, k_Gp = Gb[(i + 1) % 2]
                if i == 0:
                    S.op("dve", lambda e, G=G: e.tensor_tensor_scan(out=G, data0=ones8, data1=nl, initial=0.0,
                                                                    op0=ALU.mult, op1=ALU.add),
                         [k_ones8, k_nl], [k_G])
                else:
                    S.op("dve", lambda e, G=G, Gp=Gp: e.tensor_tensor_scan(
                        out=G, data0=ones8, data1=nl, initial=Gp[:, 511:512], op0=ALU.mult, op1=ALU.add),
                        [k_ones8, k_nl, k_Gp], [k_G])
                cp("dve", gk[:, 3, :], G, [k_G], [k_gk])
                tt("dve", r1, G, gk[:, 3, :], ALU.subtract, [k_G, k_gk], [k_r1])
                cp("dve", gk[:, 4, :], r1, [k_r1], [k_gk])
                tt("dve", r2, r1, gk[:, 4, :], ALU.subtract, [k_r1, k_gk], [k_r2])
                cp("dve", gk[:, 5, :], r2, [k_r2], [k_gk])
                ts("dve", gq[:, 0:3, :], gk[:, 3:6, :], -1.0, None, ALU.mult, None, [k_gk], [k_gq])
                dma("sp", ks_d[:, 64:70, T0:T0 + 512], gk, "gk", [k_gk], [("gkd", i)])
                dma("sp", qs_d[:, 64:70, T0:T0 + 512], gq, "gq", [k_gq], [("gqd", i)])
                tick()

                for s in range(4):
                    pa, k_pa = prot.next()
                    for c in range(8):
                        mm(pa, hT[:, c, s * 128:(s + 1) * 128], win[:, c, 1024:1536], c == 0, c == 7,
                           [(k_win, c, 0), (k_win, c, 1), (k_hT, c)], [k_pa])
                    vst, k_vst = vsts.next()
                    pv4 = pa.rearrange("p (a b d) -> p a b d", a=4, b=2)
                    act(vst[:, :, 0:64], pv4[:, :, 0, :], AF.Copy, [k_pa], [k_vst])
                    cp("dve", vst[:, :, 192:256], pv4[:, :, 1, :], [k_pa], [k_vst])
                    dma("sp", vs_d[:, 4 * i + s, :], vst.rearrange("p a b -> p (a b)"),
                        ("vst", vsts.i % 2), [k_vst], [("vsd", i, s)])
                    tick()

                cpst, k_cpst = cpsts.next()
                if i > 0:
                    cp("pool", g2[:, :, 0:30], g2[:, :, 512:542], [k_g2], [k_g2])
                pas = [proj(1544 + 128 * c, 128) for c in range(2)]
                pbs = [proj(1544 + 256 + 128 * c, 128) for c in range(2)]
                for c in range(2):
                    act(sg[:, c, :], pbs[c][0], AF.Sigmoid, [pbs[c][1]], [(k_sg, c)])
                    tt("dve", g2[:, c, 30:542], pas[c][0], sg[:, c, :], ALU.mult,
                       [pas[c][1], (k_sg, c)], [k_g2])
                tick()
                pcs = []
                for c in range(2):
                    pa, k_pa = prot.next()
                    for k in range(31):
                        mm(pa, dg[:, c, k, :], g2[:, c, k:k + 512], k == 0, k == 30, [(k_dg, c, k), k_g2], [k_pa])
                    act(cv[:, c, :], pa, AF.Identity, [k_pa, k_pp], [(k_cv, c)], bias=ppc(l, C_DWB + c))
                    cp("pool", cvb[:, c, :], cv[:, c, :], [(k_cv, c)], [(k_cvb, c)])
                    tt("pool", csq[:, c, :], cv[:, c, :], cv[:, c, :], ALU.mult, [(k_cv, c)], [(k_csq, c)])

                pps = [proj(2056 + 128 * c, 128) for c in range(2)]
                if i > 0:
                    cp("pool", PU[:, :, 0:16], PU[:, :, 512:528], [k_PU], [k_PU])
                for c in range(2):
                    act(PU[:, c, 16:528], pps[c][0], AF.Copy, [pps[c][1]], [k_PU])
                tt("pool", S2[:, :, 1:528], PU[:, :, 1:528], PU[:, :, 0:527], ALU.add, [k_PU], [k_S2])
                tt("pool", S4[:, :, 3:528], S2[:, :, 3:528], S2[:, :, 1:526], ALU.add, [k_S2], [k_S4])
                tt("pool", S8[:, 7:528], S4[:, 1, 7:528], S4[:, 1, 3:524], ALU.add, [k_S4], [k_S8])
                tt("pool", S16[64:128, 15:528], S8[64:128, 15:528], S8[64:128, 7:520], ALU.add, [k_S8], [k_S16])
                srcs = [(S2[0:64, 0, 16:528], k_S2, 0.5, 0, 0), (S4[64:128, 0, 16:528], k_S4, 0.25, 0, 64),
                        (S8[0:64, 16:528], k_S8, 0.125, 1, 0), (S16[64:128, 16:528], k_S16, 0.0625, 1, 64)]
                for (sap, ksap, inv, c, p0) in srcs:
                    if i == 0:
                        tt("dve", dd[p0:p0 + 64, c, :], sap, invdiv[p0:p0 + 64, c, :], ALU.mult,
                           [ksap, k_invdiv], [(k_dd, c)])
                        tt("dve", dT[p0:p0 + 64, c, :], dd[p0:p0 + 64, c, :], PU[p0:p0 + 64, c, 16:528],
                           ALU.subtract, [(k_dd, c), k_PU], [(k_dT, c)])
                    else:
                        stt(dT[p0:p0 + 64, c, :], sap, inv, PU[p0:p0 + 64, c, 16:528], ALU.mult, ALU.subtract,
                            [ksap, k_PU], [(k_dT, c)])

                pmu, k_pmu = prot.next()
                for c in range(2):
                    mm(pmu, o256b, cvb[:, c, :], c == 0, c == 1, [k_o256b, (k_cvb, c)], [k_pmu])
                pex, k_pex = prot.next()
                for c in range(2):
                    mm(pex, o256b, csq[:, c, :], c == 0, c == 1, [k_o256b, (k_csq, c)], [k_pex])
                cp("dve", mu, pmu, [k_pmu], [k_mu])
                tt("dve", musq, mu, mu, ALU.mult, [k_mu], [k_musq])
                tt("dve", var, pex, musq, ALU.subtract, [k_pex, k_musq], [k_var])
                ts("dve", var, var, 0.0, None, ALU.max, None, [k_var], [k_var])
                act(sdc, var, AF.Ln, [k_var], [k_sdc], scale=1.0, bias=EPS)
                act(rc, sdc, AF.Exp, [k_sdc], [k_rc], scale=-0.5)
                for c in range(2):
                    tt("dve", dd[:, c, :], cv[:, c, :], mu, ALU.subtract, [(k_cv, c), k_mu], [(k_dd, c)])
                    tt("dve", dd[:, c, :], dd[:, c, :], rc, ALU.mult, [(k_dd, c), k_rc], [(k_dd, c)])
                    act(sT[:, c, :], dd[:, c, :], AF.Silu, [(k_dd, c), k_pp], [(k_sT, c)],
                        scale=ppc(l, C_LNG + c), bias=ppc(l, C_LNB + c))
                for c in range(2):
                    pa, k_pa = prot.next()
                    mm(pa, pbd[:, c, :], dT[:, c, :], True, True, [(k_pbd, 2 * c), (k_pbd, 2 * c + 1), (k_dT, c)], [k_pa])
                    act(cpst[:, 2 + c, :], pa, AF.Copy, [k_pa, k_pp], [k_cpst], scale=ppc(l, C_PSC + c))
                for co in range(2):
                    pa, k_pa = prot.next()
                    for ci in range(2):
                        mm(pa, pwb[:, ci, co * 128:(co + 1) * 128], sT[:, ci, :], ci == 0, ci == 1,
                           [k_pwb, (k_sT, ci)], [k_pa])
                    cp("dve", cpst[:, co, :], pa, [k_pa], [k_cpst])
                dma("sp", cps_d[:, :, T0:T0 + 512], cpst, ("cpst", cpsts.i % 2), [k_cpst], [("cpsd", i)])
                flush()
            S.barrier()
            print("arena A", AR.top)

        def phase_B(l, x_src, x_dst):
            AR.top = base_top
            kaug = [AR.alloc([SEQ], BF16) for _ in range(8)]
            vaug, k_vaug = AR.alloc([32, 1024], BF16)
            wout, k_wout = AR.alloc([8, 1024], BF16)
            qcs = [AR.alloc([8, 512], BF16) for _ in range(2)]
            mixT, k_mixT = AR.alloc([8, 512], BF16)
            pts = Rot([AR.alloc([512], BF16) for _ in range(4)])
            recs = Rot([AR.alloc([512], F32) for _ in range(2)])
            xt, k_xt = AR.alloc([4, 1024], F32)

            for h in range(8):
                dma("sp", kaug[h][0][0:70, :], ks_d[h], ("kaug", h), (), [kaug[h][1]])
            for q4 in range(4):
                dma("sp", vaug[:, 8 * q4:8 * q4 + 8, :], vs_d[:, 8 * q4:8 * q4 + 8, :], "vaug", (),
                    [(k_vaug, q4)], group=("vaug", l))
            wv = w_out[l].rearrange("(c p) n -> p c n", p=128)
            for c in range(8):
                dma("pool", wout[:, c, :], wv[:, c, :], "wout", (), [(k_wout, c)], group=("wout", l))

            srot = Rot([psf(i) for i in range(4)])
            orot = Rot([psf(4), psf(5)])
            xrot = Rot([psf(6), psf(7)])

            def load_chunk(c):
                qc, k_qc = qcs[c % 2]
                T0 = c * 512
                dma("sp", qc[0:70, :, :], qs_d[:, :, T0:T0 + 512].rearrange("h r t -> r h t"),
                    ("qc", c % 2), (), [k_qc])

            load_chunk(0)
            for c in range(NT):
                T0 = c * 512
                qc, k_qc = qcs[c % 2]
                if c + 1 < NT:
                    load_chunk(c + 1)
                dma("sp", mixT[:, 4:8, :], cps_d[:, :, T0:T0 + 512], "mixcp", (), [(k_mixT, "cp")])
                xload(xt, k_xt, x_src, T0, "xtb")
                nkb = 4 * c + 4
                items = [(h, kb) for h in range(8) for kb in range(nkb)]
                state = {}

                def s_stage(h, kb):
                    q0 = max(0, 128 * kb - T0)
                    pS, k_pS = srot.next()
                    diag = kb >= 4 * c
                    ka, k_ka = kaug[h]
                    mm(pS[:, q0:512], ka[0:70, kb * 128:(kb + 1) * 128], qc[0:70, h, q0:512], True, not diag,
                       [k_ka, k_qc], [k_pS])
                    if diag:
                        mm(pS[:, q0:q0 + 128], identb, maskb, False, True, [k_identb, k_maskb], [k_pS])
                    pt, k_pt = pts.next()
                    act(pt[:, q0:512], pS[:, q0:512], AF.Exp, [k_pS], [k_pt])
                    state[(h, kb)] = (pt, k_pt, q0)

                def pv_stage(h, kb):
                    pt, k_pt, q0 = state.pop((h, kb))
                    if kb == 0:
                        state[("o", h)] = orot.next()
                    pO, k_pO = state[("o", h)]
                    mm(pO[:, q0:512], vaug[:, kb, h * 128:(h + 1) * 128], pt[:, q0:512], kb == 0, kb == nkb - 1,
                       [(k_vaug, kb // 8), k_pt], [k_pO])
                    if kb == nkb - 1:
                        hp, o = h // 2, h % 2
                        rec, k_rec = recs.next()
                        a0, b0 = (0, 64) if o == 0 else (64, 0)
                        recip(rec[b0:b0 + 64, :], pO[b0:b0 + 64, :], [k_pO], [k_rec])
                        tt("dve", mixT[a0:a0 + 64, hp, :], pO[a0:a0 + 64, :], rec[b0:b0 + 64, :], ALU.mult,
                           [k_pO, k_rec], [(k_mixT, hp)])
                        state.pop(("o", h))

                DEPTH = 2
                for n_, (h, kb) in enumerate(items):
                    s_stage(h, kb)
                    if n_ >= DEPTH:
                        pv_stage(*items[n_ - DEPTH])
                for n_ in range(max(0, len(items) - DEPTH), len(items)):
                    pv_stage(*items[n_])

                mreads = [(k_mixT, hp) for hp in range(4)] + [(k_mixT, "cp")]
                for s in range(4):
                    for n in range(2):
                        pX, k_pX = xrot.next()
                        for kc in range(8):
                            mm(pX, mixT[:, kc, s * 128:(s + 1) * 128], wout[:, kc, n * 512:(n + 1) * 512],
                               kc == 0, kc == 7, mreads + [(k_wout, kc)], [k_pX])
                        tt("dve", xt[:, s, n * 512:(n + 1) * 512], pX, xt[:, s, n * 512:(n + 1) * 512], ALU.add,
                           [k_pX, (k_xt, s)], [(k_xt, s)])
                    xstore(xt, k_xt, x_dst, T0, "xtb_st", s)
            S.barrier()

        def phase_C(l, x_src, x_dst):
            AR.top = base_top
            wup, k_wup = AR.alloc([8, 2 * DFF], BF16)
            wdn, k_wdn = AR.alloc([22, 1024], BF16)
            xt, k_xt = AR.alloc([4, 1024], F32)
            junk, k_junk = AR.alloc([1024], BF16)
            ss, k_ss = AR.alloc([4], F32)
            sd, k_sd = AR.alloc([4], F32)
            rstd, k_rstd = AR.alloc([4], F32)
            xn, k_xn = AR.alloc([4, 1024], BF16)
            hT, k_hT = AR.alloc([8, 512], BF16)
            actT, k_actT = AR.alloc([13, 512], BF16)
            abufs = Rot([AR.alloc([512], F32) for _ in range(6)])
            sgs = Rot([AR.alloc([512], F32) for _ in range(3)])
            hl, k_hl = AR.alloc([44, 2], F32)
            HC, k_HC = AR.alloc([44, 2], F32)
            htmp, k_htmp = AR.alloc([44], F32)
            fdw = pp[:, l, C_FDW:C_FDW + 132].rearrange("p (c k) -> p c k", k=3)

            wv = w_up[l].rearrange("(c p) n -> p c n", p=128)
            for c in range(8):
                for q4 in range(4):
                    dma("pool", wup[:, c, q4 * 1408:(q4 + 1) * 1408], wv[:, c, q4 * 1408:(q4 + 1) * 1408],
                        "wup", (), [(k_wup, c, q4)], group=("wup", l))
            wd = w_down[l].rearrange("(j p) n -> p j n", p=128)
            for j in range(22):
                dma("pool", wdn[:, j, :], wd[:, j, :], "wdn", (), [(k_wdn, j)], group=("wdn", l))

            psT_rot = Rot([psb(7)])
            urot = Rot([psf(i) for i in range(5)])
            drot = Rot([psf(5), psf(6)])
            sm = (junk, k_junk, ss, k_ss, sd, k_sd, rstd, k_rstd, xn, k_xn)

            xload(xt, k_xt, x_src, 0, "xtc")
            for i in range(NT):
                T0 = i * 512
                rmsnorm_to_hT(l, xt, k_xt, C_N2G, hT, k_hT, sm, psT_rot)
                if i > 0:
                    hlk = [(k_hl, ch_) for ch_ in range(44)]
                    tt("pool", HC[:, :, 0], hl[:, :, 1], fdw[:, :, 1], ALU.mult, hlk + [k_pp], [k_HC])
                    tt("pool", htmp, hl[:, :, 0], fdw[:, :, 0], ALU.mult, hlk + [k_pp], [k_htmp])
                    tt("pool", HC[:, :, 0], HC[:, :, 0], htmp, ALU.add, [k_HC, k_htmp], [k_HC])
                    tt("pool", HC[:, :, 1], hl[:, :, 1], fdw[:, :, 0], ALU.mult, hlk + [k_pp], [k_HC])
                def wdown(half, i=i, T0=T0):
                    slots = [(jj if (half == 0 or jj >= 2) else 11 + jj) for jj in range(11)]
                    areads = [(k_actT, sl) for sl in slots]
                    for s in range(4):
                        for n in range(2):
                            pD, k_pD = drot.next()
                            for jj in range(11):
                                j = half * 11 + jj
                                mm(pD, actT[:, slots[jj], s * 128:(s + 1) * 128],
                                   wdn[:, j, n * 512:(n + 1) * 512],
                                   jj == 0, jj == 10, areads + [(k_wdn, j)], [k_pD])
                            tt("dve", xt[:, s, n * 512:(n + 1) * 512], pD, xt[:, s, n * 512:(n + 1) * 512],
                               ALU.add, [k_pD, (k_xt, s)], [(k_xt, s)])
                        if half == 1:
                            xstore(xt, k_xt, x_dst, T0, "xtc_st", s)
                            if i + 1 < NT:
                                dma("sp", xt[:, s, :], x_src[T0 + 512 + s * 128:T0 + 512 + (s + 1) * 128, :],
                                    ("xtc", s), (), [(k_xt, s)])

                for half in range(2):
                    for jj in range(11):
                        j = half * 11 + jj
                        res = []
                        for which in range(2):
                            ch = j + 22 * which
                            col = ch * 128
                            pU, k_pU = urot.next()
                            for c in range(8):
                                mm(pU, wup[:, c, col:col + 128], hT[:, c, :], c == 0, c == 7,
                                   [(k_wup, c, col // 1408), (k_hT, c)], [k_pU])
                            ab, k_ab = abufs.next()
                            act(ab, pU, AF.Copy, [k_pU, k_pp], [k_ab], scale=ppc(l, C_FDW + ch * 3 + 2))
                            stt(ab[:, 1:512], pU[:, 0:511], ppc(l, C_FDW + ch * 3 + 1), ab[:, 1:512],
                                ALU.mult, ALU.add, [k_pU, k_ab, k_pp], [k_ab])
                            stt(ab[:, 2:512], pU[:, 0:510], ppc(l, C_FDW + ch * 3 + 0), ab[:, 2:512],
                                ALU.mult, ALU.add, [k_pU, k_ab, k_pp], [k_ab])
                            if i > 0:
                                tt("pool", ab[:, 0:2], ab[:, 0:2], HC[:, ch, :], ALU.add, [k_ab, k_HC], [k_ab])
                            if i + 1 < NT:
                                act(hl[:, ch, :], pU[:, 510:512], AF.Copy, [k_pU], [(k_hl, ch)])
                            res.append((ab, k_ab))
                        sgb, k_sgb = sgs.next()
                        act(sgb, res[0][0], AF.Silu, [res[0][1]], [k_sgb])
                        slot = jj if (half == 0 or jj >= 2) else 11 + jj
                        tt("pool", actT[:, slot, :], sgb, res[1][0], ALU.mult, [k_sgb, res[1][1]],
                           [(k_actT, slot)])
                        if half == 1 and jj == 1:
                            wdown(0)
                    if half == 1:
                        wdown(1)
            S.barrier()

        nl_ = len(layers)
        for li, l in enumerate(layers):
            src = x_in if li == 0 else xl_d
            dst = y_out if li == nl_ - 1 else xl_d
            phase_A(l, src)
            phase_B(l, src, xmid_d)
            phase_C(l, xmid_d, dst)
        S.emit(st)
    return nc


def _pack_params(inp):
    pp = np.zeros((2, 128, NPP), np.float32)
    for l in range(2):
        pp[l, :, C_N1G:C_N1G + 8] = inp["norm1_g"][l].reshape(8, 128).T
        pp[l, :, C_N2G:C_N2G + 8] = inp["norm2_g"][l].reshape(8, 128).T
        pp[l, :, C_QG] = np.tile(inp["q_norm_g"][l], 2)
        pp[l, :, C_KG] = np.tile(inp["k_norm_g"][l], 2)
        pp[l, :, C_DWB:C_DWB + 2] = inp["conv_dw_b"][l].reshape(2, 128).T
        pp[l, :, C_LNG:C_LNG + 2] = inp["conv_ln_g"][l].reshape(2, 128).T
        pp[l, :, C_LNB:C_LNB + 2] = inp["conv_ln_b"][l].reshape(2, 128).T
        pp[l, :, C_PSC:C_PSC + 2] = inp["pool_scale"][l].reshape(2, 128).T
        pp[l, 0:8, C_BF] = inp["b_f"][l]
        pp[l, :, C_DWW:C_DWW + 62] = inp["conv_dw_w"][l].T.reshape(2, 128, 31).transpose(1, 0, 2).reshape(128, 62)
        pp[l, :, C_FDW:C_FDW + 132] = inp["ffn_dw_w"][l].T.reshape(44, 128, 3).transpose(1, 0, 2).reshape(128, 132)
    return pp


def _consts():
    cst = np.zeros((128, 3 * 128 + 1024), np.float32)
    cst[:, 0:128] = np.eye(128, dtype=np.float32)
    k = np.arange(128)[:, None]
    q = np.arange(128)[None, :]
    cst[:, 128:256] = np.where(k > q, -30000.0, 0.0)
    cst[:, 256:384] = (k // 64 == q // 64).astype(np.float32)
    t = np.arange(512, dtype=np.float32) + 1.0
    wins = [2.0, 4.0, 8.0, 16.0]
    inv = np.zeros((128, 2, 512), np.float32)
    for g in range(4):
        c, o = g // 2, g % 2
        inv[64 * o:64 * o + 64, c, :] = 1.0 / np.minimum(t, wins[g])
    cst[:, 384:] = inv.reshape(128, 1024)
    return cst


FUSED = True
_CACHE = {}


def _get_prog(layers):
    key = tuple(layers)
    if key not in _CACHE:
        _CACHE[key] = build_program(list(layers))
    return _CACHE[key]


def kernel(**inputs):
    inp = {k: np.ascontiguousarray(np.asarray(v)) for k, v in inputs.items()}
    x = inp["x"].astype(np.float32, copy=False)
    pp = _pack_params(inp)
    cst = _consts()
    common = {"w_in": inp["w_in"], "w_out": inp["w_out"], "w_up": inp["w_up"], "w_down": inp["w_down"],
              "pw": inp["conv_pw_w"], "pool_w": inp["pool_w"], "pp": pp, "cst": cst}
    n = 8
    if FUSED:
        nc = _get_prog((0, 1))
        in_maps = [dict(common, x=x[b]) for b in range(n)]
        res = run_bass_kernel_spmd(nc, in_maps, core_ids=list(range(n)))
        return np.stack([r["y"] for r in res.results], axis=0).astype(np.float32)
    cur = x
    for l in range(2):
        nc = _get_prog((l,))
        in_maps = [dict(common, x=cur[b]) for b in range(n)]
        res = run_bass_kernel_spmd(nc, in_maps, core_ids=list(range(n)))
        cur = np.stack([r["y"] for r in res.results], axis=0).astype(np.float32)
    return cur
```

```python
import numpy as np
from contextlib import ExitStack
import concourse.bass as bass
import concourse.mybir as mybir
from concourse.bass_utils import run_bass_kernel_spmd

F32 = mybir.dt.float32
BF16 = mybir.dt.bfloat16
U8 = mybir.dt.uint8
AF = mybir.ActivationFunctionType
ALU = mybir.AluOpType

D = 1024
SEQ = 4096
NT = 8
DIN = 2312
DFF = 2816
EPS = 1e-6
NPP = 221
C_N1G, C_N2G, C_QG, C_KG, C_DWB, C_LNG, C_LNB, C_PSC, C_BF, C_DWW, C_FDW = 0, 8, 16, 17, 18, 20, 22, 24, 26, 27, 89

ENGS = ("pe", "act", "dve", "pool", "sp")


class Op:
    __slots__ = ("eng", "fn", "deps", "is_dma", "sem", "tick", "marked", "group")

    def __init__(self, eng, fn, is_dma):
        self.eng = eng
        self.fn = fn
        self.deps = []
        self.is_dma = is_dma
        self.sem = None
        self.tick = None
        self.marked = False
        self.group = None


class Sched:
    def __init__(self, nc):
        self.nc = nc
        self.streams = {e: [] for e in ENGS}
        self.last_writer = {}
        self.readers = {}
        self.all_ops = []
        self.gid = 0

    def _add(self, op, reads, writes):
        deps = {}
        for r in reads:
            w = self.last_writer.get(r)
            if w is not None:
                deps[id(w)] = w
        for wkey in writes:
            w = self.last_writer.get(wkey)
            if w is not None:
                deps[id(w)] = w
            for rd in self.readers.get(wkey, ()):
                deps[id(rd)] = rd
        op.deps = list(deps.values())
        for r in reads:
            self.readers.setdefault(r, []).append(op)
        for wkey in writes:
            self.last_writer[wkey] = op
            self.readers[wkey] = []
        self.streams[op.eng].append(op)
        self.all_ops.append(op)
        return op

    def op(self, eng, fn, reads=(), writes=()):
        return self._add(Op(eng, fn, False), reads, writes)

    def dma(self, eng, fn, semkey, reads=(), writes=(), group=None):
        o = Op(eng, fn, True)
        o.sem = semkey
        if group is None:
            self.gid += 1
            group = ("_g", self.gid)
        o.group = (semkey, group)
        return self._add(o, reads, writes)

    def barrier(self):
        lasts = []
        for e in ENGS:
            for o in reversed(self.streams[e]):
                if not o.is_dma and o.fn is not None:
                    lasts.append(o)
                    break
        dma_last = {}
        for o in self.all_ops:
            if o.is_dma:
                dma_last[o.sem] = o
        deps = lasts + list(dma_last.values())
        for e in ENGS:
            b = Op(e, None, False)
            b.deps = list(deps)
            self.streams[e].append(b)
            self.all_ops.append(b)
        self.last_writer = {}
        self.readers = {}

    def emit(self, stack):
        nc = self.nc

        def needs_wait(o, d):
            if d.is_dma:
                if o.is_dma and o.group == d.group:
                    return False
                return True
            if d.eng == o.eng and not o.is_dma and d.eng == "pe":
                return False
            return True

        for o in self.all_ops:
            for d in o.deps:
                if not d.is_dma and needs_wait(o, d):
                    d.marked = True
        esem = {}
        for e in ENGS:
            if e != "sp":
                esem[e] = stack.enter_context(nc.semaphore("s_" + e))
        for e in ENGS:
            c = 0
            for o in self.streams[e]:
                if o.is_dma or o.fn is None:
                    continue
                if o.marked:
                    c += 1
                    o.tick = c
                o.sem = esem.get(e)
        dsem = {}
        dcount = {}
        gmax = {}
        for o in self.all_ops:
            if o.is_dma:
                k = o.sem
                if k not in dsem:
                    dsem[k] = stack.enter_context(nc.semaphore("d_%d" % len(dsem)))
                    dcount[k] = 0
                dcount[k] += 16
                o.sem = dsem[k]
                gmax[o.group] = dcount[k]
        for o in self.all_ops:
            if o.is_dma:
                o.tick = gmax[o.group]
        final_waits = [(s, dcount[k]) for k, s in dsem.items()]
        print("n dma sems", len(dsem))
        block = stack.enter_context(nc.Block())

        def make(ename):
            def body(eng):
                waited = {}
                for o in self.streams[ename]:
                    for d in o.deps:
                        if not needs_wait(o, d):
                            continue
                        key = id(d.sem)
                        if waited.get(key, 0) >= d.tick:
                            continue
                        eng.wait_ge(d.sem, d.tick)
                        waited[key] = d.tick
                    if o.fn is None:
                        continue
                    inst = o.fn(eng)
                    if o.is_dma:
                        inst.then_inc(o.sem, 16)
                    elif o.marked:
                        inst.then_inc(o.sem, 1)
                if ename == "sp":
                    for s, v in final_waits:
                        eng.wait_ge(s, v)
            return body

        block.tensor(make("pe"))
        block.scalar(make("act"))
        block.vector(make("dve"))
        block.gpsimd(make("pool"))
        block.sync(make("sp"))


class Arena:
    def __init__(self, nc, nbytes):
        self.t = nc.alloc_sbuf_tensor("arena", [128, nbytes], U8)
        self.n = nbytes
        self.top = 0
        self.cnt = 0

    def alloc(self, free, dt, parts=128):
        esz = 4 if dt == F32 else 2
        n = esz
        for f in free:
            n *= f
        off = (self.top + 63) // 64 * 64
        self.top = off + n
        assert self.top <= self.n, ("SBUF arena overflow", self.top, self.n)
        ap = self.t[0:parts, off:off + n].bitcast(dt)
        if len(free) == 2:
            ap = ap.rearrange("p (a b) -> p a b", a=free[0])
        elif len(free) == 3:
            ap = ap.rearrange("p (a b c) -> p a b c", a=free[0], b=free[1])
        self.cnt += 1
        return ap, ("sb", self.cnt)


class Rot:
    def __init__(self, items):
        self.items = items
        self.i = 0

    def next(self):
        it = self.items[self.i % len(self.items)]
        self.i += 1
        return it


def build_program(layers):
    nc = bass.Bass("TRN2", target_bir_lowering=False)
    x_in = nc.dram_tensor("x", [SEQ, D], F32, kind="ExternalInput").ap()
    w_in = nc.dram_tensor("w_in", [2, D, DIN], F32, kind="ExternalInput").ap()
    w_out = nc.dram_tensor("w_out", [2, D, D], F32, kind="ExternalInput").ap()
    w_up = nc.dram_tensor("w_up", [2, D, 2 * DFF], F32, kind="ExternalInput").ap()
    w_down = nc.dram_tensor("w_down", [2, DFF, D], F32, kind="ExternalInput").ap()
    pw_w = nc.dram_tensor("pw", [2, 256, 256], F32, kind="ExternalInput").ap()
    pool_w = nc.dram_tensor("pool_w", [2, 4, 64, 64], F32, kind="ExternalInput").ap()
    pp_d = nc.dram_tensor("pp", [2, 128, NPP], F32, kind="ExternalInput").ap()
    cst_d = nc.dram_tensor("cst", [128, 3 * 128 + 1024], F32, kind="ExternalInput").ap()
    y_out = nc.dram_tensor("y", [SEQ, D], F32, kind="ExternalOutput").ap()
    qs_d = nc.dram_tensor("qs", [8, 70, SEQ], BF16, kind="Internal").ap()
    ks_d = nc.dram_tensor("ks", [8, 70, SEQ], BF16, kind="Internal").ap()
    vs_d = nc.dram_tensor("vs", [128, 32, 1024], BF16, kind="Internal").ap()
    cps_d = nc.dram_tensor("cps", [128, 4, SEQ], BF16, kind="Internal").ap()
    xmid_d = nc.dram_tensor("xmid", [SEQ, D], F32, kind="Internal").ap()
    xl_d = nc.dram_tensor("xl", [SEQ, D], F32, kind="Internal").ap()

    with ExitStack() as st:
        S = Sched(nc)
        AR = Arena(nc, 212000)
        ps = []
        for i in range(8):
            ps.append((nc.alloc_psum_tensor("ps%d" % i, [128, 512], F32), ("ps", i)))

        def psf(i):
            return ps[i][0][:, :], ps[i][1]

        def psb(i):
            return ps[i][0][:, :].bitcast(BF16), ps[i][1]

        def act(out, in_, func, reads, writes, **kw):
            S.op("act", lambda e: e.activation(out=out, in_=in_, func=func, **kw), reads, writes)

        def mm(out, lhsT, rhs, start, stop, reads, writes):
            S.op("pe", lambda e: e.matmul(out, lhsT, rhs, start=start, stop=stop), reads, writes)

        def tr(out, in_, ident, reads, writes):
            S.op("pe", lambda e: e.transpose(out=out, in_=in_, identity=ident), reads, writes)

        def tt(eng, out, in0, in1, op, reads, writes):
            S.op(eng, lambda e: e.tensor_tensor(out=out, in0=in0, in1=in1, op=op), reads, writes)

        def ts(eng, out, in0, s1, s2, op0, op1, reads, writes):
            if s2 is None:
                S.op(eng, lambda e: e.tensor_scalar(out=out, in0=in0, scalar1=s1, scalar2=None, op0=op0),
                     reads, writes)
            else:
                S.op(eng, lambda e: e.tensor_scalar(out=out, in0=in0, scalar1=s1, scalar2=s2, op0=op0, op1=op1),
                     reads, writes)

        def stt(out, in0, scalar, in1, op0, op1, reads, writes):
            S.op("dve", lambda e: e.scalar_tensor_tensor(out=out, in0=in0, scalar=scalar, in1=in1,
                                                         op0=op0, op1=op1), reads, writes)

        def cp(eng, out, in_, reads, writes):
            S.op(eng, lambda e: e.tensor_copy(out=out, in_=in_), reads, writes)

        def recip(out, in_, reads, writes):
            S.op("dve", lambda e: e.reciprocal(out=out, in_=in_), reads, writes)

        def mset(eng, ap, val, writes):
            S.op(eng, lambda e: e.memset(ap, val), (), writes)

        def dma(q, out, in_, semkey, reads, writes, group=None):
            S.dma(q, lambda e: e.dma_start(out=out, in_=in_), semkey, reads, writes, group)

        def xload(xt, k_xt, src, T0, semname):
            for s in range(4):
                dma("sp", xt[:, s, :], src[T0 + s * 128:T0 + (s + 1) * 128, :], (semname, s), (), [(k_xt, s)])

        def xstore(xt, k_xt, dst, T0, semname, s):
            dma("sp", dst[T0 + s * 128:T0 + (s + 1) * 128, :], xt[:, s, :], (semname, s), [(k_xt, s)],
                [("xdst", semname, T0, s)])

        identb, k_identb = AR.alloc([128], BF16)
        maskb, k_maskb = AR.alloc([128], BF16)
        bonesb, k_bonesb = AR.alloc([128], BF16)
        o256b, k_o256b = AR.alloc([128], BF16)
        pp, k_pp = AR.alloc([2, NPP], F32)
        mhalf, k_mhalf = AR.alloc([1], F32)
        dma("pool", identb, cst_d[:, 0:128], "c0", (), [k_identb])
        dma("pool", maskb, cst_d[:, 128:256], "c1", (), [k_maskb])
        dma("pool", bonesb, cst_d[:, 256:384], "c2", (), [k_bonesb])
        dma("sp", pp, pp_d.rearrange("l p n -> p l n"), "c4", (), [k_pp])
        mset("pool", o256b, 1.0 / 256.0, [k_o256b])
        mset("pool", mhalf, -0.5, [k_mhalf])
        base_top = AR.top

        def ppc(l, c, n=1, parts=128):
            return pp[0:parts, l, c:c + n]

        def rmsnorm_to_hT(l, xt, k_xt, gcol, hT, k_hT, sm, psT_rot):
            norm_part(xt, k_xt, sm)
            transpose_part(l, gcol, hT, k_hT, sm, psT_rot)

        def norm_part(xt, k_xt, sm, lnexp=False):
            junk, k_junk, ss, k_ss, sd, k_sd, rstd, k_rstd, xn, k_xn = sm
            for s in range(4):
                act(junk, xt[:, s, :], AF.Square, [(k_xt, s)], [k_junk, (k_ss, s)], accum_out=ss[:, s:s + 1])
            if lnexp:
                act(sd, ss, AF.Ln, [(k_ss, s_) for s_ in range(4)], [k_sd], scale=1.0 / D, bias=EPS)
                act(rstd, sd, AF.Exp, [k_sd], [k_rstd], scale=-0.5)
            else:
                act(sd, ss, AF.Sqrt, [(k_ss, s_) for s_ in range(4)], [k_sd], scale=1.0 / D, bias=EPS)
                recip(rstd, sd, [k_sd], [k_rstd])
            for s in range(4):
                if s % 2 == 0:
                    ts("dve", xn[:, s, :], xt[:, s, :], rstd[:, s:s + 1], None, ALU.mult, None,
                       [(k_xt, s), k_rstd], [(k_xn, s)])
                else:
                    act(xn[:, s, :], xt[:, s, :], AF.Copy, [(k_xt, s), k_rstd], [(k_xn, s)],
                        scale=rstd[:, s:s + 1])

        def transpose_part(l, gcol, hT, k_hT, sm, psT_rot):
            junk, k_junk, ss, k_ss, sd, k_sd, rstd, k_rstd, xn, k_xn = sm
            for c in range(8):
                pt, k_pt = psT_rot.next()
                for s in range(4):
                    tr(pt[:, s * 128:(s + 1) * 128], xn[:, s, c * 128:(c + 1) * 128], identb,
                       [(k_xn, s), k_identb], [k_pt])
                if c % 2 == 0:
                    act(hT[:, c, :], pt[:, 0:512], AF.Copy, [k_pt, k_pp], [(k_hT, c)], scale=ppc(l, gcol + c))
                else:
                    ts("dve", hT[:, c, :], pt[:, 0:512], ppc(l, gcol + c), None, ALU.mult, None,
                       [k_pt, k_pp], [(k_hT, c)])

        def phase_A(l, x_src):
            AR.top = base_top
            invdiv, k_invdiv = AR.alloc([2, 512], F32)
            ones8, k_ones8 = AR.alloc([512], F32, parts=8)
            dma("sp", invdiv, cst_d[:, 384:1408].rearrange("p (a b) -> p a b", a=2), "c3", (), [k_invdiv])
            mset("pool", ones8, 1.0, [k_ones8])
            win, k_win = AR.alloc([8, DIN], BF16)
            pwb, k_pwb = AR.alloc([2, 256], BF16)
            pbd, k_pbd = AR.alloc([2, 128], BF16)
            dg, k_dg = AR.alloc([2, 31, 128], BF16)
            xt, k_xt = AR.alloc([4, 1024], F32)
            junk, k_junk = AR.alloc([1024], BF16)
            ss, k_ss = AR.alloc([4], F32)
            sd, k_sd = AR.alloc([4], F32)
            rstd, k_rstd = AR.alloc([4], F32)
            xn, k_xn = AR.alloc([4, 1024], BF16)
            hTs = [AR.alloc([8, 512], BF16) for _ in range(2)]
            qraws = Rot([AR.alloc([512], F32) for _ in range(2)])
            qsqs = Rot([AR.alloc([512], BF16) for _ in range(2)])
            sdqs = Rot([AR.alloc([512], F32) for _ in range(2)])
            rqs = Rot([AR.alloc([512], F32) for _ in range(2)])
            qsts = Rot([AR.alloc([512], BF16) for _ in range(2)])
            zf, k_zf = AR.alloc([512], F32, parts=8)
            ef, k_ef = zf, k_zf
            nl, k_nl = AR.alloc([512], F32, parts=8)
            Gb = [AR.alloc([512], F32, parts=8) for _ in range(2)]
            r1, k_r1 = AR.alloc([512], F32, parts=8)
            r2, k_r2 = AR.alloc([512], F32, parts=8)
            gk, k_gk = AR.alloc([6, 512], BF16, parts=8)
            gq, k_gq = AR.alloc([6, 512], BF16, parts=8)
            vsts = Rot([AR.alloc([4, 256], BF16) for _ in range(2)])
            sg, k_sg = AR.alloc([2, 512], F32)
            g2, k_g2 = AR.alloc([2, 542], BF16)
            cv, k_cv = AR.alloc([2, 512], F32)
            cvb, k_cvb = AR.alloc([2, 512], BF16)
            csq, k_csq = AR.alloc([2, 512], BF16)
            mu, k_mu = AR.alloc([512], F32)
            musq, k_musq = AR.alloc([512], F32)
            var, k_var = musq, k_musq
            sdc, k_sdc = AR.alloc([512], F32)
            rc, k_rc = sdc, k_sdc
            dd, k_dd = AR.alloc([2, 512], F32)
            sT, k_sT = AR.alloc([2, 512], BF16)
            PU, k_PU = AR.alloc([2, 528], F32)
            S2, k_S2 = AR.alloc([2, 528], F32)
            S4, k_S4 = AR.alloc([2, 528], F32)
            S8, k_S8 = AR.alloc([528], F32)
            S16, k_S16 = AR.alloc([528], F32)
            dT, k_dT = AR.alloc([2, 512], BF16)
            cpsts = Rot([AR.alloc([4, 512], BF16) for _ in range(2)])

            wv = w_in[l].rearrange("(c p) n -> p c n", p=128)
            for hf in (1, 0):
                for c in range(8):
                    dma("pool", win[:, c, hf * 1156:(hf + 1) * 1156], wv[:, c, hf * 1156:(hf + 1) * 1156],
                        ("win", hf), (), [(k_win, c, hf)], group=("win", hf, l))
            dma("pool", pwb, pw_w[l].rearrange("(c p) n -> p c n", p=128), "pw", (), [k_pwb])
            mset("dve", pbd, 0.0, [(k_pbd, g_) for g_ in range(4)])
            for g in range(4):
                c, o = g // 2, g % 2
                dma("pool", pbd[64 * o:64 * o + 64, c, 64 * o:64 * o + 64], pool_w[l, g], "pbd", [], [(k_pbd, g)],
                    group=("pbd", l))
            for c in range(2):
                for k in range(31):
                    if k % 2 == 0:
                        ts("dve", dg[:, c, k, :], identb, ppc(l, C_DWW + c * 31 + k), None, ALU.mult, None,
                           [k_identb, k_pp], [(k_dg, c, k)])
                    else:
                        act(dg[:, c, k, :], identb, AF.Copy, [k_identb, k_pp], [(k_dg, c, k)],
                            scale=ppc(l, C_DWW + c * 31 + k))
            mset("pool", gk, 1.0, [k_gk])
            mset("pool", gq, 1.0, [k_gq])
            for (v_, kv_) in vsts.items:
                mset("pool", v_, 1.0, [kv_])
            mset("dve", g2[:, :, 0:30], 0.0, [k_g2])
            mset("dve", PU[:, :, 0:16], 0.0, [k_PU])
            mset("pool", S2, 0.0, [k_S2])
            mset("pool", S4, 0.0, [k_S4])
            mset("pool", S8, 0.0, [k_S8])
            mset("pool", S16, 0.0, [k_S16])

            psT_rot = Rot([psb(6), psb(7)])
            prot = Rot([psf(i) for i in range(6)])
            sm = (junk, k_junk, ss, k_ss, sd, k_sd, rstd, k_rstd, xn, k_xn)

            pending = []

            def defer(n, fn):
                pending.append([n, fn])

            def tick():
                for p in list(pending):
                    p[0] -= 1
                    if p[0] <= 0:
                        pending.remove(p)
                        p[1]()

            def flush():
                while pending:
                    tick()

            xload(xt, k_xt, x_src, 0, "xt")
            norm_part(xt, k_xt, sm, True)
            for i in range(NT):
                T0 = i * 512
                hT, k_hT = hTs[i % 2]
                transpose_part(l, C_N1G, hT, k_hT, sm, psT_rot)
                if i + 1 < NT:
                    xload(xt, k_xt, x_src, T0 + 512, "xt")
                hreads = [(k_hT, c) for c in range(8)]

                def proj(col, M):
                    pa, k_pa = prot.next()
                    for c in range(8):
                        wk = [(k_win, c, hf_) for hf_ in range(2) if (col < (hf_ + 1) * 1156 and col + M > hf_ * 1156)]
                        mm(pa[0:M, :], win[:, c, col:col + M], hT[:, c, :], c == 0, c == 7,
                           wk + [(k_hT, c)], [k_pa])
                    return pa, k_pa

                def qk_group(isq, hp):
                    col = (0 if isq else 512) + hp * 128
                    pa, k_pa = proj(col, 128)
                    qraw, k_qraw = qraws.next()
                    qsq, k_qsq = qsqs.next()
                    sdq, k_sdq = sdqs.next()
                    rq, k_rq = rqs.next()
                    qst, k_qst = qsts.next()
                    act(qraw, pa, AF.Copy, [k_pa], [k_qraw])
                    tt("pool", qsq, qraw, qraw, ALU.mult, [k_qraw], [k_qsq])

                    def stage2():
                        pb, k_pb = prot.next()
                        mm(pb, bonesb, qsq, True, True, [k_bonesb, k_qsq], [k_pb])
                        if isq:
                            act(sdq, pb, AF.Ln, [k_pb], [k_sdq], scale=1.0, bias=64.0 * EPS)
                        else:
                            act(sdq, pb, AF.Ln, [k_pb], [k_sdq], scale=1.0 / 64.0, bias=EPS)
                        act(rq, sdq, AF.Exp, [k_sdq], [k_rq], scale=-0.5)
                        stt(qst, qraw, ppc(l, C_QG if isq else C_KG), rq, ALU.mult, ALU.mult,
                            [k_qraw, k_rq, k_pp], [k_qst])
                        dst = qs_d if isq else ks_d
                        for o in range(2):
                            h = 2 * hp + o
                            dma("sp", dst[h, 0:64, T0:T0 + 512], qst[64 * o:64 * o + 64, :],
                                ("qst", qsts.i % 2), [k_qst], [("qk", isq, h, i)])
                    defer(2, stage2)

                cpst, k_cpst = cpsts.next()
                if i > 0:
                    cp("pool", g2[:, :, 0:30], g2[:, :, 512:542], [k_g2], [k_g2])
                pbs = [proj(1544 + 256 + 128 * c, 128) for c in range(2)]
                pas = [proj(1544 + 128 * c, 128) for c in range(2)]
                for c in range(2):
                    act(sg[:, c, :], pbs[c][0], AF.Tanh, [pbs[c][1]], [(k_sg, c)], scale=0.5)
                    stt(g2[:, c, 30:542], sg[:, c, :], 1.0, pas[c][0], ALU.add, ALU.mult,
                        [pas[c][1], (k_sg, c)], [k_g2])
                tick()
                pps = [proj(2056 + 128 * c, 128) for c in range(2)]
                if i > 0:
                    cp("pool", PU[:, :, 0:16], PU[:, :, 512:528], [k_PU], [k_PU])
                for c in range(2):
                    act(PU[:, c, 16:528], pps[c][0], AF.Copy, [pps[c][1]], [k_PU])
                tt("pool", S2[:, :, 1:528], PU[:, :, 1:528], PU[:, :, 0:527], ALU.add, [k_PU], [k_S2])
                tt("pool", S4[:, :, 3:528], S2[:, :, 3:528], S2[:, :, 1:526], ALU.add, [k_S2], [k_S4])
                tt("pool", S8[:, 7:528], S4[:, 1, 7:528], S4[:, 1, 3:524], ALU.add, [k_S4], [k_S8])
                tt("pool", S16[64:128, 15:528], S8[64:128, 15:528], S8[64:128, 7:520], ALU.add, [k_S8], [k_S16])
                srcs = [(S2[0:64, 0, 16:528], k_S2, 0.5, 0, 0), (S4[64:128, 0, 16:528], k_S4, 0.25, 0, 64),
                        (S8[0:64, 16:528], k_S8, 0.125, 1, 0), (S16[64:128, 16:528], k_S16, 0.0625, 1, 64)]
                for (sap, ksap, inv, c, p0) in srcs:
                    if i == 0:
                        tt("dve", dd[p0:p0 + 64, c, :], sap, invdiv[p0:p0 + 64, c, :], ALU.mult,
                           [ksap, k_invdiv], [(k_dd, c)])
                        tt("dve", dT[p0:p0 + 64, c, :], dd[p0:p0 + 64, c, :], PU[p0:p0 + 64, c, 16:528],
                           ALU.subtract, [(k_dd, c), k_PU], [(k_dT, c)])
                    else:
                        stt(dT[p0:p0 + 64, c, :], sap, inv, PU[p0:p0 + 64, c, 16:528], ALU.mult, ALU.subtract,
                            [ksap, k_PU], [(k_dT, c)])

                qk_group(True, 0)
                tick()
                qk_group(False, 0)
                tick()
                qk_group(True, 1)
                tick()
                qk_group(False, 1)
                tick()
                if i + 1 < NT:
                    norm_part(xt, k_xt, sm, True)
                pcs = []
                for c in range(2):
                    pa, k_pa = prot.next()
                    for k in range(31):
                        mm(pa, dg[:, c, k, :], g2[:, c, k:k + 512], k == 0, k == 30, [(k_dg, c, k), k_g2], [k_pa])
                    act(cv[:, c, :], pa, AF.Identity, [k_pa, k_pp], [(k_cv, c)], bias=ppc(l, C_DWB + c), scale=0.5)
                    cp("pool", cvb[:, c, :], cv[:, c, :], [(k_cv, c)], [(k_cvb, c)])
                    tt("pool", csq[:, c, :], cv[:, c, :], cv[:, c, :], ALU.mult, [(k_cv, c)], [(k_csq, c)])

                qk_group(True, 2)
                tick()
                qk_group(False, 2)
                tick()
                qk_group(True, 3)
                tick()
                qk_group(False, 3)
                tick()
                pa, k_pa = proj(1536, 8)
                act(zf, pa[0:8, :], AF.Identity, [k_pa, k_pp], [k_zf], bias=ppc(l, C_BF, 1, 8))
                act(ef, zf, AF.Exp, [k_zf], [k_ef], scale=-1.0)
                act(nl, ef, AF.Ln, [k_ef], [k_nl], bias=1.0)
                G, k_G = Gb[i % 2]
                Gp, k_Gp = Gb[(i + 1) % 2]
                if i == 0:
                    S.op("dve", lambda e, G=G: e.tensor_tensor_scan(out=G, data0=ones8, data1=nl, initial=0.0,
                                                                    op0=ALU.mult, op1=ALU.add),
                         [k_ones8, k_nl], [k_G])
                else:
                    S.op("dve", lambda e, G=G, Gp=Gp: e.tensor_tensor_scan(
                        out=G, data0=ones8, data1=nl, initial=Gp[:, 511:512], op0=ALU.mult, op1=ALU.add),
                        [k_ones8, k_nl, k_Gp], [k_G])
                cp("dve", gk[:, 3, :], G, [k_G], [k_gk])
                tt("dve", r1, G, gk[:, 3, :], ALU.subtract, [k_G, k_gk], [k_r1])
                cp("dve", gk[:, 4, :], r1, [k_r1], [k_gk])
                tt("dve", r2, r1, gk[:, 4, :], ALU.subtract, [k_r1, k_gk], [k_r2])
                cp("dve", gk[:, 5, :], r2, [k_r2], [k_gk])
                ts("dve", gq[:, 0:3, :], gk[:, 3:6, :], -1.0, None, ALU.mult, None, [k_gk], [k_gq])
                dma("sp", ks_d[:, 64:70, T0:T0 + 512], gk, "gk", [k_gk], [("gkd", i)])
                dma("sp", qs_d[:, 64:70, T0:T0 + 512], gq, "gq", [k_gq], [("gqd", i)])
                tick()

                pmu, k_pmu = prot.next()
                for c in range(2):
                    mm(pmu, o256b, cvb[:, c, :], c == 0, c == 1, [k_o256b, (k_cvb, c)], [k_pmu])
                pex, k_pex = prot.next()
                for c in range(2):
                    mm(pex, o256b, csq[:, c, :], c == 0, c == 1, [k_o256b, (k_csq, c)], [k_pex])
                cp("dve", mu, pmu, [k_pmu], [k_mu])
                tt("dve", musq, mu, mu, ALU.mult, [k_mu], [k_musq])
                tt("dve", var, pex, musq, ALU.subtract, [k_pex, k_musq], [k_var])
                ts("dve", var, var, 0.0, None, ALU.max, None, [k_var], [k_var])
                act(sdc, var, AF.Ln, [k_var], [k_sdc], scale=1.0, bias=EPS)
                act(rc, sdc, AF.Exp, [k_sdc], [k_rc], scale=-0.5)
                for c in range(2):
                    tt("dve", dd[:, c, :], cv[:, c, :], mu, ALU.subtract, [(k_cv, c), k_mu], [(k_dd, c)])
                    tt("dve", dd[:, c, :], dd[:, c, :], rc, ALU.mult, [(k_dd, c), k_rc], [(k_dd, c)])
                    act(sT[:, c, :], dd[:, c, :], AF.Silu, [(k_dd, c), k_pp], [(k_sT, c)],
                        scale=ppc(l, C_LNG + c), bias=ppc(l, C_LNB + c))
                for s in range(4):
                    pa, k_pa = prot.next()
                    for c in range(8):
                        mm(pa, hT[:, c, s * 128:(s + 1) * 128], win[:, c, 1024:1536], c == 0, c == 7,
                           [(k_win, c, 0), (k_win, c, 1), (k_hT, c)], [k_pa])
                    vst, k_vst = vsts.next()
                    pv4 = pa.rearrange("p (a b d) -> p a b d", a=4, b=2)
                    act(vst[:, :, 0:64], pv4[:, :, 0, :], AF.Copy, [k_pa], [k_vst])
                    cp("dve", vst[:, :, 192:256], pv4[:, :, 1, :], [k_pa], [k_vst])
                    dma("sp", vs_d[:, 4 * i + s, :], vst.rearrange("p a b -> p (a b)"),
                        ("vst", vsts.i % 2), [k_vst], [("vsd", i, s)])
                    tick()

                for c in range(2):
                    pa, k_pa = prot.next()
                    mm(pa, pbd[:, c, :], dT[:, c, :], True, True, [(k_pbd, 2 * c), (k_pbd, 2 * c + 1), (k_dT, c)], [k_pa])
                    act(cpst[:, 2 + c, :], pa, AF.Copy, [k_pa, k_pp], [k_cpst], scale=ppc(l, C_PSC + c))
                for co in range(2):
                    pa, k_pa = prot.next()
                    for ci in range(2):
                        mm(pa, pwb[:, ci, co * 128:(co + 1) * 128], sT[:, ci, :], ci == 0, ci == 1,
                           [k_pwb, (k_sT, ci)], [k_pa])
                    cp("dve", cpst[:, co, :], pa, [k_pa], [k_cpst])
                dma("sp", cps_d[:, :, T0:T0 + 512], cpst, ("cpst", cpsts.i % 2), [k_cpst], [("cpsd", i)])
                flush()
            S.barrier()
            print("arena A", AR.top)

        def phase_B(l, x_src, x_dst):
            AR.top = base_top
            kaug = [AR.alloc([SEQ], BF16) for _ in range(8)]
            vaug, k_vaug = AR.alloc([32, 1024], BF16)
            wout, k_wout = AR.alloc([8, 1024], BF16)
            qcs = [AR.alloc([8, 512], BF16) for _ in range(2)]
            mixT, k_mixT = AR.alloc([8, 512], BF16)
            pts = Rot([AR.alloc([512], BF16) for _ in range(4)])
            recs = Rot([AR.alloc([512], F32) for _ in range(2)])
            xt, k_xt = AR.alloc([4, 1024], F32)

            def load_kv(cc):
                for h in range(8):
                    dma("sp", kaug[h][0][0:70, cc * 512:(cc + 1) * 512], ks_d[h, :, cc * 512:(cc + 1) * 512],
                        ("kaug", cc), (), [(kaug[h][1], cc)], group=("kaug", cc, l))
                dma("sp", vaug[:, 4 * cc:4 * cc + 4, :], vs_d[:, 4 * cc:4 * cc + 4, :], ("vaug", cc), (),
                    [(k_vaug, cc)])

            wv = w_out[l].rearrange("(c p) n -> p c n", p=128)
            for c in range(8):
                dma("pool", wout[:, c, :], wv[:, c, :], "wout", (), [(k_wout, c)], group=("wout", l))
            srot = Rot([psf(i) for i in range(4)])
            orot = Rot([psf(4), psf(5)])
            xrot = Rot([psf(6), psf(7)])

            def load_chunk(c):
                qc, k_qc = qcs[c % 2]
                T0 = c * 512
                dma("sp", qc[0:70, :, :], qs_d[:, :, T0:T0 + 512].rearrange("h r t -> r h t"),
                    ("qc", c % 2), (), [k_qc])

            load_chunk(0)
            load_kv(0)
            for c in range(NT):
                T0 = c * 512
                qc, k_qc = qcs[c % 2]
                dma("sp", mixT[:, 4:8, :], cps_d[:, :, T0:T0 + 512], "mixcp", (), [(k_mixT, "cp")])
                xload(xt, k_xt, x_src, T0, "xtb")
                if c + 1 < NT:
                    load_chunk(c + 1)
                    load_kv(c + 1)
                nkb = 4 * c + 4
                items = [(h, kb) for h in range(8) for kb in range(nkb)]
                state = {}

                def s_stage(h, kb):
                    q0 = max(0, 128 * kb - T0)
                    pS, k_pS = srot.next()
                    diag = kb >= 4 * c
                    ka, k_ka = kaug[h]
                    mm(pS[:, q0:512], ka[0:70, kb * 128:(kb + 1) * 128], qc[0:70, h, q0:512], True, not diag,
                       [(k_ka, kb // 4), k_qc], [k_pS])
                    if diag:
                        mm(pS[:, q0:q0 + 128], identb, maskb, False, True, [k_identb, k_maskb], [k_pS])
                    pt, k_pt = pts.next()
                    act(pt[:, q0:512], pS[:, q0:512], AF.Exp, [k_pS], [k_pt])
                    state[(h, kb)] = (pt, k_pt, q0)

                def pv_stage(h, kb):
                    pt, k_pt, q0 = state.pop((h, kb))
                    if kb == 0:
                        state[("o", h)] = orot.next()
                    pO, k_pO = state[("o", h)]
                    mm(pO[:, q0:512], vaug[:, kb, h * 128:(h + 1) * 128], pt[:, q0:512], kb == 0, kb == nkb - 1,
                       [(k_vaug, kb // 4), k_pt], [k_pO])
                    if kb == nkb - 1:
                        hp, o = h // 2, h % 2
                        rec, k_rec = recs.next()
                        a0, b0 = (0, 64) if o == 0 else (64, 0)
                        recip(rec[b0:b0 + 64, :], pO[b0:b0 + 64, :], [k_pO], [k_rec])
                        tt("dve", mixT[a0:a0 + 64, hp, :], pO[a0:a0 + 64, :], rec[b0:b0 + 64, :], ALU.mult,
                           [k_pO, k_rec], [(k_mixT, hp)])
                        state.pop(("o", h))

                DEPTH = 2
                for n_, (h, kb) in enumerate(items):
                    s_stage(h, kb)
                    if n_ >= DEPTH:
                        pv_stage(*items[n_ - DEPTH])
                for n_ in range(max(0, len(items) - DEPTH), len(items)):
                    pv_stage(*items[n_])

                mreads = [(k_mixT, hp) for hp in range(4)] + [(k_mixT, "cp")]
                for s in range(4):
                    for n in range(2):
                        pX, k_pX = xrot.next()
                        for kc in range(8):
                            mm(pX, mixT[:, kc, s * 128:(s + 1) * 128], wout[:, kc, n * 512:(n + 1) * 512],
                               kc == 0, kc == 7, mreads + [(k_wout, kc)], [k_pX])
                        tt("dve", xt[:, s, n * 512:(n + 1) * 512], pX, xt[:, s, n * 512:(n + 1) * 512], ALU.add,
                           [k_pX, (k_xt, s)], [(k_xt, s)])
                    xstore(xt, k_xt, x_dst, T0, "xtb_st", s)
            S.barrier()

        def phase_C(l, x_src, x_dst):
            AR.top = base_top
            wup, k_wup = AR.alloc([8, 2 * DFF], BF16)
            wdn, k_wdn = AR.alloc([22, 1024], BF16)
            xt, k_xt = AR.alloc([4, 1024], F32)
            xs, k_xs = AR.alloc([1024], F32)
            ss, k_ss = AR.alloc([4], F32)
            sd, k_sd = AR.alloc([4], F32)
            rstd, k_rstd = AR.alloc([4], F32)
            xn, k_xn = AR.alloc([4, 1024], BF16)
            hTs = [AR.alloc([8, 512], BF16) for _ in range(2)]
            actT, k_actT = AR.alloc([13, 512], BF16)
            abufs = Rot([AR.alloc([512], F32) for _ in range(5)])
            sgs = Rot([AR.alloc([512], F32) for _ in range(2)])
            hl, k_hl = AR.alloc([44, 2], F32)
            HC, k_HC = AR.alloc([44, 2], F32)
            htmp, k_htmp = AR.alloc([44], F32)
            fdw = pp[:, l, C_FDW:C_FDW + 132].rearrange("p (c k) -> p c k", k=3)

            wv = w_up[l].rearrange("(c p) n -> p c n", p=128)
            for q4 in (0, 2, 1, 3):
                for c in range(8):
                    dma("pool", wup[:, c, q4 * 1408:(q4 + 1) * 1408], wv[:, c, q4 * 1408:(q4 + 1) * 1408],
                        ("wup", q4), (), [(k_wup, c, q4)], group=("wup", q4, l))
            wd = w_down[l].rearrange("(j p) n -> p j n", p=128)
            for j in range(22):
                dma("pool", wdn[:, j, :], wd[:, j, :], "wdn", (), [(k_wdn, j)], group=("wdn", l))

            psT_rot = Rot([psb(7)])
            urot = Rot([psf(i) for i in range(5)])
            drot = Rot([psf(5), psf(6)])
            sm = (None, None, ss, k_ss, sd, k_sd, rstd, k_rstd, xn, k_xn)

            def nc_load(ti, s):
                r0 = ti * 512 + s * 128
                dma("sp", xs, x_src[r0:r0 + 128, :], "xs", (), [k_xs])

            def nc_comp(s):
                act(xn[:, s, :], xs, AF.Square, [k_xs], [(k_xn, s), (k_ss, s)], accum_out=ss[:, s:s + 1])
                ts("dve", sd[:, s:s + 1], ss[:, s:s + 1], 1.0 / D, EPS, ALU.mult, ALU.add, [(k_ss, s)], [(k_sd, s)])
                tt("pool", rstd[:, s:s + 1], sd[:, s:s + 1], mhalf, ALU.pow, [(k_sd, s), k_mhalf], [(k_rstd, s)])
                ts("dve", xn[:, s, :], xs, rstd[:, s:s + 1], None, ALU.mult, None, [k_xs, (k_rstd, s)],
                   [(k_xn, s)])

            for s in range(4):
                nc_load(0, s)
                nc_comp(s)
            transpose_part(l, C_N2G, hTs[0][0], hTs[0][1], sm, psT_rot)
            xload(xt, k_xt, x_src, 0, "xtc")
            for i in range(NT):
                T0 = i * 512
                hT, k_hT = hTs[i % 2]
                if i > 0:
                    hlk = [(k_hl, ch_) for ch_ in range(44)]
                    tt("pool", HC[:, :, 0], hl[:, :, 1], fdw[:, :, 1], ALU.mult, hlk + [k_pp], [k_HC])
                    tt("pool", htmp, hl[:, :, 0], fdw[:, :, 0], ALU.mult, hlk + [k_pp], [k_htmp])
                    tt("pool", HC[:, :, 0], HC[:, :, 0], htmp, ALU.add, [k_HC, k_htmp], [k_HC])
                    tt("pool", HC[:, :, 1], hl[:, :, 1], fdw[:, :, 0], ALU.mult, hlk + [k_pp], [k_HC])
                def wdown(half, i=i, T0=T0):
                    slots = [(jj if (half == 0 or jj >= 2) else 11 + jj) for jj in range(11)]
                    areads = [(k_actT, sl) for sl in slots]
                    for s in range(4):
                        for n in range(2):
                            pD, k_pD = drot.next()
                            for jj in range(11):
                                j = half * 11 + jj
                                mm(pD, actT[:, slots[jj], s * 128:(s + 1) * 128],
                                   wdn[:, j, n * 512:(n + 1) * 512],
                                   jj == 0, jj == 10, areads + [(k_wdn, j)], [k_pD])
                            tt("dve", xt[:, s, n * 512:(n + 1) * 512], pD, xt[:, s, n * 512:(n + 1) * 512],
                               ALU.add, [k_pD, (k_xt, s)], [(k_xt, s)])
                        if half == 1:
                            xstore(xt, k_xt, x_dst, T0, "xtc_st", s)
                            if i + 1 < NT:
                                dma("sp", xt[:, s, :], x_src[T0 + 512 + s * 128:T0 + 512 + (s + 1) * 128, :],
                                    ("xtc", s), (), [(k_xt, s)])

                for half in range(2):
                    for jj in range(11):
                        j = half * 11 + jj
                        res = []
                        for which in range(2):
                            ch = j + 22 * which
                            col = ch * 128
                            pU, k_pU = urot.next()
                            for c in range(8):
                                mm(pU, wup[:, c, col:col + 128], hT[:, c, :], c == 0, c == 7,
                                   [(k_wup, c, col // 1408), (k_hT, c)], [k_pU])
                            ab, k_ab = abufs.next()
                            act(ab, pU, AF.Copy, [k_pU, k_pp], [k_ab], scale=ppc(l, C_FDW + ch * 3 + 2))
                            stt(ab[:, 1:512], pU[:, 0:511], ppc(l, C_FDW + ch * 3 + 1), ab[:, 1:512],
                                ALU.mult, ALU.add, [k_pU, k_ab, k_pp], [k_ab])
                            stt(ab[:, 2:512], pU[:, 0:510], ppc(l, C_FDW + ch * 3 + 0), ab[:, 2:512],
                                ALU.mult, ALU.add, [k_pU, k_ab, k_pp], [k_ab])
                            if i > 0:
                                tt("pool", ab[:, 0:2], ab[:, 0:2], HC[:, ch, :], ALU.add, [k_ab, k_HC], [k_ab])
                            if i + 1 < NT:
                                act(hl[:, ch, :], pU[:, 510:512], AF.Copy, [k_pU], [(k_hl, ch)])
                            res.append((ab, k_ab))
                        sgb, k_sgb = sgs.next()
                        act(sgb, res[0][0], AF.Silu, [res[0][1]], [k_sgb])
                        slot = jj if (half == 0 or jj >= 2) else 11 + jj
                        tt("pool", actT[:, slot, :], sgb, res[1][0], ALU.mult, [k_sgb, res[1][1]],
                           [(k_actT, slot)])
                        if half == 1 and jj == 1:
                            wdown(0)
                        if half == 0 and i + 1 < NT:
                            if jj in (1, 3, 5, 7):
                                if jj > 1:
                                    nc_comp((jj - 3) // 2)
                                nc_load(i + 1, (jj - 1) // 2)
                            elif jj == 9:
                                nc_comp(3)
                    if half == 1:
                        if i + 1 < NT:
                            transpose_part(l, C_N2G, hTs[(i + 1) % 2][0], hTs[(i + 1) % 2][1], sm, psT_rot)
                        wdown(1)
            S.barrier()
            print("arena C", AR.top)

        nl_ = len(layers)
        for li, l in enumerate(layers):
            src = x_in if li == 0 else xl_d
            dst = y_out if li == nl_ - 1 else xl_d
            phase_A(l, src)
            phase_B(l, src, xmid_d)
            phase_C(l, xmid_d, dst)
        S.emit(st)
    return nc


def _pack_params(inp):
    pp = np.zeros((2, 128, NPP), np.float32)
    for l in range(2):
        pp[l, :, C_N1G:C_N1G + 8] = inp["norm1_g"][l].reshape(8, 128).T
        pp[l, :, C_N2G:C_N2G + 8] = inp["norm2_g"][l].reshape(8, 128).T
        pp[l, :, C_QG] = np.tile(inp["q_norm_g"][l], 2)
        pp[l, :, C_KG] = np.tile(inp["k_norm_g"][l], 2)
        pp[l, :, C_DWB:C_DWB + 2] = inp["conv_dw_b"][l].reshape(2, 128).T
        pp[l, :, C_LNG:C_LNG + 2] = inp["conv_ln_g"][l].reshape(2, 128).T
        pp[l, :, C_LNB:C_LNB + 2] = inp["conv_ln_b"][l].reshape(2, 128).T
        pp[l, :, C_PSC:C_PSC + 2] = inp["pool_scale"][l].reshape(2, 128).T
        pp[l, 0:8, C_BF] = inp["b_f"][l]
        pp[l, :, C_DWW:C_DWW + 62] = inp["conv_dw_w"][l].T.reshape(2, 128, 31).transpose(1, 0, 2).reshape(128, 62)
        pp[l, :, C_FDW:C_FDW + 132] = inp["ffn_dw_w"][l].T.reshape(44, 128, 3).transpose(1, 0, 2).reshape(128, 132)
    return pp


def _consts():
    cst = np.zeros((128, 3 * 128 + 1024), np.float32)
    cst[:, 0:128] = np.eye(128, dtype=np.float32)
    k = np.arange(128)[:, None]
    q = np.arange(128)[None, :]
    cst[:, 128:256] = np.where(k > q, -30000.0, 0.0)
    cst[:, 256:384] = (k // 64 == q // 64).astype(np.float32)
    t = np.arange(512, dtype=np.float32) + 1.0
    wins = [2.0, 4.0, 8.0, 16.0]
    inv = np.zeros((128, 2, 512), np.float32)
    for g in range(4):
        c, o = g // 2, g % 2
        inv[64 * o:64 * o + 64, c, :] = 1.0 / np.minimum(t, wins[g])
    cst[:, 384:] = inv.reshape(128, 1024)
    return cst


FUSED = True
_CACHE = {}


def _get_prog(layers):
    key = tuple(layers)
    if key not in _CACHE:
        _CACHE[key] = build_program(list(layers))
    return _CACHE[key]


def kernel(**inputs):
    inp = {k: np.ascontiguousarray(np.asarray(v)) for k, v in inputs.items()}
    x = inp["x"].astype(np.float32, copy=False)
    pp = _pack_params(inp)
    cst = _consts()
    common = {"w_in": inp["w_in"], "w_out": inp["w_out"], "w_up": inp["w_up"], "w_down": inp["w_down"],
              "pw": inp["conv_pw_w"], "pool_w": inp["pool_w"], "pp": pp, "cst": cst}
    n = 8
    if FUSED:
        nc = _get_prog((0, 1))
        in_maps = [dict(common, x=x[b]) for b in range(n)]
        res = run_bass_kernel_spmd(nc, in_maps, core_ids=list(range(n)))
        return np.stack([r["y"] for r in res.results], axis=0).astype(np.float32)
    cur = x
    for l in range(2):
        nc = _get_prog((l,))
        in_maps = [dict(common, x=cur[b]) for b in range(n)]
        res = run_bass_kernel_spmd(nc, in_maps, core_ids=list(range(n)))
        cur = np.stack([r["y"] for r in res.results], axis=0).astype(np.float32)
    return cur
```

```python
import numpy as np
from contextlib import ExitStack
import concourse.bass as bass
import concourse.mybir as mybir
from concourse.bass_utils import run_bass_kernel_spmd

F32 = mybir.dt.float32
BF16 = mybir.dt.bfloat16
U8 = mybir.dt.uint8
AF = mybir.ActivationFunctionType
ALU = mybir.AluOpType

D = 1024
SEQ = 4096
NT = 8
DIN = 2312
DFF = 2816
EPS = 1e-6
NPP = 221
C_N1G, C_N2G, C_QG, C_KG, C_DWB, C_LNG, C_LNB, C_PSC, C_BF, C_DWW, C_FDW = 0, 8, 16, 17, 18, 20, 22, 24, 26, 27, 89

ENGS = ("pe", "act", "dve", "pool", "sp")


class Op:
    __slots__ = ("eng", "fn", "deps", "is_dma", "sem", "tick", "marked", "group")

    def __init__(self, eng, fn, is_dma):
        self.eng = eng
        self.fn = fn
        self.deps = []
        self.is_dma = is_dma
        self.sem = None
        self.tick = None
        self.marked = False
        self.group = None


class Sched:
    def __init__(self, nc):
        self.nc = nc
        self.streams = {e: [] for e in ENGS}
        self.last_writer = {}
        self.readers = {}
        self.all_ops = []
        self.gid = 0

    def _add(self, op, reads, writes):
        deps = {}
        for r in reads:
            w = self.last_writer.get(r)
            if w is not None:
                deps[id(w)] = w
        for wkey in writes:
            w = self.last_writer.get(wkey)
            if w is not None:
                deps[id(w)] = w
            for rd in self.readers.get(wkey, ()):
                deps[id(rd)] = rd
        op.deps = list(deps.values())
        for r in reads:
            self.readers.setdefault(r, []).append(op)
        for wkey in writes:
            self.last_writer[wkey] = op
            self.readers[wkey] = []
        self.streams[op.eng].append(op)
        self.all_ops.append(op)
        return op

    def op(self, eng, fn, reads=(), writes=()):
        return self._add(Op(eng, fn, False), reads, writes)

    def dma(self, eng, fn, semkey, reads=(), writes=(), group=None):
        o = Op(eng, fn, True)
        o.sem = semkey
        if group is None:
            self.gid += 1
            group = ("_g", self.gid)
        o.group = (semkey, group)
        return self._add(o, reads, writes)

    def barrier(self):
        lasts = []
        for e in ENGS:
            for o in reversed(self.streams[e]):
                if not o.is_dma and o.fn is not None:
                    lasts.append(o)
                    break
        dma_last = {}
        for o in self.all_ops:
            if o.is_dma:
                dma_last[o.sem] = o
        deps = lasts + list(dma_last.values())
        for e in ENGS:
            b = Op(e, None, False)
            b.deps = list(deps)
            self.streams[e].append(b)
            self.all_ops.append(b)
        self.last_writer = {}
        self.readers = {}

    def emit(self, stack):
        nc = self.nc

        def needs_wait(o, d):
            if d.is_dma:
                if o.is_dma and o.group == d.group:
                    return False
                return True
            if d.eng == o.eng and not o.is_dma and d.eng == "pe":
                return False
            return True

        for o in self.all_ops:
            for d in o.deps:
                if not d.is_dma and needs_wait(o, d):
                    d.marked = True
        esem = {}
        for e in ENGS:
            if e != "sp":
                esem[e] = stack.enter_context(nc.semaphore("s_" + e))
        for e in ENGS:
            c = 0
            for o in self.streams[e]:
                if o.is_dma or o.fn is None:
                    continue
                if o.marked:
                    c += 1
                    o.tick = c
                o.sem = esem.get(e)
        dsem = {}
        dcount = {}
        gmax = {}
        for o in self.all_ops:
            if o.is_dma:
                k = o.sem
                if k not in dsem:
                    dsem[k] = stack.enter_context(nc.semaphore("d_%d" % len(dsem)))
                    dcount[k] = 0
                dcount[k] += 16
                o.sem = dsem[k]
                gmax[o.group] = dcount[k]
        for o in self.all_ops:
            if o.is_dma:
                o.tick = gmax[o.group]
        final_waits = [(s, dcount[k]) for k, s in dsem.items()]
        print("n dma sems", len(dsem))
        block = stack.enter_context(nc.Block())

        def make(ename):
            def body(eng):
                waited = {}
                for o in self.streams[ename]:
                    for d in o.deps:
                        if not needs_wait(o, d):
                            continue
                        key = id(d.sem)
                        if waited.get(key, 0) >= d.tick:
                            continue
                        eng.wait_ge(d.sem, d.tick)
                        waited[key] = d.tick
                    if o.fn is None:
                        continue
                    inst = o.fn(eng)
                    if o.is_dma:
                        inst.then_inc(o.sem, 16)
                    elif o.marked:
                        inst.then_inc(o.sem, 1)
                if ename == "sp":
                    for s, v in final_waits:
                        eng.wait_ge(s, v)
            return body

        block.tensor(make("pe"))
        block.scalar(make("act"))
        block.vector(make("dve"))
        block.gpsimd(make("pool"))
        block.sync(make("sp"))


class Arena:
    def __init__(self, nc, nbytes):
        self.t = nc.alloc_sbuf_tensor("arena", [128, nbytes], U8)
        self.n = nbytes
        self.top = 0
        self.cnt = 0

    def alloc(self, free, dt, parts=128):
        esz = 4 if dt == F32 else 2
        n = esz
        for f in free:
            n *= f
        off = (self.top + 63) // 64 * 64
        self.top = off + n
        assert self.top <= self.n, ("SBUF arena overflow", self.top, self.n)
        ap = self.t[0:parts, off:off + n].bitcast(dt)
        if len(free) == 2:
            ap = ap.rearrange("p (a b) -> p a b", a=free[0])
        elif len(free) == 3:
            ap = ap.rearrange("p (a b c) -> p a b c", a=free[0], b=free[1])
        self.cnt += 1
        return ap, ("sb", self.cnt)


class Rot:
    def __init__(self, items):
        self.items = items
        self.i = 0

    def next(self):
        it = self.items[self.i % len(self.items)]
        self.i += 1
        return it


def build_program(layers):
    nc = bass.Bass("TRN2", target_bir_lowering=False)
    x_in = nc.dram_tensor("x", [SEQ, D], F32, kind="ExternalInput").ap()
    w_in = nc.dram_tensor("w_in", [2, D, DIN], F32, kind="ExternalInput").ap()
    w_out = nc.dram_tensor("w_out", [2, D, D], F32, kind="ExternalInput").ap()
    w_up = nc.dram_tensor("w_up", [2, D, 2 * DFF], F32, kind="ExternalInput").ap()
    w_down = nc.dram_tensor("w_down", [2, DFF, D], F32, kind="ExternalInput").ap()
    pw_w = nc.dram_tensor("pw", [2, 256, 256], F32, kind="ExternalInput").ap()
    pool_w = nc.dram_tensor("pool_w", [2, 4, 64, 64], F32, kind="ExternalInput").ap()
    pp_d = nc.dram_tensor("pp", [2, 128, NPP], F32, kind="ExternalInput").ap()
    cst_d = nc.dram_tensor("cst", [128, 3 * 128 + 1024], F32, kind="ExternalInput").ap()
    y_out = nc.dram_tensor("y", [SEQ, D], F32, kind="ExternalOutput").ap()
    qs_d = nc.dram_tensor("qs", [8, 70, SEQ], BF16, kind="Internal").ap()
    ks_d = nc.dram_tensor("ks", [8, 70, SEQ], BF16, kind="Internal").ap()
    vs_d = nc.dram_tensor("vs", [128, 32, 1024], BF16, kind="Internal").ap()
    cps_d = nc.dram_tensor("cps", [128, 4, SEQ], BF16, kind="Internal").ap()
    xmid_d = nc.dram_tensor("xmid", [SEQ, D], F32, kind="Internal").ap()
    xl_d = nc.dram_tensor("xl", [SEQ, D], F32, kind="Internal").ap()

    with ExitStack() as st:
        S = Sched(nc)
        AR = Arena(nc, 212000)
        ps = []
        for i in range(8):
            ps.append((nc.alloc_psum_tensor("ps%d" % i, [128, 512], F32), ("ps", i)))

        def psf(i):
            return ps[i][0][:, :], ps[i][1]

        def psb(i):
            return ps[i][0][:, :].bitcast(BF16), ps[i][1]

        def act(out, in_, func, reads, writes, **kw):
            S.op("act", lambda e: e.activation(out=out, in_=in_, func=func, **kw), reads, writes)

        def mm(out, lhsT, rhs, start, stop, reads, writes):
            S.op("pe", lambda e: e.matmul(out, lhsT, rhs, start=start, stop=stop), reads, writes)

        def tr(out, in_, ident, reads, writes):
            S.op("pe", lambda e: e.transpose(out=out, in_=in_, identity=ident), reads, writes)

        def tt(eng, out, in0, in1, op, reads, writes):
            S.op(eng, lambda e: e.tensor_tensor(out=out, in0=in0, in1=in1, op=op), reads, writes)

        def ts(eng, out, in0, s1, s2, op0, op1, reads, writes):
            if s2 is None:
                S.op(eng, lambda e: e.tensor_scalar(out=out, in0=in0, scalar1=s1, scalar2=None, op0=op0),
                     reads, writes)
            else:
                S.op(eng, lambda e: e.tensor_scalar(out=out, in0=in0, scalar1=s1, scalar2=s2, op0=op0, op1=op1),
                     reads, writes)

        def stt(out, in0, scalar, in1, op0, op1, reads, writes):
            S.op("dve", lambda e: e.scalar_tensor_tensor(out=out, in0=in0, scalar=scalar, in1=in1,
                                                         op0=op0, op1=op1), reads, writes)

        def cp(eng, out, in_, reads, writes):
            S.op(eng, lambda e: e.tensor_copy(out=out, in_=in_), reads, writes)

        def recip(out, in_, reads, writes):
            S.op("dve", lambda e: e.reciprocal(out=out, in_=in_), reads, writes)

        def mset(eng, ap, val, writes):
            S.op(eng, lambda e: e.memset(ap, val), (), writes)

        def dma(q, out, in_, semkey, reads, writes, group=None):
            S.dma(q, lambda e: e.dma_start(out=out, in_=in_), semkey, reads, writes, group)

        def xload(xt, k_xt, src, T0, semname):
            for s in range(4):
                dma("sp", xt[:, s, :], src[T0 + s * 128:T0 + (s + 1) * 128, :], (semname, s), (), [(k_xt, s)])

        def xstore(xt, k_xt, dst, T0, semname, s):
            dma("sp", dst[T0 + s * 128:T0 + (s + 1) * 128, :], xt[:, s, :], (semname, s), [(k_xt, s)],
                [("xdst", semname, T0, s)])

        identb, k_identb = AR.alloc([128], BF16)
        maskb, k_maskb = AR.alloc([128], BF16)
        bonesb, k_bonesb = AR.alloc([128], BF16)
        o256b, k_o256b = AR.alloc([128], BF16)
        pp, k_pp = AR.alloc([2, NPP], F32)
        mhalf, k_mhalf = AR.alloc([1], F32)
        dma("pool", identb, cst_d[:, 0:128], "c0", (), [k_identb])
        dma("pool", maskb, cst_d[:, 128:256], "c1", (), [k_maskb])
        dma("pool", bonesb, cst_d[:, 256:384], "c2", (), [k_bonesb])
        dma("sp", pp, pp_d.rearrange("l p n -> p l n"), "c4", (), [k_pp])
        mset("pool", o256b, 1.0 / 256.0, [k_o256b])
        mset("pool", mhalf, -0.5, [k_mhalf])
        base_top = AR.top

        def ppc(l, c, n=1, parts=128):
            return pp[0:parts, l, c:c + n]

        def rmsnorm_to_hT(l, xt, k_xt, gcol, hT, k_hT, sm, psT_rot):
            norm_part(xt, k_xt, sm)
            transpose_part(l, gcol, hT, k_hT, sm, psT_rot)

        def norm_part(xt, k_xt, sm, lnexp=False):
            junk, k_junk, ss, k_ss, sd, k_sd, rstd, k_rstd, xn, k_xn = sm
            for s in range(4):
                act(junk, xt[:, s, :], AF.Square, [(k_xt, s)], [k_junk, (k_ss, s)], accum_out=ss[:, s:s + 1])
            if lnexp:
                act(sd, ss, AF.Ln, [(k_ss, s_) for s_ in range(4)], [k_sd], scale=1.0 / D, bias=EPS)
                act(rstd, sd, AF.Exp, [k_sd], [k_rstd], scale=-0.5)
            else:
                act(sd, ss, AF.Sqrt, [(k_ss, s_) for s_ in range(4)], [k_sd], scale=1.0 / D, bias=EPS)
                recip(rstd, sd, [k_sd], [k_rstd])
            for s in range(4):
                if s % 2 == 0:
                    ts("dve", xn[:, s, :], xt[:, s, :], rstd[:, s:s + 1], None, ALU.mult, None,
                       [(k_xt, s), k_rstd], [(k_xn, s)])
                else:
                    act(xn[:, s, :], xt[:, s, :], AF.Copy, [(k_xt, s), k_rstd], [(k_xn, s)],
                        scale=rstd[:, s:s + 1])

        def transpose_part(l, gcol, hT, k_hT, sm, psT_rot):
            junk, k_junk, ss, k_ss, sd, k_sd, rstd, k_rstd, xn, k_xn = sm
            for c in range(8):
                pt, k_pt = psT_rot.next()
                for s in range(4):
                    tr(pt[:, s * 128:(s + 1) * 128], xn[:, s, c * 128:(c + 1) * 128], identb,
                       [(k_xn, s), k_identb], [k_pt])
                if c % 2 == 0:
                    act(hT[:, c, :], pt[:, 0:512], AF.Copy, [k_pt, k_pp], [(k_hT, c)], scale=ppc(l, gcol + c))
                else:
                    ts("dve", hT[:, c, :], pt[:, 0:512], ppc(l, gcol + c), None, ALU.mult, None,
                       [k_pt, k_pp], [(k_hT, c)])

        def phase_A(l, x_src):
            AR.top = base_top
            invdiv, k_invdiv = AR.alloc([2, 512], F32)
            ones8, k_ones8 = AR.alloc([512], F32, parts=8)
            dma("sp", invdiv, cst_d[:, 384:1408].rearrange("p (a b) -> p a b", a=2), "c3", (), [k_invdiv])
            mset("pool", ones8, 1.0, [k_ones8])
            win, k_win = AR.alloc([8, DIN], BF16)
            pwb, k_pwb = AR.alloc([2, 256], BF16)
            pbd, k_pbd = AR.alloc([2, 128], BF16)
            dg, k_dg = AR.alloc([2, 31, 128], BF16)
            xt, k_xt = AR.alloc([4, 1024], F32)
            junk, k_junk = AR.alloc([1024], BF16)
            ss, k_ss = AR.alloc([4], F32)
            sd, k_sd = AR.alloc([4], F32)
            rstd, k_rstd = AR.alloc([4], F32)
            xn, k_xn = AR.alloc([4, 1024], BF16)
            hTs = [AR.alloc([8, 512], BF16) for _ in range(2)]
            qraws = Rot([AR.alloc([512], F32) for _ in range(2)])
            qsqs = Rot([AR.alloc([512], BF16) for _ in range(2)])
            sdqs = Rot([AR.alloc([512], F32) for _ in range(2)])
            rqs = Rot([AR.alloc([512], F32) for _ in range(2)])
            qsts = Rot([AR.alloc([512], BF16) for _ in range(2)])
            zf, k_zf = AR.alloc([512], F32, parts=8)
            ef, k_ef = zf, k_zf
            nl, k_nl = AR.alloc([512], F32, parts=8)
            Gb = [AR.alloc([512], F32, parts=8) for _ in range(2)]
            r1, k_r1 = AR.alloc([512], F32, parts=8)
            r2, k_r2 = AR.alloc([512], F32, parts=8)
            gk, k_gk = AR.alloc([6, 512], BF16, parts=8)
            gq, k_gq = AR.alloc([6, 512], BF16, parts=8)
            vsts = Rot([AR.alloc([4, 256], BF16) for _ in range(2)])
            sg, k_sg = AR.alloc([2, 512], F32)
            g2, k_g2 = AR.alloc([2, 542], BF16)
            cv, k_cv = AR.alloc([2, 512], F32)
            cvb, k_cvb = AR.alloc([2, 512], BF16)
            csq, k_csq = AR.alloc([2, 512], BF16)
            mu, k_mu = AR.alloc([512], F32)
            musq, k_musq = AR.alloc([512], F32)
            var, k_var = musq, k_musq
            sdc, k_sdc = AR.alloc([512], F32)
            rc, k_rc = sdc, k_sdc
            dd, k_dd = AR.alloc([2, 512], F32)
            sT, k_sT = AR.alloc([2, 512], BF16)
            PU, k_PU = AR.alloc([2, 528], F32)
            S2, k_S2 = AR.alloc([2, 528], F32)
            S4, k_S4 = AR.alloc([2, 528], F32)
            S8, k_S8 = AR.alloc([528], F32)
            S16, k_S16 = AR.alloc([528], F32)
            dT, k_dT = AR.alloc([2, 512], BF16)
            cpsts = Rot([AR.alloc([4, 512], BF16) for _ in range(2)])

            wv = w_in[l].rearrange("(c p) n -> p c n", p=128)
            for hf in (1, 0):
                for c in range(8):
                    dma("pool", win[:, c, hf * 1156:(hf + 1) * 1156], wv[:, c, hf * 1156:(hf + 1) * 1156],
                        ("win", hf), (), [(k_win, c, hf)], group=("win", hf, l))
            dma("pool", pwb, pw_w[l].rearrange("(c p) n -> p c n", p=128), "pw", (), [k_pwb])
            mset("dve", pbd, 0.0, [(k_pbd, g_) for g_ in range(4)])
            for g in range(4):
                c, o = g // 2, g % 2
                dma("pool", pbd[64 * o:64 * o + 64, c, 64 * o:64 * o + 64], pool_w[l, g], "pbd", [], [(k_pbd, g)],
                    group=("pbd", l))
            for c in range(2):
                for k in range(31):
                    if k % 2 == 0:
                        ts("dve", dg[:, c, k, :], identb, ppc(l, C_DWW + c * 31 + k), None, ALU.mult, None,
                           [k_identb, k_pp], [(k_dg, c, k)])
                    else:
                        act(dg[:, c, k, :], identb, AF.Copy, [k_identb, k_pp], [(k_dg, c, k)],
                            scale=ppc(l, C_DWW + c * 31 + k))
            mset("pool", gk, 1.0, [k_gk])
            mset("pool", gq, 1.0, [k_gq])
            for (v_, kv_) in vsts.items:
                mset("pool", v_, 1.0, [kv_])
            mset("dve", g2[:, :, 0:30], 0.0, [k_g2])
            mset("dve", PU[:, :, 0:16], 0.0, [k_PU])
            mset("pool", S2, 0.0, [k_S2])
            mset("pool", S4, 0.0, [k_S4])
            mset("pool", S8, 0.0, [k_S8])
            mset("pool", S16, 0.0, [k_S16])

            psT_rot = Rot([psb(6), psb(7)])
            prot = Rot([psf(i) for i in range(6)])
            sm = (junk, k_junk, ss, k_ss, sd, k_sd, rstd, k_rstd, xn, k_xn)

            pending = []

            def defer(n, fn):
                pending.append([n, fn])

            def tick():
                for p in list(pending):
                    p[0] -= 1
                    if p[0] <= 0:
                        pending.remove(p)
                        p[1]()

            def flush():
                while pending:
                    tick()

            xload(xt, k_xt, x_src, 0, "xt")
            norm_part(xt, k_xt, sm, True)
            for i in range(NT):
                T0 = i * 512
                hT, k_hT = hTs[i % 2]
                transpose_part(l, C_N1G, hT, k_hT, sm, psT_rot)
                if i + 1 < NT:
                    xload(xt, k_xt, x_src, T0 + 512, "xt")
                hreads = [(k_hT, c) for c in range(8)]

                def proj(col, M):
                    pa, k_pa = prot.next()
                    for c in range(8):
                        wk = [(k_win, c, hf_) for hf_ in range(2) if (col < (hf_ + 1) * 1156 and col + M > hf_ * 1156)]
                        mm(pa[0:M, :], win[:, c, col:col + M], hT[:, c, :], c == 0, c == 7,
                           wk + [(k_hT, c)], [k_pa])
                    return pa, k_pa

                def qk_group(isq, hp):
                    col = (0 if isq else 512) + hp * 128
                    pa, k_pa = proj(col, 128)
                    qraw, k_qraw = qraws.next()
                    qsq, k_qsq = qsqs.next()
                    sdq, k_sdq = sdqs.next()
                    rq, k_rq = rqs.next()
                    qst, k_qst = qsts.next()
                    act(qraw, pa, AF.Copy, [k_pa], [k_qraw])
                    tt("pool", qsq, qraw, qraw, ALU.mult, [k_qraw], [k_qsq])

                    def stage2():
                        pb, k_pb = prot.next()
                        mm(pb, bonesb, qsq, True, True, [k_bonesb, k_qsq], [k_pb])
                        if isq:
                            act(sdq, pb, AF.Ln, [k_pb], [k_sdq], scale=1.0, bias=64.0 * EPS)
                        else:
                            act(sdq, pb, AF.Ln, [k_pb], [k_sdq], scale=1.0 / 64.0, bias=EPS)
                        act(rq, sdq, AF.Exp, [k_sdq], [k_rq], scale=-0.5)
                        stt(qst, qraw, ppc(l, C_QG if isq else C_KG), rq, ALU.mult, ALU.mult,
                            [k_qraw, k_rq, k_pp], [k_qst])
                        dst = qs_d if isq else ks_d
                        for o in range(2):
                            h = 2 * hp + o
                            dma("sp", dst[h, 0:64, T0:T0 + 512], qst[64 * o:64 * o + 64, :],
                                ("qst", qsts.i % 2), [k_qst], [("qk", isq, h, i)])
                    defer(2, stage2)

                cpst, k_cpst = cpsts.next()
                if i > 0:
                    cp("pool", g2[:, :, 0:30], g2[:, :, 512:542], [k_g2], [k_g2])
                pbs = [proj(1544 + 256 + 128 * c, 128) for c in range(2)]
                pas = [proj(1544 + 128 * c, 128) for c in range(2)]
                for c in range(2):
                    act(sg[:, c, :], pbs[c][0], AF.Tanh, [pbs[c][1]], [(k_sg, c)], scale=0.5)
                    stt(g2[:, c, 30:542], sg[:, c, :], 1.0, pas[c][0], ALU.add, ALU.mult,
                        [pas[c][1], (k_sg, c)], [k_g2])
                tick()
                pps = [proj(2056 + 128 * c, 128) for c in range(2)]
                if i > 0:
                    cp("pool", PU[:, :, 0:16], PU[:, :, 512:528], [k_PU], [k_PU])
                for c in range(2):
                    act(PU[:, c, 16:528], pps[c][0], AF.Copy, [pps[c][1]], [k_PU])
                tt("pool", S2[:, :, 1:528], PU[:, :, 1:528], PU[:, :, 0:527], ALU.add, [k_PU], [k_S2])
                tt("pool", S4[:, :, 3:528], S2[:, :, 3:528], S2[:, :, 1:526], ALU.add, [k_S2], [k_S4])
                tt("pool", S8[:, 7:528], S4[:, 1, 7:528], S4[:, 1, 3:524], ALU.add, [k_S4], [k_S8])
                tt("pool", S16[64:128, 15:528], S8[64:128, 15:528], S8[64:128, 7:520], ALU.add, [k_S8], [k_S16])
                srcs = [(S2[0:64, 0, 16:528], k_S2, 0.5, 0, 0), (S4[64:128, 0, 16:528], k_S4, 0.25, 0, 64),
                        (S8[0:64, 16:528], k_S8, 0.125, 1, 0), (S16[64:128, 16:528], k_S16, 0.0625, 1, 64)]
                for (sap, ksap, inv, c, p0) in srcs:
                    if i == 0:
                        tt("dve", dd[p0:p0 + 64, c, :], sap, invdiv[p0:p0 + 64, c, :], ALU.mult,
                           [ksap, k_invdiv], [(k_dd, c)])
                        tt("dve", dT[p0:p0 + 64, c, :], dd[p0:p0 + 64, c, :], PU[p0:p0 + 64, c, 16:528],
                           ALU.subtract, [(k_dd, c), k_PU], [(k_dT, c)])
                    else:
                        stt(dT[p0:p0 + 64, c, :], sap, inv, PU[p0:p0 + 64, c, 16:528], ALU.mult, ALU.subtract,
                            [ksap, k_PU], [(k_dT, c)])

                qk_group(True, 0)
                tick()
                qk_group(False, 0)
                tick()
                qk_group(True, 1)
                tick()
                qk_group(False, 1)
                tick()
                if i + 1 < NT:
                    norm_part(xt, k_xt, sm, True)
                pcs = []
                for c in range(2):
                    pa, k_pa = prot.next()
                    for k in range(31):
                        mm(pa, dg[:, c, k, :], g2[:, c, k:k + 512], k == 0, k == 30, [(k_dg, c, k), k_g2], [k_pa])
                    act(cv[:, c, :], pa, AF.Identity, [k_pa, k_pp], [(k_cv, c)], bias=ppc(l, C_DWB + c), scale=0.5)
                    cp("pool", cvb[:, c, :], cv[:, c, :], [(k_cv, c)], [(k_cvb, c)])
                    tt("pool", csq[:, c, :], cv[:, c, :], cv[:, c, :], ALU.mult, [(k_cv, c)], [(k_csq, c)])

                qk_group(True, 2)
                tick()
                qk_group(False, 2)
                tick()
                qk_group(True, 3)
                tick()
                qk_group(False, 3)
                tick()
                pa, k_pa = proj(1536, 8)
                act(zf, pa[0:8, :], AF.Identity, [k_pa, k_pp], [k_zf], bias=ppc(l, C_BF, 1, 8))
                act(ef, zf, AF.Exp, [k_zf], [k_ef], scale=-1.0)
                act(nl, ef, AF.Ln, [k_ef], [k_nl], bias=1.0)
                G, k_G = Gb[i % 2]
                Gp, k_Gp = Gb[(i + 1) % 2]
                if i == 0:
                    S.op("dve", lambda e, G=G: e.tensor_tensor_scan(out=G, data0=ones8, data1=nl, initial=0.0,
                                                                    op0=ALU.mult, op1=ALU.add),
                         [k_ones8, k_nl], [k_G])
                else:
                    S.op("dve", lambda e, G=G, Gp=Gp: e.tensor_tensor_scan(
                        out=G, data0=ones8, data1=nl, initial=Gp[:, 511:512], op0=ALU.mult, op1=ALU.add),
                        [k_ones8, k_nl, k_Gp], [k_G])
                cp("dve", gk[:, 3, :], G, [k_G], [k_gk])
                tt("dve", r1, G, gk[:, 3, :], ALU.subtract, [k_G, k_gk], [k_r1])
                cp("dve", gk[:, 4, :], r1, [k_r1], [k_gk])
                tt("dve", r2, r1, gk[:, 4, :], ALU.subtract, [k_r1, k_gk], [k_r2])
                cp("dve", gk[:, 5, :], r2, [k_r2], [k_gk])
                ts("dve", gq[:, 0:3, :], gk[:, 3:6, :], -1.0, None, ALU.mult, None, [k_gk], [k_gq])
                dma("sp", ks_d[:, 64:70, T0:T0 + 512], gk, "gk", [k_gk], [("gkd", i)])
                dma("sp", qs_d[:, 64:70, T0:T0 + 512], gq, "gq", [k_gq], [("gqd", i)])
                tick()

                pmu, k_pmu = prot.next()
                for c in range(2):
                    mm(pmu, o256b, cvb[:, c, :], c == 0, c == 1, [k_o256b, (k_cvb, c)], [k_pmu])
                pex, k_pex = prot.next()
                for c in range(2):
                    mm(pex, o256b, csq[:, c, :], c == 0, c == 1, [k_o256b, (k_csq, c)], [k_pex])
                cp("dve", mu, pmu, [k_pmu], [k_mu])
                tt("dve", musq, mu, mu, ALU.mult, [k_mu], [k_musq])
                tt("dve", var, pex, musq, ALU.subtract, [k_pex, k_musq], [k_var])
                ts("dve", var, var, 0.0, None, ALU.max, None, [k_var], [k_var])
                act(sdc, var, AF.Ln, [k_var], [k_sdc], scale=1.0, bias=EPS)
                act(rc, sdc, AF.Exp, [k_sdc], [k_rc], scale=-0.5)
                for c in range(2):
                    tt("dve", dd[:, c, :], cv[:, c, :], mu, ALU.subtract, [(k_cv, c), k_mu], [(k_dd, c)])
                    tt("dve", dd[:, c, :], dd[:, c, :], rc, ALU.mult, [(k_dd, c), k_rc], [(k_dd, c)])
                    act(sT[:, c, :], dd[:, c, :], AF.Silu, [(k_dd, c), k_pp], [(k_sT, c)],
                        scale=ppc(l, C_LNG + c), bias=ppc(l, C_LNB + c))
                for s in range(4):
                    pa, k_pa = prot.next()
                    for c in range(8):
                        mm(pa, hT[:, c, s * 128:(s + 1) * 128], win[:, c, 1024:1536], c == 0, c == 7,
                           [(k_win, c, 0), (k_win, c, 1), (k_hT, c)], [k_pa])
                    vst, k_vst = vsts.next()
                    pv4 = pa.rearrange("p (a b d) -> p a b d", a=4, b=2)
                    act(vst[:, :, 0:64], pv4[:, :, 0, :], AF.Copy, [k_pa], [k_vst])
                    cp("dve", vst[:, :, 192:256], pv4[:, :, 1, :], [k_pa], [k_vst])
                    dma("sp", vs_d[:, 4 * i + s, :], vst.rearrange("p a b -> p (a b)"),
                        ("vst", vsts.i % 2), [k_vst], [("vsd", i, s)])
                    tick()

                for c in range(2):
                    pa, k_pa = prot.next()
                    mm(pa, pbd[:, c, :], dT[:, c, :], True, True, [(k_pbd, 2 * c), (k_pbd, 2 * c + 1), (k_dT, c)], [k_pa])
                    act(cpst[:, 2 + c, :], pa, AF.Copy, [k_pa, k_pp], [k_cpst], scale=ppc(l, C_PSC + c))
                for co in range(2):
                    pa, k_pa = prot.next()
                    for ci in range(2):
                        mm(pa, pwb[:, ci, co * 128:(co + 1) * 128], sT[:, ci, :], ci == 0, ci == 1,
                           [k_pwb, (k_sT, ci)], [k_pa])
                    cp("dve", cpst[:, co, :], pa, [k_pa], [k_cpst])
                dma("sp", cps_d[:, :, T0:T0 + 512], cpst, ("cpst", cpsts.i % 2), [k_cpst], [("cpsd", i)])
                flush()
            S.barrier()
            print("arena A", AR.top)

        def phase_B(l, x_src, x_dst):
            AR.top = base_top
            kaug = [AR.alloc([SEQ], BF16) for _ in range(8)]
            vaug, k_vaug = AR.alloc([32, 1024], BF16)
            wout, k_wout = AR.alloc([8, 1024], BF16)
            qcs = [AR.alloc([8, 512], BF16) for _ in range(2)]
            mixT, k_mixT = AR.alloc([8, 512], BF16)
            pts = Rot([AR.alloc([512], BF16) for _ in range(4)])
            recs = Rot([AR.alloc([512], F32) for _ in range(2)])
            xt, k_xt = AR.alloc([4, 1024], F32)

            def load_kv(cc):
                for h in range(8):
                    dma("sp", kaug[h][0][0:70, cc * 512:(cc + 1) * 512], ks_d[h, :, cc * 512:(cc + 1) * 512],
                        ("kaug", cc), (), [(kaug[h][1], cc)], group=("kaug", cc, l))
                dma("sp", vaug[:, 4 * cc:4 * cc + 4, :], vs_d[:, 4 * cc:4 * cc + 4, :], ("vaug", cc), (),
                    [(k_vaug, cc)])

            wv = w_out[l].rearrange("(c p) n -> p c n", p=128)
            for c in range(8):
                dma("pool", wout[:, c, :], wv[:, c, :], "wout", (), [(k_wout, c)], group=("wout", l))
            srot = Rot([psf(i) for i in range(4)])
            orot = Rot([psf(4), psf(5)])
            xrot = Rot([psf(6), psf(7)])

            def load_chunk(c):
                qc, k_qc = qcs[c % 2]
                T0 = c * 512
                dma("sp", qc[0:70, :, :], qs_d[:, :, T0:T0 + 512].rearrange("h r t -> r h t"),
                    ("qc", c % 2), (), [k_qc])

            load_chunk(0)
            load_kv(0)
            for c in range(NT):
                T0 = c * 512
                qc, k_qc = qcs[c % 2]
                dma("sp", mixT[:, 4:8, :], cps_d[:, :, T0:T0 + 512], "mixcp", (), [(k_mixT, "cp")])
                xload(xt, k_xt, x_src, T0, "xtb")
                if c + 1 < NT:
                    load_chunk(c + 1)
                    load_kv(c + 1)
                nkb = 4 * c + 4
                items = [(h, kb) for h in range(8) for kb in range(nkb)]
                state = {}

                def s_stage(h, kb):
                    q0 = max(0, 128 * kb - T0)
                    pS, k_pS = srot.next()
                    diag = kb >= 4 * c
                    ka, k_ka = kaug[h]
                    mm(pS[:, q0:512], ka[0:70, kb * 128:(kb + 1) * 128], qc[0:70, h, q0:512], True, not diag,
                       [(k_ka, kb // 4), k_qc], [k_pS])
                    if diag:
                        mm(pS[:, q0:q0 + 128], identb, maskb, False, True, [k_identb, k_maskb], [k_pS])
                    pt, k_pt = pts.next()
                    act(pt[:, q0:512], pS[:, q0:512], AF.Exp, [k_pS], [k_pt])
                    state[(h, kb)] = (pt, k_pt, q0)

                def pv_stage(h, kb):
                    pt, k_pt, q0 = state.pop((h, kb))
                    if kb == 0:
                        state[("o", h)] = orot.next()
                    pO, k_pO = state[("o", h)]
                    mm(pO[:, q0:512], vaug[:, kb, h * 128:(h + 1) * 128], pt[:, q0:512], kb == 0, kb == nkb - 1,
                       [(k_vaug, kb // 4), k_pt], [k_pO])
                    if kb == nkb - 1:
                        hp, o = h // 2, h % 2
                        rec, k_rec = recs.next()
                        a0, b0 = (0, 64) if o == 0 else (64, 0)
                        recip(rec[b0:b0 + 64, :], pO[b0:b0 + 64, :], [k_pO], [k_rec])
                        tt("dve", mixT[a0:a0 + 64, hp, :], pO[a0:a0 + 64, :], rec[b0:b0 + 64, :], ALU.mult,
                           [k_pO, k_rec], [(k_mixT, hp)])
                        state.pop(("o", h))

                DEPTH = 2
                for n_, (h, kb) in enumerate(items):
                    s_stage(h, kb)
                    if n_ >= DEPTH:
                        pv_stage(*items[n_ - DEPTH])
                for n_ in range(max(0, len(items) - DEPTH), len(items)):
                    pv_stage(*items[n_])

                mreads = [(k_mixT, hp) for hp in range(4)] + [(k_mixT, "cp")]
                for s in range(4):
                    for n in range(2):
                        pX, k_pX = xrot.next()
                        for kc in range(8):
                            mm(pX, mixT[:, kc, s * 128:(s + 1) * 128], wout[:, kc, n * 512:(n + 1) * 512],
                               kc == 0, kc == 7, mreads + [(k_wout, kc)], [k_pX])
                        tt("dve", xt[:, s, n * 512:(n + 1) * 512], pX, xt[:, s, n * 512:(n + 1) * 512], ALU.add,
                           [k_pX, (k_xt, s)], [(k_xt, s)])
                    xstore(xt, k_xt, x_dst, T0, "xtb_st", s)
            S.barrier()

        def phase_C(l, x_src, x_dst):
            AR.top = base_top
            wup, k_wup = AR.alloc([8, 2 * DFF], BF16)
            wdn, k_wdn = AR.alloc([22, 1024], BF16)
            xt, k_xt = AR.alloc([4, 1024], F32)
            xs, k_xs = AR.alloc([1024], F32)
            ss, k_ss = AR.alloc([4], F32)
            sd, k_sd = AR.alloc([4], F32)
            rstd, k_rstd = AR.alloc([4], F32)
            xn, k_xn = AR.alloc([4, 1024], BF16)
            hTs = [AR.alloc([8, 512], BF16) for _ in range(2)]
            actT, k_actT = AR.alloc([13, 512], BF16)
            abufs = Rot([AR.alloc([512], F32) for _ in range(5)])
            sgs = Rot([AR.alloc([512], F32) for _ in range(2)])
            hl, k_hl = AR.alloc([44, 2], F32)
            HC, k_HC = AR.alloc([44, 2], F32)
            htmp, k_htmp = AR.alloc([44], F32)
            fdw = pp[:, l, C_FDW:C_FDW + 132].rearrange("p (c k) -> p c k", k=3)

            wv = w_up[l].rearrange("(c p) n -> p c n", p=128)
            for q4 in (0, 2, 1, 3):
                for c in range(8):
                    dma("pool", wup[:, c, q4 * 1408:(q4 + 1) * 1408], wv[:, c, q4 * 1408:(q4 + 1) * 1408],
                        ("wup", q4), (), [(k_wup, c, q4)], group=("wup", q4, l))
            wd = w_down[l].rearrange("(j p) n -> p j n", p=128)
            for j in range(22):
                dma("pool", wdn[:, j, :], wd[:, j, :], "wdn", (), [(k_wdn, j)], group=("wdn", l))

            psT_rot = Rot([psb(7)])
            urot = Rot([psf(i) for i in range(5)])
            drot = Rot([psf(5), psf(6)])
            sm = (None, None, ss, k_ss, sd, k_sd, rstd, k_rstd, xn, k_xn)

            def nc_load(ti, s):
                r0 = ti * 512 + s * 128
                dma("sp", xs, x_src[r0:r0 + 128, :], "xs", (), [k_xs])

            def nc_comp(s, xs=xs, k_xs=k_xs):
                act(xn[:, s, :], xs, AF.Square, [k_xs], [(k_xn, s), (k_ss, s)], accum_out=ss[:, s:s + 1])
                ts("dve", sd[:, s:s + 1], ss[:, s:s + 1], 1.0 / D, EPS, ALU.mult, ALU.add, [(k_ss, s)], [(k_sd, s)])
                tt("pool", rstd[:, s:s + 1], sd[:, s:s + 1], mhalf, ALU.pow, [(k_sd, s), k_mhalf], [(k_rstd, s)])
                ts("dve", xn[:, s, :], xs, rstd[:, s:s + 1], None, ALU.mult, None, [k_xs, (k_rstd, s)],
                   [(k_xn, s)])

            xload(xt, k_xt, x_src, 0, "xtc")
            for s in range(4):
                nc_comp(s, xt[:, s, :], (k_xt, s))
            transpose_part(l, C_N2G, hTs[0][0], hTs[0][1], sm, psT_rot)
            for i in range(NT):
                T0 = i * 512
                hT, k_hT = hTs[i % 2]
                if i > 0:
                    hlk = [(k_hl, ch_) for ch_ in range(44)]
                    tt("pool", HC[:, :, 0], hl[:, :, 1], fdw[:, :, 1], ALU.mult, hlk + [k_pp], [k_HC])
                    tt("pool", htmp, hl[:, :, 0], fdw[:, :, 0], ALU.mult, hlk + [k_pp], [k_htmp])
                    tt("pool", HC[:, :, 0], HC[:, :, 0], htmp, ALU.add, [k_HC, k_htmp], [k_HC])
                    tt("pool", HC[:, :, 1], hl[:, :, 1], fdw[:, :, 0], ALU.mult, hlk + [k_pp], [k_HC])
                def wdown(half, i=i, T0=T0):
                    slots = [(jj if (half == 0 or jj >= 2) else 11 + jj) for jj in range(11)]
                    areads = [(k_actT, sl) for sl in slots]
                    for s in range(4):
                        for n in range(2):
                            pD, k_pD = drot.next()
                            for jj in range(11):
                                j = half * 11 + jj
                                mm(pD, actT[:, slots[jj], s * 128:(s + 1) * 128],
                                   wdn[:, j, n * 512:(n + 1) * 512],
                                   jj == 0, jj == 10, areads + [(k_wdn, j)], [k_pD])
                            tt("dve", xt[:, s, n * 512:(n + 1) * 512], pD, xt[:, s, n * 512:(n + 1) * 512],
                               ALU.add, [k_pD, (k_xt, s)], [(k_xt, s)])
                        if half == 1:
                            xstore(xt, k_xt, x_dst, T0, "xtc_st", s)
                            if i + 1 < NT:
                                dma("sp", xt[:, s, :], x_src[T0 + 512 + s * 128:T0 + 512 + (s + 1) * 128, :],
                                    ("xtc", s), (), [(k_xt, s)])

                for half in range(2):
                    for jj in range(11):
                        j = half * 11 + jj
                        res = []
                        for which in range(2):
                            ch = j + 22 * which
                            col = ch * 128
                            pU, k_pU = urot.next()
                            for c in range(8):
                                mm(pU, wup[:, c, col:col + 128], hT[:, c, :], c == 0, c == 7,
                                   [(k_wup, c, col // 1408), (k_hT, c)], [k_pU])
                            ab, k_ab = abufs.next()
                            act(ab, pU, AF.Copy, [k_pU, k_pp], [k_ab], scale=ppc(l, C_FDW + ch * 3 + 2))
                            stt(ab[:, 1:512], pU[:, 0:511], ppc(l, C_FDW + ch * 3 + 1), ab[:, 1:512],
                                ALU.mult, ALU.add, [k_pU, k_ab, k_pp], [k_ab])
                            stt(ab[:, 2:512], pU[:, 0:510], ppc(l, C_FDW + ch * 3 + 0), ab[:, 2:512],
                                ALU.mult, ALU.add, [k_pU, k_ab, k_pp], [k_ab])
                            if i > 0:
                                tt("pool", ab[:, 0:2], ab[:, 0:2], HC[:, ch, :], ALU.add, [k_ab, k_HC], [k_ab])
                            if i + 1 < NT:
                                act(hl[:, ch, :], pU[:, 510:512], AF.Copy, [k_pU], [(k_hl, ch)])
                            res.append((ab, k_ab))
                        sgb, k_sgb = sgs.next()
                        act(sgb, res[0][0], AF.Silu, [res[0][1]], [k_sgb])
                        slot = jj if (half == 0 or jj >= 2) else 11 + jj
                        tt("pool", actT[:, slot, :], sgb, res[1][0], ALU.mult, [k_sgb, res[1][1]],
                           [(k_actT, slot)])
                        if half == 1 and jj == 1:
                            wdown(0)
                        if half == 0 and i + 1 < NT:
                            if jj in (1, 3, 5, 7):
                                if jj > 1:
                                    nc_comp((jj - 3) // 2)
                                nc_load(i + 1, (jj - 1) // 2)
                            elif jj == 9:
                                nc_comp(3)
                    if half == 1:
                        if i + 1 < NT:
                            transpose_part(l, C_N2G, hTs[(i + 1) % 2][0], hTs[(i + 1) % 2][1], sm, psT_rot)
                        wdown(1)
            S.barrier()
            print("arena C", AR.top)

        nl_ = len(layers)
        for li, l in enumerate(layers):
            src = x_in if li == 0 else xl_d
            dst = y_out if li == nl_ - 1 else xl_d
            phase_A(l, src)
            phase_B(l, src, xmid_d)
            phase_C(l, xmid_d, dst)
        S.emit(st)
    return nc


def _pack_params(inp):
    pp = np.zeros((2, 128, NPP), np.float32)
    for l in range(2):
        pp[l, :, C_N1G:C_N1G + 8] = inp["norm1_g"][l].reshape(8, 128).T
        pp[l, :, C_N2G:C_N2G + 8] = inp["norm2_g"][l].reshape(8, 128).T
        pp[l, :, C_QG] = np.tile(inp["q_norm_g"][l], 2)
        pp[l, :, C_KG] = np.tile(inp["k_norm_g"][l], 2)
        pp[l, :, C_DWB:C_DWB + 2] = inp["conv_dw_b"][l].reshape(2, 128).T
        pp[l, :, C_LNG:C_LNG + 2] = inp["conv_ln_g"][l].reshape(2, 128).T
        pp[l, :, C_LNB:C_LNB + 2] = inp["conv_ln_b"][l].reshape(2, 128).T
        pp[l, :, C_PSC:C_PSC + 2] = inp["pool_scale"][l].reshape(2, 128).T
        pp[l, 0:8, C_BF] = inp["b_f"][l]
        pp[l, :, C_DWW:C_DWW + 62] = inp["conv_dw_w"][l].T.reshape(2, 128, 31).transpose(1, 0, 2).reshape(128, 62)
        pp[l, :, C_FDW:C_FDW + 132] = inp["ffn_dw_w"][l].T.reshape(44, 128, 3).transpose(1, 0, 2).reshape(128, 132)
    return pp


def _consts():
    cst = np.zeros((128, 3 * 128 + 1024), np.float32)
    cst[:, 0:128] = np.eye(128, dtype=np.float32)
    k = np.arange(128)[:, None]
    q = np.arange(128)[None, :]
    cst[:, 128:256] = np.where(k > q, -30000.0, 0.0)
    cst[:, 256:384] = (k // 64 == q // 64).astype(np.float32)
    t = np.arange(512, dtype=np.float32) + 1.0
    wins = [2.0, 4.0, 8.0, 16.0]
    inv = np.zeros((128, 2, 512), np.float32)
    for g in range(4):
        c, o = g // 2, g % 2
        inv[64 * o:64 * o + 64, c, :] = 1.0 / np.minimum(t, wins[g])
    cst[:, 384:] = inv.reshape(128, 1024)
    return cst


FUSED = True
_CACHE = {}


def _get_prog(layers):
    key = tuple(layers)
    if key not in _CACHE:
        _CACHE[key] = build_program(list(layers))
    return _CACHE[key]


def kernel(**inputs):
    inp = {k: np.ascontiguousarray(np.asarray(v)) for k, v in inputs.items()}
    x = inp["x"].astype(np.float32, copy=False)
    pp = _pack_params(inp)
    cst = _consts()
    common = {"w_in": inp["w_in"], "w_out": inp["w_out"], "w_up": inp["w_up"], "w_down": inp["w_down"],
              "pw": inp["conv_pw_w"], "pool_w": inp["pool_w"], "pp": pp, "cst": cst}
    n = 8
    if FUSED:
        nc = _get_prog((0, 1))
        in_maps = [dict(common, x=x[b]) for b in range(n)]
        res = run_bass_kernel_spmd(nc, in_maps, core_ids=list(range(n)))
        return np.stack([r["y"] for r in res.results], axis=0).astype(np.float32)
    cur = x
    for l in range(2):
        nc = _get_prog((l,))
        in_maps = [dict(common, x=cur[b]) for b in range(n)]
        res = run_bass_kernel_spmd(nc, in_maps, core_ids=list(range(n)))
        cur = np.stack([r["y"] for r in res.results], axis=0).astype(np.float32)
    return cur
```
